# Optimizing a Trainium2 kernel written in Bass

```python
import math
import jax, jax.numpy as jnp
from jax import lax
import numpy as np

D_MODEL = 1024
BATCH = 8
SEQ = 2048
DEPTH = 2

EPS = 1e-6
SSM_WIDTH = D_MODEL // 2
SSM_GROUP_SIZE = 16
SSM_GROUPS = SSM_WIDTH // SSM_GROUP_SIZE
SSM_STATE = 64
DT_MIN = 1e-3
DT_MAX = 1e-1
POOL_WIDTH = D_MODEL // 2
POOL_WINDOWS = (2, 4, 8, 16)
POOL_GROUPS = len(POOL_WINDOWS)
POOL_GROUP = POOL_WIDTH // POOL_GROUPS
N_IN = 2 * SSM_WIDTH + 2 * POOL_WIDTH + 2 * D_MODEL
SPLITS = (SSM_WIDTH, 2 * SSM_WIDTH, 2 * SSM_WIDTH + POOL_WIDTH,
          2 * SSM_WIDTH + 2 * POOL_WIDTH, 2 * SSM_WIDTH + 2 * POOL_WIDTH + D_MODEL)

kernel_name = "hawk_merge_s5_pool_hybrid"


def _rmsnorm(x, g):
    x32 = x.astype(jnp.float32)
    y = x32 * lax.rsqrt(jnp.mean(x32 * x32, axis=-1, keepdims=True) + EPS)
    return y.astype(x.dtype) * g


def _complex_linear_combine(left, right):
    a1r, a1i, b1r, b1i = left
    a2r, a2i, b2r, b2i = right
    ar = a2r * a1r - a2i * a1i
    ai = a2r * a1i + a2i * a1r
    br = a2r * b1r - a2i * b1i + b2r
    bi = a2r * b1i + a2i * b1r + b2i
    return ar, ai, br, bi


def _s5_branch(u, log_dt, lam_re, lam_im, b_re, b_im, c_re, c_im, d_skip, w_glu, b_glu):
    bsz, seq, _ = u.shape
    ug = u.reshape(bsz, seq, SSM_GROUPS, SSM_GROUP_SIZE)
    dt = jnp.exp(log_dt)[:, None]
    mag = jnp.exp(lam_re * dt)
    ang = lam_im * dt
    abar_re = mag * jnp.cos(ang)
    abar_im = mag * jnp.sin(ang)
    num_re = abar_re - 1.0
    num_im = abar_im
    den = lam_re * lam_re + lam_im * lam_im
    coef_re = (num_re * lam_re + num_im * lam_im) / den
    coef_im = (num_im * lam_re - num_re * lam_im) / den
    bbar_re = coef_re[..., None] * b_re - coef_im[..., None] * b_im
    bbar_im = coef_re[..., None] * b_im + coef_im[..., None] * b_re
    bu_re = jnp.einsum('blgc,gpc->blgp', ug, bbar_re)
    bu_im = jnp.einsum('blgc,gpc->blgp', ug, bbar_im)
    a_re = jnp.broadcast_to(abar_re, bu_re.shape)
    a_im = jnp.broadcast_to(abar_im, bu_im.shape)
    _, _, s_re, s_im = lax.associative_scan(_complex_linear_combine,
                                            (a_re, a_im, bu_re, bu_im), axis=1)
    y = (jnp.einsum('blgp,gcp->blgc', s_re, c_re)
         - jnp.einsum('blgp,gcp->blgc', s_im, c_im))
    y = y.reshape(bsz, seq, SSM_WIDTH) + d_skip * u
    y = jax.nn.gelu(y)
    return y * jax.nn.sigmoid(y @ w_glu + b_glu)


def _pool_branch(u, w_group, scale):
    bsz, seq, _ = u.shape
    u32 = u.astype(jnp.float32)
    cs = jnp.cumsum(u32, axis=1)
    pos = jnp.arange(seq)
    outs = []
    for gi, win in enumerate(POOL_WINDOWS):
        csg = cs[:, :, gi * POOL_GROUP:(gi + 1) * POOL_GROUP]
        shifted = jnp.pad(csg, ((0, 0), (win, 0), (0, 0)))[:, :seq]
        count = jnp.minimum(pos + 1, win).astype(jnp.float32)[None, :, None]
        mean = (csg - shifted) / count
        outs.append(mean - u32[:, :, gi * POOL_GROUP:(gi + 1) * POOL_GROUP])
    pooled = jnp.stack(outs, axis=2).astype(u.dtype)
    mixed = jnp.einsum('blgc,gcd->blgd', pooled, w_group).reshape(bsz, seq, POOL_WIDTH)
    return mixed * scale


def setup_inputs(seed: int = 0) -> dict:
    key = jax.random.key(seed)
    ks = jax.random.split(key, 24)
    f32 = jnp.float32
    nrm = lambda k, shape, std: (jax.random.normal(k, shape, f32) * std)
    x = jax.random.normal(ks[0], (BATCH, SEQ, D_MODEL), f32)
    norm_g = 1.0 + nrm(ks[1], (DEPTH, D_MODEL), 0.05)
    w_in = nrm(ks[2], (DEPTH, D_MODEL, N_IN), D_MODEL ** -0.5)
    b_in = nrm(ks[3], (DEPTH, N_IN), 0.02)
    ssm_log_dt = jax.random.uniform(ks[4], (DEPTH, SSM_GROUPS), f32,
                                    math.log(DT_MIN), math.log(DT_MAX))
    n_idx = jnp.arange(SSM_STATE, dtype=f32)
    ssm_lam_re = -0.5 + nrm(ks[5], (DEPTH, SSM_GROUPS, SSM_STATE), 0.01)
    ssm_lam_im = math.pi * n_idx[None, None, :] + nrm(ks[6], (DEPTH, SSM_GROUPS, SSM_STATE), 0.01)
    b_std = (2.0 * SSM_GROUP_SIZE) ** -0.5
    ssm_b_re = nrm(ks[7], (DEPTH, SSM_GROUPS, SSM_STATE, SSM_GROUP_SIZE), b_std)
    ssm_b_im = nrm(ks[8], (DEPTH, SSM_GROUPS, SSM_STATE, SSM_GROUP_SIZE), b_std)
    c_std = SSM_STATE ** -0.5
    ssm_c_re = nrm(ks[9], (DEPTH, SSM_GROUPS, SSM_GROUP_SIZE, SSM_STATE), c_std)
    ssm_c_im = nrm(ks[10], (DEPTH, SSM_GROUPS, SSM_GROUP_SIZE, SSM_STATE), c_std)
    ssm_d = nrm(ks[11], (DEPTH, SSM_WIDTH), 1.0)
    ssm_w_glu = nrm(ks[12], (DEPTH, SSM_WIDTH, SSM_WIDTH), SSM_WIDTH ** -0.5)
    ssm_b_glu = nrm(ks[13], (DEPTH, SSM_WIDTH), 0.02)
    pool_w = nrm(ks[14], (DEPTH, POOL_GROUPS, POOL_GROUP, POOL_GROUP), POOL_GROUP ** -0.5)
    pool_scale = 1.0 + nrm(ks[15], (DEPTH, POOL_WIDTH), 0.1)
    w_branch_a = nrm(ks[16], (DEPTH, SSM_WIDTH, D_MODEL), SSM_WIDTH ** -0.5)
    w_branch_b = nrm(ks[17], (DEPTH, POOL_WIDTH, D_MODEL), POOL_WIDTH ** -0.5)
    w_out = nrm(ks[18], (DEPTH, D_MODEL, D_MODEL), D_MODEL ** -0.5)
    final_norm_g = 1.0 + nrm(ks[19], (D_MODEL,), 0.05)
    return {"x": x, "norm_g": norm_g, "w_in": w_in, "b_in": b_in,
            "ssm_log_dt": ssm_log_dt, "ssm_lam_re": ssm_lam_re, "ssm_lam_im": ssm_lam_im,
            "ssm_b_re": ssm_b_re, "ssm_b_im": ssm_b_im, "ssm_c_re": ssm_c_re, "ssm_c_im": ssm_c_im,
            "ssm_d": ssm_d, "ssm_w_glu": ssm_w_glu, "ssm_b_glu": ssm_b_glu,
            "pool_w": pool_w, "pool_scale": pool_scale,
            "w_branch_a": w_branch_a, "w_branch_b": w_branch_b, "w_out": w_out,
            "final_norm_g": final_norm_g}


def reference(x, norm_g, w_in, b_in, ssm_log_dt, ssm_lam_re, ssm_lam_im, ssm_b_re, ssm_b_im,
              ssm_c_re, ssm_c_im, ssm_d, ssm_w_glu, ssm_b_glu, pool_w, pool_scale,
              w_branch_a, w_branch_b, w_out, final_norm_g):
    for l in range(DEPTH):
        h = _rmsnorm(x, norm_g[l])
        proj = h @ w_in[l] + b_in[l]
        ua, za, ub, zb, ga, gb = jnp.split(proj, SPLITS, axis=-1)
        ya = _s5_branch(ua, ssm_log_dt[l], ssm_lam_re[l], ssm_lam_im[l], ssm_b_re[l], ssm_b_im[l],
                        ssm_c_re[l], ssm_c_im[l], ssm_d[l], ssm_w_glu[l], ssm_b_glu[l])
        ya = ya * jax.nn.silu(za)
        yb = _pool_branch(ub, pool_w[l], pool_scale[l]) * jax.nn.silu(zb)
        merged = (jax.nn.sigmoid(ga) * (ya @ w_branch_a[l])
                  + jax.nn.sigmoid(gb) * (yb @ w_branch_b[l]))
        x = x + merged @ w_out[l]
    return _rmsnorm(x, final_norm_g)
```

```python
import os
import numpy as np
from contextlib import ExitStack
import concourse.bass as bass
import concourse.mybir as mybir
from concourse.bass_utils import run_bass_kernel_spmd

F32 = mybir.dt.float32
BF16 = mybir.dt.bfloat16
I32 = mybir.dt.int32
AF = mybir.ActivationFunctionType
ALU = mybir.AluOpType

NCORES = 8
D = 1024
L = 2048
DEPTH = 2
EPS = 1e-6
BLK = 64
SB_BASE = 16512
SB_END = 229376
TWO_PI = float(2 * np.pi)


class TT:
    def __init__(self, h, space, addr, esz, ncols):
        self.h, self.space, self.addr, self.esz, self.ncols = h, space, addr, esz, ncols

    def r(self, lo=0, hi=None):
        hi = self.ncols if hi is None else hi
        a0 = self.addr + lo * self.esz
        a1 = self.addr + hi * self.esz - 1
        if self.space == "ps":
            return [("ps", b) for b in range(a0 // 2048, a1 // 2048 + 1)]
        return [(self.space, b) for b in range(a0 // BLK, a1 // BLK + 1)]

    def __getitem__(self, k):
        return self.h[k]


class _Dummy:
    def then_inc(self, *a, **k):
        return self


class Rec:
    def __init__(self):
        self.items = []

    def __getattr__(self, nm):
        def f(*a, **kw):
            self.items.append((nm, a, kw))
            return _Dummy()
        return f


class Sched:
    ENG = ["pe", "act", "dve", "pool", "sp"]

    def __init__(self):
        self.ops = {e: [] for e in self.ENG}
        self.cnt = {e: 0 for e in self.ENG}
        self.waited = {}
        self.lw = {}
        self.rd = {}
        self.gtot = {}

    def op(self, eng, fn, reads=(), writes=(), group=None):
        tags = set()
        for b in reads:
            t = self.lw.get(b)
            if t:
                tags.add(t)
        for b in writes:
            t = self.lw.get(b)
            if t:
                tags.add(t)
            for t in self.rd.get(b, ()):
                tags.add(t)
        waits = []
        for (k, v) in sorted(tags, key=str):
            if isinstance(k, tuple):
                if group is not None and k[1] == group:
                    continue
                if self.waited.get((eng, k)):
                    continue
                self.waited[(eng, k)] = 1
                waits.append((k, None))
            else:
                if k == eng and eng == "pe" and group is None:
                    continue
                if self.waited.get((eng, k), 0) >= v:
                    continue
                self.waited[(eng, k)] = v
                waits.append((k, v))
        if group is None:
            self.cnt[eng] += 1
            tag = (eng, self.cnt[eng])
        else:
            self.gtot[group] = self.gtot.get(group, 0) + 16
            tag = (("g", group), 0)
        if callable(fn):
            rec = Rec()
            fn(rec)
            fn = rec.items
        self.ops[eng].append((waits, fn, group, tag[1] if group is None else None))
        for b in writes:
            self.lw[b] = tag
            self.rd[b] = []
        for b in reads:
            self.rd.setdefault(b, []).append(tag)

    def final_wait(self, eng, group):
        self.ops[eng].append(([(("g", group), None)], None, None, None))

    def emit(self, nc, es):
        sems = {e: es.enter_context(nc.semaphore("s_" + e)) for e in self.ENG}
        for g in self.gtot:
            sems[("g", g)] = es.enter_context(nc.semaphore("g_" + g))
        block = es.enter_context(nc.Block())
        import bisect
        ref = {e: set() for e in self.ENG}
        for e in self.ENG:
            for waits, fn, group, ordn in self.ops[e]:
                for (k, v) in waits:
                    if not isinstance(k, tuple):
                        ref[k].add(v)
        refl = {e: sorted(ref[e]) for e in self.ENG}

        def cnt_of(e, v):
            return bisect.bisect_right(refl[e], v)

        def run(eng_name):
            def body(eng):
                for _ in range(int(os.environ.get("KNOP_" + eng_name, "0"))):
                    eng.nop()
                for waits, fn, group, ordn in self.ops[eng_name]:
                    for (k, v) in waits:
                        if isinstance(k, tuple):
                            eng.wait_ge(sems[k], self.gtot[k[1]])
                        else:
                            eng.wait_ge(sems[k], cnt_of(k, v))
                    if fn is None:
                        continue
                    ins = None
                    for (nm, a, kw) in fn:
                        ins = getattr(eng, nm)(*a, **kw)
                    if group is None:
                        if ordn in ref[eng_name]:
                            ins.then_inc(sems[eng_name], 1)
                    else:
                        ins.then_inc(sems[("g", group)], 16)
            return body

        block.tensor(run("pe"))
        block.scalar(run("act"))
        block.vector(run("dve"))
        block.gpsimd(run("pool"))
        block.sync(run("sp"))


def build_nc():
    nc = bass.Bass("TRN2", target_bir_lowering=False)
    S = Sched()

    def din(name, shape):
        return nc.dram_tensor(name, list(shape), F32, kind="ExternalInput").ap()

    x_d = din("x", [L, D])
    norm_g_d = din("norm_g", [DEPTH, D])
    w_in_d = din("w_in", [DEPTH, D, 4096])
    b_in_d = din("b_in", [DEPTH, 4096])
    ldt_d = din("ssm_log_dt", [DEPTH, 32])
    lamr_d = din("ssm_lam_re", [DEPTH, 32, 64])
    lami_d = din("ssm_lam_im", [DEPTH, 32, 64])
    bre_d = din("ssm_b_re", [DEPTH, 32, 64, 16])
    bim_d = din("ssm_b_im", [DEPTH, 32, 64, 16])
    cre_d = din("ssm_c_re", [DEPTH, 512, 64])
    cim_d = din("ssm_c_im", [DEPTH, 512, 64])
    ssmd_d = din("ssm_d", [DEPTH, 32, 16])
    wglu_d = din("ssm_w_glu", [DEPTH, 512, 512])
    bglu_d = din("ssm_b_glu", [DEPTH, 512])
    poolw_d = din("pool_w", [DEPTH, 4, 128, 128])
    pscale_d = din("pool_scale", [DEPTH, 512])
    wa_d = din("w_branch_a", [DEPTH, 512, D])
    wb_d = din("w_branch_b", [DEPTH, 512, D])
    wout_d = din("w_out", [DEPTH, D, D])
    gfin_d = din("final_norm_g", [D])
    out_d = nc.dram_tensor("out", [L, D], F32, kind="ExternalOutput").ap()

    cur = [SB_BASE]
    names = [0]

    def alloc(ncols, dt, at=None):
        esz = 2 if dt == BF16 else 4
        if at is None:
            a = cur[0]
            cur[0] += (ncols * esz + 63) // 64 * 64
            assert cur[0] <= SB_END, "SBUF overflow %d" % cur[0]
        else:
            a = at
        names[0] += 1
        h = nc.alloc_sbuf_tensor_at("t%d" % names[0], [128, ncols], dt, offset=a)
        return TT(h, "sb", a, esz, ncols)

    X = alloc(16 * D, F32)
    YT = alloc(4 * L, BF16)
    WA = alloc(8 * 512, BF16)
    WGLU = alloc(4 * 512, BF16)
    WUB = alloc(8 * 512, BF16)
    WZB = alloc(8 * 512, BF16)
    WPOOL = alloc(4 * 128, BF16)
    WG = [alloc(8 * 128 * 2 + 4 * 128 * 2, BF16) for _ in range(2)]
    WOUT = alloc(8 * D, BF16)
    IDB = alloc(128, BF16)
    IDF = alloc(128, F32)
    IPM = alloc(256, BF16)
    CM = alloc(128, F32)
    JTAB = alloc(256, F32)
    ETAB = alloc(16, F32)
    ONES = alloc(128, BF16)
    INVC = alloc(64, F32)
    GTAB = alloc(D, F32)
    BPP = alloc(32, F32)
    BUA = alloc(512, BF16)
    BGLU = alloc(4, F32)
    PSC = alloc(4, F32)
    LDT = alloc(32, F32)
    LAMR = alloc(32, F32)
    LAMI = alloc(32, F32)
    BY1 = alloc(512, F32)
    BY2 = alloc(512, F32)
    DVEC = alloc(32, F32)
    HALO = alloc(64, F32)
    SMALL = alloc(64, F32)
    JD = alloc(128, F32)
    EPSB = alloc(16, F32)
    ONEB = alloc(16, F32)
    AR0 = cur[0]
    ARENA = SB_END - AR0
    assert ARENA >= 56 * 1024, ARENA

    def ar(off, ncols, dt):
        esz = 2 if dt == BF16 else 4
        assert off + ncols * esz <= ARENA, (off, ncols)
        return alloc(ncols, dt, at=AR0 + off)

    K = 1024
    HT = ar(0, 8 * 1024, BF16)
    A_TM = ar(16 * K, 4096, BF16)
    U = ar(24 * K, 32 * 256, BF16)
    CTR = ar(40 * K, 512, F32)
    CTI = ar(42 * K, 512, F32)
    P1 = ar(44 * K, 512, F32)
    P2 = ar(46 * K, 512, F32)
    P3 = ar(48 * K, 512, F32)
    GX1 = ar(50 * K, 256, F32)
    GX2 = ar(51 * K, 256, F32)
    R8 = ar(52 * K, 32, F32)
    X8 = ar(52 * K + 128, 32, F32)
    DTT = ar(52 * K + 256, 32, F32)
    LRD = ar(52 * K + 384, 32, F32)
    ANGV = ar(52 * K + 512, 32, F32)
    CNR = ar(0, 512, F32)
    CNI = ar(2 * K, 512, F32)
    ARE = ar(4 * K, 512, F32)
    AIM = ar(6 * K, 512, F32)
    TG = [ar(8 * K + i * 2 * K, 512, F32) for i in range(4)]
    TGI = alloc(512, I32, at=TG[3].addr)
    STG = ar(12 * K, 768, F32)
    COS = ar(0, 512, F32)
    SIN = ar(2 * K, 512, F32)
    VV = ar(4 * K, 512, F32)
    ZZ = ar(6 * K, 512, F32)
    TMPA = ar(10 * K, 384, F32)
    TMPB = ar(11 * K + 512, 256, F32)
    SINB = ar(12 * K + 512, 512, F32)
    COSB = ar(55936, 512, F32)
    CSET = [(COS, SIN, alloc(512, I32, at=COS.addr)), (COSB, SINB, alloc(512, I32, at=COSB.addr))]
    E1 = ar(14 * K + 512, 2 * 264, BF16)
    E2 = ar(15936, 2 * 264, BF16)
    TBL = []
    for i in range(2):
        o = 17 * K + i * 3584
        TBL.append((ar(o, 256, BF16), ar(o + 512, 256, BF16), ar(o + 1024, 256, BF16), ar(o + 1536, 256, BF16),
                    ar(o + 2048, 512, BF16), ar(o + 3072, 256, BF16)))
    BF_, CFM, CF1, CF2, BM12, M0 = TBL[0]
    YTM = [ar(53888, 1024, BF16), ar(8 * K, 1024, BF16)]
    YB = ar(16 * K, 4 * 1024, BF16)
    RT = [ar(16 * K + i * 2 * K, 512, F32) for i in range(2)]
    MG = ar(24 * K, 8 * 1024, BF16)
    PB = [ar(24 * K + i * K, 512, BF16) for i in range(4)]
    CH = [ar(40 * K + i * K, 512, BF16) for i in range(5)]
    SLB = [CH[4], ar(28 * K, 512, BF16)]
    UBX = [ar(29 * K + i * 2176, 528, F32) for i in range(4)]
    CF = [ar(45 * K + i * 2176, 512 + 16, F32) for i in range(4)]
    XN = ar(45 * K + 4 * 2176, 1024, BF16)
    XN2 = ar(45 * K, 1024, BF16)
    JUNK = alloc(512, mybir.dt.int8, at=AR0 + 45 * K + 4 * 2176 + 2048)
    JUNK.esz = 1
    SMALL2 = ar(45 * K + 4 * 2176 + 2048 + 512, 128, F32)
    print("ARENA", ARENA, "XN end", 45 * K + 4 * 2176 + 2048)
    psh = es_global.enter_context(nc.psum_tensor("PS", [128, 4096], F32))
    PS = TT(psh, "ps", 0, 4, 4096)
    psb = [0]

    def bank(n=1):
        b = psb[0]
        if b + n > 8:
            b = 0
        psb[0] = (b + n) % 8
        return b * 512

    def cp(eng_name):
        return eng_name

    def dma(q, out_ap, in_ap, group, reads=(), writes=(), slow=False):
        def fn(e):
            if slow:
                return e.dma_start(out=out_ap, in_=in_ap, allow_slow_non_contiguous=True)
            return e.dma_start(out=out_ap, in_=in_ap)
        S.op(q, fn, reads, writes, group=group + "_" + q)

    def v3(ap, **kw):
        return ap

    def _c(e):
        return e.iota(IDF[:, :], [[1, 128]], base=0, channel_multiplier=-1, allow_small_or_imprecise_dtypes=True)
    S.op("pool", _c, writes=IDF.r())
    S.op("pool", lambda e: e.iota(CM[:, :], [[16, 8], [0, 16]], base=0, channel_multiplier=1,
                                  allow_small_or_imprecise_dtypes=True), writes=CM.r())
    S.op("pool", lambda e: e.iota(JTAB[:, :], [[1, 256]], base=0, channel_multiplier=0,
                                  allow_small_or_imprecise_dtypes=True), writes=JTAB.r())
    S.op("pool", lambda e: e.iota(ETAB[:, :], [[1, 16]], base=-7, channel_multiplier=0,
                                  allow_small_or_imprecise_dtypes=True), writes=ETAB.r())
    S.op("dve", lambda e: e.tensor_scalar(out=IPM[:, 0:128], in0=IDF[:, :], scalar1=0.0, scalar2=None, op0=ALU.is_equal),
         reads=IDF.r(), writes=IPM.r(0, 128))
    S.op("dve", lambda e: e.tensor_scalar(out=TG[0][:, 0:128], in0=IDF[:, :], scalar1=-64.0, scalar2=None, op0=ALU.is_equal),
         reads=IDF.r(), writes=TG[0].r(0, 128))
    S.op("dve", lambda e: e.tensor_scalar(out=TG[1][:, 0:128], in0=IDF[:, :], scalar1=64.0, scalar2=None, op0=ALU.is_equal),
         reads=IDF.r(), writes=TG[1].r(0, 128))
    S.op("dve", lambda e: e.tensor_tensor(out=IPM[:, 128:256], in0=TG[0][:, 0:128], in1=TG[1][:, 0:128], op=ALU.subtract),
         reads=TG[0].r(0, 128) + TG[1].r(0, 128), writes=IPM.r(128, 256))
    S.op("dve", lambda e: e.tensor_copy(out=IDB[:, :], in_=IPM[:, 0:128]), reads=IPM.r(0, 128), writes=IDB.r())
    S.op("dve", lambda e: e.tensor_scalar(out=IDF[:, :], in0=IDF[:, :], scalar1=0.0, scalar2=None, op0=ALU.is_equal),
         reads=IDF.r() + IPM.r() + TG[0].r(0, 128) + TG[1].r(0, 128), writes=IDF.r())
    S.op("dve", lambda e: e.tensor_scalar(out=CM[:, :], in0=CM[:, :], scalar1=112.0, scalar2=None, op0=ALU.is_ge),
         reads=CM.r(), writes=CM.r())
    S.op("dve", lambda e: e.memset(ONES[:, :], 1.0), writes=ONES.r())
    S.op("pool", lambda e: e.iota(JD[:, :], [[-16, 8], [1, 16]], base=112, channel_multiplier=-1,
                                  allow_small_or_imprecise_dtypes=True), writes=JD.r())
    S.op("dve", lambda e: e.tensor_scalar(out=JD[:, :], in0=JD[:, :], scalar1=0.0, scalar2=None, op0=ALU.is_equal),
         reads=JD.r(), writes=JD.r())
    for wi, win in enumerate((2, 4, 8, 16)):
        S.op("dve", lambda e, wi=wi, win=win: e.tensor_scalar(out=INVC[:, wi * 16:(wi + 1) * 16], in0=JTAB[:, 0:16],
                                                              scalar1=1.0, scalar2=float(win), op0=ALU.add, op1=ALU.min),
             reads=JTAB.r(0, 16), writes=INVC.r(wi * 16, wi * 16 + 16))
    S.op("dve", lambda e: e.reciprocal(out=INVC[:, :], in_=INVC[:, :]), reads=INVC.r(), writes=INVC.r())

    class NormPipe:
        def __init__(self, h, xbufs):
            self.h = h
            self.pending = None
            self.xbufs = xbufs

        def front(self, t8):
            h = self.h
            tt = h * 8 + t8
            xn = self.xbufs[t8 % len(self.xbufs)]
            xs = X[:, tt * D:(tt + 1) * D]
            xr = X.r(tt * D, (tt + 1) * D)
            c0 = 16 * (tt % 8)
            sc = SMALL2[:, c0:c0 + 1]
            sc2 = SMALL2[:, c0 + 1:c0 + 2]
            scr = SMALL2.r(c0, c0 + 16)
            S.op("act", lambda e: e.activation(out=JUNK[:, :], in_=xs[:, 0:512], func=AF.Square, accum_out=sc), reads=xr, writes=JUNK.r() + scr)
            S.op("act", lambda e: e.activation(out=JUNK[:, :], in_=xs[:, 512:1024], func=AF.Square, accum_out=sc2), reads=xr, writes=JUNK.r() + scr)
            S.op("dve", lambda e: e.tensor_tensor(out=sc, in0=sc, in1=sc2, op=ALU.add), reads=scr, writes=scr)
            S.op("act", lambda e: e.activation(out=sc, in_=sc, func=AF.Sqrt, scale=1.0 / D, bias=EPSB[:, 0:1]), reads=scr + EPSB.r(), writes=scr)
            S.op("dve", lambda e: e.reciprocal(out=sc, in_=sc), reads=scr, writes=scr)
            S.op("dve", lambda e: e.scalar_tensor_tensor(out=xn[:, 0:1024], in0=xs, scalar=sc, in1=GTAB[:, :], op0=ALU.mult, op1=ALU.mult),
                 reads=xr + scr + GTAB.r(), writes=xn.r(0, 1024))

        def back(self, t8):
            htv = HT[:, :].rearrange("p (k n) -> p k n", k=8)
            xn = self.xbufs[t8 % len(self.xbufs)]
            b = bank()
            psb16 = PS[:, b:b + 512].bitcast(BF16)

            def tr(e):
                ins = None
                for k in range(8):
                    ins = e.transpose(psb16[:, k * 128:(k + 1) * 128], xn[:, k * 128:(k + 1) * 128], IDB[:, :])
                return ins
            S.op("pe", tr, reads=xn.r(0, 1024) + IDB.r(), writes=PS.r(b, b + 512))
            if self.pending is not None:
                self.pending()

            def evac():
                S.op("dve", lambda e: e.tensor_copy(out=htv[:, :, t8 * 128:(t8 + 1) * 128], in_=psb16.rearrange("p (k n) -> p k n", k=8)),
                     reads=PS.r(b, b + 512), writes=sum([HT.r(k * 1024 + t8 * 128, k * 1024 + (t8 + 1) * 128) for k in range(8)], []))
            self.pending = evac

        def flush(self):
            if self.pending is not None:
                self.pending()
                self.pending = None

    def norm_half(h):
        np_ = NormPipe(h, [XN])
        for t8 in range(8):
            np_.front(t8)
            np_.back(t8)
        np_.flush()

    S.op("dve", lambda e: e.memset(EPSB[:, :], EPS), writes=EPSB.r())
    S.op("dve", lambda e: e.memset(ONEB[:, :], 1.0), writes=ONEB.r())

    def load_w(dst, dst_lo, src_ap_kpn, ncols, group, ktiles):
        hi = dst_lo + ktiles * ncols
        dma("pool", dst[:, dst_lo:hi].rearrange("p (k n) -> p k n", k=ktiles),
            src_ap_kpn.rearrange("(k p) n -> p k n", p=128), group, writes=dst.r(dst_lo, hi))

    def load_wg(l, o, wg, grp):
        dma("pool", wg[:, 0:1024].rearrange("p (k n) -> p k n", k=8),
            w_in_d[l, :, 2048 + o * 128:2048 + (o + 1) * 128].rearrange("(k p) n -> p k n", p=128), grp, writes=wg.r(0, 1024))
        dma("pool", wg[:, 1024:2048].rearrange("p (k n) -> p k n", k=8),
            w_in_d[l, :, 3072 + o * 128:3072 + (o + 1) * 128].rearrange("(k p) n -> p k n", p=128), grp, writes=wg.r(1024, 2048))
        dma("pool", wg[:, 2048:2560].rearrange("p (k n) -> p k n", k=4),
            wa_d[l, :, o * 128:(o + 1) * 128].rearrange("(k p) n -> p k n", p=128), grp, writes=wg.r(2048, 2560))
        dma("pool", wg[:, 2560:3072].rearrange("p (k n) -> p k n", k=4),
            wb_d[l, :, o * 128:(o + 1) * 128].rearrange("(k p) n -> p k n", p=128), grp, writes=wg.r(2560, 3072))

    KSTOP = int(os.environ.get("KSTOP", "99"))

    class StopBuild(Exception):
        pass

    def stop(n):
        if KSTOP <= n:
            raise StopBuild()

    GF = ar(0, D, F32)
    OUTT = [ar(4 * K + i * 4 * K, D, F32) for i in range(3)]
    fin_state = {"gf": False}

    def final_tile(tt):
        if not fin_state["gf"]:
            dma("sp", GF[:, :], gfin_d.partition_broadcast(128), "gfin", writes=GF.r())
            fin_state["gf"] = True
        xs = X[:, tt * D:(tt + 1) * D]
        xr = X.r(tt * D, (tt + 1) * D)
        c0 = 16 * (tt % 8)
        sc = SMALL2[:, c0:c0 + 1]
        sc2 = SMALL2[:, c0 + 1:c0 + 2]
        scr = SMALL2.r(c0, c0 + 16)
        ot = OUTT[tt % 3]
        S.op("act", lambda e: e.activation(out=JUNK[:, :], in_=xs[:, 0:512], func=AF.Square, accum_out=sc), reads=xr, writes=JUNK.r() + scr)
        S.op("act", lambda e: e.activation(out=JUNK[:, :], in_=xs[:, 512:1024], func=AF.Square, accum_out=sc2), reads=xr, writes=JUNK.r() + scr)
        S.op("dve", lambda e: e.tensor_tensor(out=sc, in0=sc, in1=sc2, op=ALU.add), reads=scr, writes=scr)
        S.op("act", lambda e: e.activation(out=sc, in_=sc, func=AF.Sqrt, scale=1.0 / D, bias=EPSB[:, 0:1]), reads=scr + EPSB.r(), writes=scr)
        S.op("dve", lambda e: e.reciprocal(out=sc, in_=sc), reads=scr, writes=scr)
        S.op("dve", lambda e: e.scalar_tensor_tensor(out=ot[:, :], in0=xs, scalar=sc, in1=GF[:, :], op0=ALU.mult, op1=ALU.mult),
             reads=xr + scr + GF.r(), writes=ot.r())
        dma("sp", out_d[tt * 128:(tt + 1) * 128, :], ot[:, :], "outst%d" % tt, reads=ot.r())

    def ew(eng, fn, reads, writes):
        S.op(eng, fn, reads, writes)

    def g32(t):
        return t[:, 0:32]

    def v16(t):
        return t[:, :].rearrange("p (g e) -> p g e", g=32)

    def prologue_early(l):
        gp = "L%d" % l
        by1v = BY1[:, :].rearrange("p (g c) -> p g c", g=32)
        by2v = BY2[:, :].rearrange("p (g c) -> p g c", g=32)
        dma("sp", by1v[0:64], bre_d[l].rearrange("g p c -> p g c"), gp + "pe", writes=BY1.r())
        dma("sp", by1v[64:128], bim_d[l].rearrange("g p c -> p g c"), gp + "pe", writes=BY1.r())
        dma("sp", by2v[0:64], bim_d[l].rearrange("g p c -> p g c"), gp + "pe", writes=BY2.r())
        dma("sp", by2v[64:128], bre_d[l].rearrange("g p c -> p g c"), gp + "pe", writes=BY2.r())

    def prologue_dma(l):
        gp = "L%d" % l
        dma("sp", GTAB[:, :], norm_g_d[l].partition_broadcast(128), gp + "par", writes=GTAB.r())
        dma("sp", LDT[:, :], ldt_d[l].partition_broadcast(128), gp + "par", writes=LDT.r())
        dma("pool", BUA[0:1, :], b_in_d[l, 0:512].rearrange("(o n) -> o n", o=1), gp + "par", writes=BUA.r())
        S.op("dve", lambda e: e.memset(STG[:, :], 0.0), writes=STG.r())
        for hh in range(2):
            dma("sp", STG[0:32, hh * 64:(hh + 1) * 64], lamr_d[l], gp + "stg", writes=STG.r())
            dma("sp", STG[0:32, 128 + hh * 64:128 + (hh + 1) * 64], lami_d[l], gp + "stg", writes=STG.r())
        for q in range(8):
            dma("sp", STG[0:32, 256 + q * 16:256 + (q + 1) * 16], ssmd_d[l], gp + "stg", writes=STG.r())
        dma("sp", STG[0:28, 384:512], b_in_d[l, 512:4096].rearrange("(t p) -> t p", p=128), gp + "stg", writes=STG.r())
        dma("sp", STG[0:4, 512:640], bglu_d[l].rearrange("(t p) -> t p", p=128), gp + "stg", writes=STG.r())
        dma("sp", STG[0:4, 640:768], pscale_d[l].rearrange("(t p) -> t p", p=128), gp + "stg", writes=STG.r())
        cnrv = CNR[:, :].rearrange("p (i n) -> p i n", i=4)
        cniv = CNI[:, :].rearrange("p (i n) -> p i n", i=4)
        for hh in range(2):
            dma("sp", cnrv[:, :, hh * 64:(hh + 1) * 64], cre_d[l].rearrange("(i p) n -> p i n", p=128), gp + "par", writes=CNR.r())
            dma("sp", cniv[:, :, hh * 64:(hh + 1) * 64], cim_d[l].rearrange("(i p) n -> p i n", p=128), gp + "par", writes=CNI.r())
        if l == 0:
            xv = X[:, :].rearrange("p (t d) -> p t d", t=16)
            for tt in range(16):
                dma("sp", xv[:, tt, :], x_d[tt * 128:(tt + 1) * 128, :], "xin%d" % tt, writes=X.r(tt * D, (tt + 1) * D))
        load_w(WA, 0, w_in_d[l, :, 0:512], 512, gp + "wua", 8)
        load_w(WGLU, 0, wglu_d[l], 512, gp + "wglu", 4)
        load_w(WUB, 0, w_in_d[l, :, 1024:1536], 512, gp + "wub", 8)
        load_w(WZB, 0, w_in_d[l, :, 1536:2048], 512, gp + "wzb", 8)
        dma("pool", WPOOL[:, :].rearrange("p (w n) -> p w n", w=4), poolw_d[l].rearrange("w p n -> p w n"), gp + "wpool", writes=WPOOL.r())
        wg_use = [0]
        load_wg(l, 0, WG[0], gp + "wgu0")
        load_wg(l, 1, WG[1], gp + "wgu1")


    def prologue_compute(l):
        gp = "L%d" % l
        stop(1)
        bst = bank()

        def trs(e, bst=bst):
            ins = None
            for k in range(6):
                ins = e.transpose(PS[:, bst + k * 32:bst + (k + 1) * 32], STG[0:32, k * 128:(k + 1) * 128], IDF[0:32, 0:32])
            return ins
        S.op("pe", trs, reads=STG.r() + IDF.r(), writes=PS.r(bst, bst + 192))
        for (dst, k, n) in ((LAMR, 0, 32), (LAMI, 1, 32), (DVEC, 2, 32), (BPP, 3, 28), (BGLU, 4, 4), (PSC, 5, 4)):
            S.op("dve", lambda e, dst=dst, k=k, n=n, bst=bst: e.tensor_copy(out=dst[:, 0:n], in_=PS[:, bst + k * 32:bst + k * 32 + n]),
                 reads=PS.r(bst, bst + 192), writes=dst.r())
        def ew(eng, fn, reads, writes):
            S.op(eng, fn, reads, writes)

        def g32(t):
            return t[:, 0:32]

        T0, T1, T2, T3 = TG
        for (src, dst) in ((CNR, CTR), (CNI, CTI)):
            b = bank()

            def trc(e, src=src, b=b):
                ins = None
                for i in range(4):
                    ins = e.transpose(PS[:, b + i * 128:b + (i + 1) * 128], src[:, i * 128:(i + 1) * 128], IDF[:, :])
                return ins
            S.op("pe", trc, reads=src.r() + IDF.r(), writes=PS.r(b, b + 512))
            S.op("dve", lambda e, dst=dst, b=b: e.tensor_copy(out=dst[:, :], in_=PS[:, b:b + 512]),
                 reads=PS.r(b, b + 512), writes=dst.r())
        LN2H, LN2L = 0.693359375, -2.12194440e-4
        tA, tB, tC = T0[:, 0:32], T0[:, 32:64], T0[:, 64:96]
        R0 = T0.r()
        ew("dve", lambda e: e.tensor_scalar(out=tA, in0=LDT[:, :], scalar1=1.4426950408889634, scalar2=None, op0=ALU.mult), LDT.r(), R0)
        ew("dve", lambda e: e.tensor_copy(out=TGI[:, 0:32], in_=tA), R0, TGI.r())
        ew("dve", lambda e: e.tensor_copy(out=tA, in_=TGI[:, 0:32]), TGI.r(), R0)
        ew("dve", lambda e: e.scalar_tensor_tensor(out=tB, in0=tA, scalar=-LN2H, in1=LDT[:, :], op0=ALU.mult, op1=ALU.add), R0 + LDT.r(), R0)
        ew("dve", lambda e: e.scalar_tensor_tensor(out=tB, in0=tA, scalar=-LN2L, in1=tB, op0=ALU.mult, op1=ALU.add), R0, R0)
        ew("dve", lambda e: e.tensor_scalar(out=tC, in0=tB, scalar1=1.0 / 9, scalar2=1.0, op0=ALU.mult, op1=ALU.add), R0, R0)
        for kk in (8, 7, 6, 5, 4, 3, 2, 1):
            ew("dve", lambda e: e.tensor_tensor(out=tC, in0=tC, in1=tB, op=ALU.mult), R0, R0)
            ew("dve", lambda e, kk=kk: e.tensor_scalar(out=tC, in0=tC, scalar1=1.0 / kk, scalar2=1.0, op0=ALU.mult, op1=ALU.add), R0, R0)
        ew("dve", lambda e: e.tensor_scalar(out=tA, in0=tA, scalar1=127.0, scalar2=8388608.0, op0=ALU.add, op1=ALU.mult), R0, R0)
        ew("dve", lambda e: e.tensor_copy(out=TGI[:, 0:32], in_=tA), R0, TGI.r())
        ew("dve", lambda e: e.tensor_tensor(out=g32(DTT), in0=tC, in1=TGI[:, 0:32].bitcast(F32), op=ALU.mult), R0 + TGI.r(), DTT.r())
        ew("dve", lambda e: e.tensor_tensor(out=g32(LRD), in0=LAMR[:, :], in1=g32(DTT), op=ALU.mult), LAMR.r() + DTT.r(), LRD.r())
        ew("dve", lambda e: e.tensor_tensor(out=g32(ANGV), in0=LAMI[:, :], in1=g32(DTT), op=ALU.mult), LAMI.r() + DTT.r(), ANGV.r())
        ew("dve", lambda e: e.tensor_scalar(out=g32(ANGV), in0=g32(ANGV), scalar1=1.0 / TWO_PI, scalar2=None, op0=ALU.mult),
           ANGV.r(), ANGV.r())
        e3 = ETAB[:, :].rearrange("p (o e) -> p o e", o=1).to_broadcast([128, 32, 16])
        a3 = g32(ANGV).rearrange("p (g o) -> p g o", o=1).to_broadcast([128, 32, 16])
        l3 = g32(LRD).rearrange("p (g o) -> p g o", o=1).to_broadcast([128, 32, 16])

        def v16(t):
            return t[:, :].rearrange("p (g e) -> p g e", g=32)

        def sincos(vsrc_reads, make_v, n, out_sin, out_cos, tv, tc, ti, ri_reads=()):
            make_v()
            for (outt, shift, tw) in ((out_sin, 0.0, tv), (out_cos, 0.25, tc)):
                if outt is None:
                    continue
                if shift != 0.0:
                    ew("dve", lambda e, tw=tw: e.tensor_scalar(out=tw[:, 0:n], in0=tv[:, 0:n], scalar1=shift, scalar2=None, op0=ALU.add),
                       tv.r(0, n), tw.r(0, n))
                ew("dve", lambda e, tw=tw: e.tensor_copy(out=ti[:, 0:n], in_=tw[:, 0:n]), tw.r(0, n), ti.r(0, n))
                ew("dve", lambda e, tw=tw: e.tensor_tensor(out=tw[:, 0:n], in0=tw[:, 0:n], in1=ti[:, 0:n], op=ALU.subtract),
                   tw.r(0, n) + ti.r(0, n), tw.r(0, n))
                ew("act", lambda e, tw=tw, outt=outt: e.activation(out=outt[:, 0:n], in_=tw[:, 0:n], func=AF.Sin, scale=TWO_PI * 0.99999),
                   tw.r(0, n), outt.r(0, n))

        def mk_v_pow():
            ew("dve", lambda e: e.tensor_tensor(out=v16(T0), in0=a3, in1=e3, op=ALU.mult), ANGV.r() + ETAB.r(), T0.r())
        mk_v_pow()
        ew("dve", lambda e: e.tensor_scalar(out=T1[:, :], in0=T0[:, :], scalar1=0.25, scalar2=None, op0=ALU.add), T0.r(), T1.r())
        for tw, outt in ((T1, AIM), (T0, ARE)):
            ew("dve", lambda e, tw=tw: e.tensor_copy(out=TGI[:, :], in_=tw[:, :]), tw.r(), TGI.r())
            ew("dve", lambda e, tw=tw: e.tensor_tensor(out=tw[:, :], in0=tw[:, :], in1=TGI[:, :], op=ALU.subtract), tw.r() + TGI.r(), tw.r())
            ew("act", lambda e, tw=tw, outt=outt: e.activation(out=outt[:, :], in_=tw[:, :], func=AF.Sin, scale=TWO_PI * 0.99999),
               tw.r(), outt.r())
        ew("dve", lambda e: e.tensor_tensor(out=v16(T2), in0=l3, in1=e3, op=ALU.mult), LRD.r() + ETAB.r(), T2.r())
        ew("act", lambda e: e.activation(out=T2[:, :], in_=T2[:, :], func=AF.Exp), T2.r(), T2.r())
        ew("dve", lambda e: e.tensor_tensor(out=T0[:, :], in0=AIM[:, :], in1=T2[:, :], op=ALU.mult), AIM.r() + T2.r(), T0.r())
        ew("dve", lambda e: e.tensor_tensor(out=T1[:, :], in0=ARE[:, :], in1=T2[:, :], op=ALU.mult), ARE.r() + T2.r(), T1.r())
        ew("dve", lambda e: e.tensor_copy(out=ARE[:, :], in_=T0[:, :]), T0.r(), ARE.r())
        ew("dve", lambda e: e.tensor_copy(out=AIM[:, :], in_=T1[:, :]), T1.r(), AIM.r())
        for (PT, top, ts_, bot, bs_) in ((P1, ARE, 1.0, AIM, -1.0), (P2, AIM, -1.0, ARE, -1.0), (P3, ARE, -1.0, AIM, 1.0)):
            ew("dve", lambda e, PT=PT, top=top, ts_=ts_: e.tensor_scalar(out=PT[0:64, :], in0=top[0:64, :], scalar1=ts_, scalar2=None, op0=ALU.mult),
               top.r(), PT.r())
            ew("dve", lambda e, PT=PT, bot=bot, bs_=bs_: e.tensor_scalar(out=PT[64:128, :], in0=bot[64:128, :], scalar1=bs_, scalar2=None, op0=ALU.mult),
               bot.r(), PT.r())
        are1 = v16(ARE)[:, :, 8]
        aim1 = v16(AIM)[:, :, 8]
        c0, c1, c2, c3 = (T2[:, 0:32], T2[:, 32:64], T2[:, 64:96], T2[:, 96:128])
        c4, c5, c6, c7 = (T2[:, 128:160], T2[:, 160:192], T2[:, 192:224], T2[:, 224:256])
        RW = T2.r()
        AR_ = ARE.r() + AIM.r()
        ew("dve", lambda e: e.tensor_scalar(out=c0, in0=are1, scalar1=-1.0, scalar2=None, op0=ALU.add), AR_ + RW, RW)
        ew("dve", lambda e: e.tensor_tensor(out=c1, in0=LAMR[:, :], in1=LAMR[:, :], op=ALU.mult), LAMR.r() + RW, RW)
        ew("dve", lambda e: e.tensor_tensor(out=c2, in0=LAMI[:, :], in1=LAMI[:, :], op=ALU.mult), LAMI.r() + RW, RW)
        ew("dve", lambda e: e.tensor_tensor(out=c1, in0=c1, in1=c2, op=ALU.add), RW, RW)
        ew("dve", lambda e: e.reciprocal(out=c1, in_=c1), RW, RW)
        ew("dve", lambda e: e.tensor_tensor(out=c2, in0=c0, in1=LAMR[:, :], op=ALU.mult), LAMR.r() + RW, RW)
        ew("dve", lambda e: e.tensor_tensor(out=c3, in0=aim1, in1=LAMI[:, :], op=ALU.mult), AR_ + LAMI.r() + RW, RW)
        ew("dve", lambda e: e.tensor_tensor(out=c2, in0=c2, in1=c3, op=ALU.add), RW, RW)
        ew("dve", lambda e: e.tensor_tensor(out=c4, in0=c2, in1=c1, op=ALU.mult), RW, RW)
        ew("dve", lambda e: e.tensor_tensor(out=c2, in0=aim1, in1=LAMR[:, :], op=ALU.mult), AR_ + LAMR.r() + RW, RW)
        ew("dve", lambda e: e.tensor_tensor(out=c3, in0=c0, in1=LAMI[:, :], op=ALU.mult), LAMI.r() + RW, RW)
        ew("dve", lambda e: e.tensor_tensor(out=c2, in0=c2, in1=c3, op=ALU.subtract), RW, RW)
        ew("dve", lambda e: e.tensor_tensor(out=c5, in0=c2, in1=c1, op=ALU.mult), RW, RW)
        arq = v16(ARE)[:, :, 7:15]
        aiq = v16(AIM)[:, :, 7:15]
        cr3 = c4.rearrange("p (g o) -> p g o", o=1).to_broadcast([128, 32, 8])
        ci3 = c5.rearrange("p (g o) -> p g o", o=1).to_broadcast([128, 32, 8])

        def v8(t, lo=0):
            return t[:, lo:lo + 256].rearrange("p (g q) -> p g q", g=32)
        ew("dve", lambda e: e.tensor_tensor(out=v8(T0), in0=arq, in1=cr3, op=ALU.mult), AR_ + RW, T0.r())
        ew("dve", lambda e: e.tensor_tensor(out=v8(T0, 256), in0=aiq, in1=ci3, op=ALU.mult), AR_ + RW, T0.r())
        ew("dve", lambda e: e.tensor_tensor(out=v8(T1), in0=arq, in1=ci3, op=ALU.mult), AR_ + RW, T1.r())
        ew("dve", lambda e: e.tensor_tensor(out=v8(T1, 256), in0=aiq, in1=cr3, op=ALU.mult), AR_ + RW, T1.r())
        ew("dve", lambda e: e.tensor_tensor(out=GX1[:, :], in0=T0[:, 0:256], in1=T0[:, 256:512], op=ALU.subtract), T0.r(), GX1.r())
        ew("dve", lambda e: e.tensor_tensor(out=GX2[:, :], in0=T1[:, 0:256], in1=T1[:, 256:512], op=ALU.add), T1.r(), GX2.r())
        ew("dve", lambda e: e.tensor_scalar(out=GX2[0:64, :], in0=GX2[0:64, :], scalar1=-1.0, scalar2=None, op0=ALU.mult), GX2.r(), GX2.r())
        ew("act", lambda e: e.activation(out=g32(R8), in_=g32(LRD), func=AF.Exp, scale=8.0), LRD.r(), R8.r())
        ew("dve", lambda e: e.tensor_scalar(out=g32(X8), in0=g32(ANGV), scalar1=8.0, scalar2=None, op0=ALU.mult), ANGV.r(), X8.r())
        ew("dve", lambda e: e.tensor_copy(out=TGI[:, 0:32], in_=g32(X8)), X8.r(), TGI.r())
        ew("dve", lambda e: e.tensor_tensor(out=g32(X8), in0=g32(X8), in1=TGI[:, 0:32], op=ALU.subtract), X8.r() + TGI.r(), X8.r())


    for l in range(DEPTH if KSTOP > 0 else 0):
      try:
            gp = "L%d" % l
            if l == 0:
                prologue_dma(0)
                prologue_early(0)
                prologue_compute(0)
            load_w(WOUT, 0, wout_d[l], D, gp + "wout", 8)
            stop(2)
            htv4 = HT[:, :].rearrange("p (k j s) -> p k j s", k=8, s=8)
            wav = WA[:, :].rearrange("p (k n) -> p k n", k=8)
            atv = A_TM[:, :].rearrange("p (g q c) -> p g q c", g=32, q=8)
            uv = U[:, :].rearrange("p (g j) -> p g j", g=32)
            for h in range(2):
                norm_half(h)
                for s in range(8):
                    b = bank()

                    def mm(e, s=s, b=b):
                        for k in range(8):
                            e.matmul(PS[:, b:b + 512], lhsT=htv4[:, k, :, s], rhs=wav[:, k, :], start=(k == 0), stop=False)
                        return e.matmul(PS[:, b:b + 512], lhsT=ONES[0:1, :], rhs=BUA[0:1, :], start=False, stop=True)
                    S.op("pe", mm, reads=HT.r() + WA.r() + ONES.r() + BUA.r(), writes=PS.r(b, b + 512))
                    q = 7 - s
                    S.op("act" if s % 2 else "dve",
                         (lambda e, b=b, q=q: e.activation(out=atv[:, :, q, :], in_=PS[:, b:b + 512].rearrange("p (g c) -> p g c", g=32), func=AF.Copy))
                         if s % 2 else
                         (lambda e, b=b, q=q: e.tensor_copy(out=atv[:, :, q, :], in_=PS[:, b:b + 512].rearrange("p (g c) -> p g c", g=32))),
                         reads=PS.r(b, b + 512), writes=A_TM.r())
                for g4 in range(8):
                    b = bank()
                    psb16 = PS[:, b:b + 512].bitcast(BF16)

                    def tr(e, g4=g4, psb16=psb16):
                        ins = None
                        for gi in range(4):
                            g = g4 * 4 + gi
                            ins = e.transpose(psb16[:, gi * 128:(gi + 1) * 128], A_TM[:, g * 128:(g + 1) * 128], IDB[:, :])
                        return ins
                    S.op("pe", tr, reads=A_TM.r() + IDB.r(), writes=PS.r(b, b + 512))
                    S.op("act" if g4 % 2 else "dve",
                         (lambda e, g4=g4, psb16=psb16, h=h: e.activation(out=uv[:, g4 * 4:(g4 + 1) * 4, h * 128:(h + 1) * 128],
                                                                          in_=psb16[:, 0:512].rearrange("p (g j) -> p g j", g=4), func=AF.Copy))
                         if g4 % 2 else
                         (lambda e, g4=g4, psb16=psb16, h=h: e.tensor_copy(out=uv[:, g4 * 4:(g4 + 1) * 4, h * 128:(h + 1) * 128],
                                                                           in_=psb16[:, 0:512].rearrange("p (g j) -> p g j", g=4))),
                         reads=PS.r(b, b + 512), writes=U.r(g4 * 4 * 256, (g4 + 1) * 4 * 256))

            stop(3)
            load_w(WA, 0, w_in_d[l, :, 512:1024], 512, gp + "wza", 8)
            ctr3 = CTR[:, :].rearrange("p (g c) -> p g c", g=32)
            cti3 = CTI[:, :].rearrange("p (g c) -> p g c", g=32)
            by13 = BY1[:, :].rearrange("p (g c) -> p g c", g=32)
            by23 = BY2[:, :].rearrange("p (g c) -> p g c", g=32)
            gx13 = GX1[:, :].rearrange("p (g q) -> p g q", g=32)
            gx23 = GX2[:, :].rearrange("p (g q) -> p g q", g=32)
            p13 = v16(P1)
            p23 = v16(P2)
            p33 = v16(P3)
            ytv = YT[:, :].rearrange("p (i j t) -> p i j t", i=4, t=8)
            def bx(v):
                return v.rearrange("p g (a o) -> p g a o", o=1).to_broadcast([128, 2, 8, 16])

            def by(v):
                return v.rearrange("p g (o b) -> p g o b", o=1).to_broadcast([128, 2, 8, 16])

            def outer(dst, Xs, Ys, X2s, Y2s, rd, part=None):
                d4 = dst[:, :].rearrange("p (g a b) -> p g a b", g=2, a=8)
                ta = TMPA[:, 0:256].rearrange("p (g a b) -> p g a b", g=2, a=8)
                tb = TMPB[:, 0:256].rearrange("p (g a b) -> p g a b", g=2, a=8)
                if part in (None, 0):
                    ew("pool", lambda e: e.tensor_tensor(out=ta, in0=bx(Xs), in1=by(Ys), op=ALU.mult), rd, TMPA.r(0, 256))
                    ew("pool", lambda e: e.tensor_tensor(out=tb, in0=bx(X2s), in1=by(Y2s), op=ALU.mult), rd, TMPB.r(0, 256))
                if part in (None, 1):
                    ew("dve", lambda e: e.tensor_tensor(out=d4, in0=ta, in1=tb, op=ALU.add), TMPA.r(0, 256) + TMPB.r(0, 256), dst.r())

            rdB = GX1.r() + GX2.r() + BY1.r() + BY2.r()
            rdC = P1.r() + P2.r() + P3.r() + CTR.r() + CTI.r()

            def stage_T0(bt):
                g0 = bt * 2
                BF_ = TBL[bt % 2][0]
                gs = slice(g0, g0 + 2)
                outer(BF_, gx13[:, gs, :], by13[:, gs, :], gx23[:, gs, :], by23[:, gs, :], rdB, part=0)

            def stage_T(bt, first_done=False, mid_hook=None):
                g0 = bt * 2
                BF_, CFM, CF1, CF2, BM12, M0 = TBL[bt % 2]
                gs = slice(g0, g0 + 2)
                outer(BF_, gx13[:, gs, :], by13[:, gs, :], gx23[:, gs, :], by23[:, gs, :], rdB, part=(1 if first_done else None))
                outer(CFM, p13[:, gs, 0:8], ctr3[:, gs, :], p23[:, gs, 0:8], cti3[:, gs, :], rdC)
                outer(CF1, p13[:, gs, 8:16], ctr3[:, gs, :], p23[:, gs, 8:16], cti3[:, gs, :], rdC, part=0)
                if mid_hook is not None:
                    mid_hook()
                outer(CF1, p13[:, gs, 8:16], ctr3[:, gs, :], p23[:, gs, 8:16], cti3[:, gs, :], rdC, part=1)
                outer(CF2, p23[:, gs, 8:16], ctr3[:, gs, :], p33[:, gs, 8:16], cti3[:, gs, :], rdC)
                b = bank(2)
                for gi in range(2):
                    def mt(e, gi=gi, b=b):
                        o = b + gi * 512
                        e.matmul(PS[:, o:o + 256], lhsT=BF_[:, gi * 128:(gi + 1) * 128], rhs=IPM[:, :], start=True, stop=True)
                        return e.matmul(PS[:, o + 256:o + 384], lhsT=BF_[:, gi * 128:(gi + 1) * 128], rhs=CFM[:, gi * 128:(gi + 1) * 128],
                                        start=True, stop=True)
                    S.op("pe", mt, reads=BF_.r() + IPM.r() + CFM.r(), writes=PS.r(b + gi * 512, b + gi * 512 + 384))
                for gi in range(2):
                    o = b + gi * 512
                    ew("act", lambda e, gi=gi, o=o: e.activation(out=BM12[:, gi * 256:(gi + 1) * 256], in_=PS[:, o:o + 256], func=AF.Copy),
                       PS.r(o, o + 256), BM12.r(gi * 256, (gi + 1) * 256))
                psm = PS[:, b:b + 1024].rearrange("p (g n) -> p g n", g=2)[:, :, 256:384]
                cm3 = CM[:, :].rearrange("p (o n) -> p o n", o=1).to_broadcast([128, 2, 128])
                ew("dve", lambda e: e.tensor_tensor(out=TMPB[:, 0:256].rearrange("p (g n) -> p g n", g=2), in0=psm, in1=cm3, op=ALU.mult),
                   PS.r(b, b + 1024) + CM.r(), TMPB.r(0, 256))
                for gi in range(2):
                    ew("dve", lambda e, gi=gi: e.scalar_tensor_tensor(out=M0[:, gi * 128:(gi + 1) * 128], in0=JD[:, :],
                                                                      scalar=DVEC[:, g0 + gi:g0 + gi + 1], in1=TMPB[:, gi * 128:(gi + 1) * 128],
                                                                      op0=ALU.mult, op1=ALU.add),
                       JD.r() + DVEC.r() + TMPB.r(0, 256), M0.r(gi * 128, (gi + 1) * 128))

            j3 = JTAB[:, :].rearrange("p (o j) -> p o j", o=1).to_broadcast([128, 2, 256])

            def stage_P(bt):
                g0 = bt * 2
                COSp, SINp, TIp = CSET[bt % 2]
                ti3 = TIp[:, :].rearrange("p (g j) -> p g j", g=2)
                x83 = g32(X8)[:, g0:g0 + 2].rearrange("p (g o) -> p g o", o=1).to_broadcast([128, 2, 256])
                ew("dve", lambda e: e.tensor_tensor(out=ti3, in0=x83, in1=j3, op=ALU.mult), X8.r() + JTAB.r(), TIp.r())
                for gi in range(2):
                    ew("dve", lambda e, gi=gi: e.scalar_tensor_tensor(out=SINp[:, gi * 256:(gi + 1) * 256], in0=JTAB[:, :],
                                                                      scalar=X8[:, g0 + gi:g0 + gi + 1], in1=TIp[:, gi * 256:(gi + 1) * 256],
                                                                      op0=ALU.mult, op1=ALU.subtract),
                       JTAB.r() + X8.r() + TIp.r(), SINp.r(gi * 256, (gi + 1) * 256))
                ew("act", lambda e: e.activation(out=COSp[:, :], in_=SINp[:, :], func=AF.Sin, scale=TWO_PI * 0.5 * 0.99999), SINp.r() + TIp.r(), COSp.r())
                ew("act", lambda e: e.activation(out=SINp[:, :], in_=SINp[:, :], func=AF.Sin, scale=TWO_PI * 0.99999), SINp.r() + COSp.r(), SINp.r())
                ew("act", lambda e: e.activation(out=COSp[:, :], in_=COSp[:, :], func=AF.Square), COSp.r(), COSp.r())
                ew("act", lambda e: e.activation(out=COSp[:, :], in_=COSp[:, :], func=AF.Identity, scale=-2.0, bias=ONEB[:, 0:1]), COSp.r() + ONEB.r(), COSp.r())

            e13 = E1[:, :].rearrange("p (g j) -> p g j", g=2)
            e23 = E2[:, :].rearrange("p (g j) -> p g j", g=2)
            z3 = ZZ[:, :].rearrange("p (g j) -> p g j", g=2)

            def stage_S(bt):
                g0 = bt * 2
                BF_, CFM, CF1, CF2, BM12, M0 = TBL[bt % 2]
                COS, SIN, _ti = CSET[bt % 2]
                c3v = COS[:, :].rearrange("p (g j) -> p g j", g=2)
                s3v = SIN[:, :].rearrange("p (g j) -> p g j", g=2)
                b2 = bank(2)

                def ms(e, b2=b2):
                    ins = None
                    for gi in range(2):
                        for w in range(2):
                            o = b2 + w * 512 + gi * 256
                            ins = e.matmul(PS[:, o:o + 256], lhsT=BM12[:, gi * 256 + w * 128:gi * 256 + (w + 1) * 128],
                                           rhs=uv[:, g0 + gi, :], start=True, stop=True)
                    return ins
                S.op("pe", ms, reads=BM12.r() + U.r(g0 * 256, (g0 + 2) * 256), writes=PS.r(b2, b2 + 1024))
                ew("dve", lambda e, b2=b2: e.tensor_tensor(out=VV[:, :], in0=PS[:, b2:b2 + 512], in1=COS[:, :], op=ALU.mult),
                   PS.r(b2, b2 + 512) + COS.r(), VV.r())
                ew("dve", lambda e, b2=b2: e.tensor_tensor(out=ZZ[:, :], in0=PS[:, b2 + 512:b2 + 1024], in1=SIN[:, :], op=ALU.mult),
                   PS.r(b2 + 512, b2 + 1024) + SIN.r(), ZZ.r())
                ew("dve", lambda e: e.tensor_tensor(out=VV[:, :], in0=VV[:, :], in1=ZZ[:, :], op=ALU.add), VV.r() + ZZ.r(), VV.r())
                for gi in range(2):
                    ew("dve", lambda e, gi=gi: e.tensor_tensor_scan(
                        out=ZZ[:, gi * 256:(gi + 1) * 256], data0=R8[:, g0 + gi:g0 + gi + 1].to_broadcast([128, 256]),
                        data1=VV[:, gi * 256:(gi + 1) * 256], initial=0.0, op0=ALU.mult, op1=ALU.add),
                       R8.r() + VV.r(gi * 256, (gi + 1) * 256), ZZ.r(gi * 256, (gi + 1) * 256))
                ew("pool", lambda e: e.memset(e13[:, :, 0:1], 0.0), (), E1.r())
                ew("pool", lambda e: e.memset(e23[:, :, 0:1], 0.0), (), E2.r())
                if bt + 2 < 16:
                    stage_T0(bt + 2)
                ew("dve", lambda e: e.tensor_tensor(out=e13[:, :, 1:257], in0=z3, in1=c3v, op=ALU.mult), ZZ.r() + COS.r(), E1.r())
                if bt + 1 < 16 and bt >= 1:
                    stage_P(bt + 1)
                def part_b():
                    ew("pool", lambda e: e.tensor_tensor(out=e23[:, :, 1:257], in0=z3, in1=s3v, op=ALU.mult), ZZ.r() + SIN.r(), E2.r())
                    pair = bt % 4
                    for h in range(2):
                        b3 = bank()

                        def mo(e, h=h, b3=b3):
                            ins = None
                            for gi in range(2):
                                o = b3 + gi * 128
                                e.matmul(PS[:, o:o + 128], lhsT=uv[:, g0 + gi, h * 128:(h + 1) * 128], rhs=M0[:, gi * 128:(gi + 1) * 128],
                                         start=True, stop=False)
                                e.matmul(PS[:, o:o + 128], lhsT=e13[:, gi, h * 128:(h + 1) * 128], rhs=CF1[:, gi * 128:(gi + 1) * 128],
                                         start=False, stop=False)
                                ins = e.matmul(PS[:, o:o + 128], lhsT=e23[:, gi, h * 128:(h + 1) * 128], rhs=CF2[:, gi * 128:(gi + 1) * 128],
                                               start=False, stop=True)
                            return ins
                        S.op("pe", mo, reads=U.r(g0 * 256, (g0 + 2) * 256) + M0.r() + E1.r() + E2.r() + CF1.r() + CF2.r(), writes=PS.r(b3, b3 + 256))
                        yv = YTM[h][:, :].rearrange("p (t g c) -> p g t c", t=8, g=8)
                        ew("act", lambda e, h=h, b3=b3, yv=yv, pair=pair: e.activation(
                            out=yv[:, pair * 2:pair * 2 + 2, :, :],
                            in_=PS[:, b3:b3 + 256].rearrange("p (g t c) -> p g t c", g=2, t=8), func=AF.Gelu_apprx_tanh),
                           PS.r(b3, b3 + 256), YTM[h].r())
                    if pair == 3:
                        i = bt // 4
                        for h in range(2):
                            for t2 in range(2):
                                b4 = bank()
                                psb16 = PS[:, b4:b4 + 512].bitcast(BF16)

                                def ty(e, h=h, t2=t2, psb16=psb16):
                                    ins = None
                                    for tq in range(4):
                                        t = t2 * 4 + tq
                                        ins = e.transpose(psb16[:, tq * 128:(tq + 1) * 128], YTM[h][:, t * 128:(t + 1) * 128], IDB[:, :])
                                    return ins
                                S.op("pe", ty, reads=YTM[h].r() + IDB.r(), writes=PS.r(b4, b4 + 512))
                                ew("act", lambda e, h=h, t2=t2, psb16=psb16, i=i: e.activation(
                                    out=ytv[:, i, h * 128:(h + 1) * 128, t2 * 4:(t2 + 1) * 4].rearrange("p j t -> p t j"),
                                    in_=psb16[:, 0:512].rearrange("p (t j) -> p t j", t=4), func=AF.Copy),
                                   PS.r(b4, b4 + 512), YT.r(i * L, (i + 1) * L))
                return part_b

            stage_T(0)
            stage_T(1)
            stage_P(0)
            stage_P(1)
            for bt in range(16):
                pb = stage_S(bt)
                if bt + 2 < 16:
                    stage_T(bt + 2, first_done=True, mid_hook=pb)
                else:
                    pb()

            if l + 1 < DEPTH:
                prologue_early(l + 1)
            stop(4)
            for h in range(2):
                hp = gp + "h%d" % h
                if h == 0:
                    norm_half(h)
                htv = HT[:, :].rearrange("p (k n) -> p k n", k=8)
                ytk = YT[:, :].rearrange("p (i n) -> p i n", i=4)
                wzav = WA[:, :].rearrange("p (k n) -> p k n", k=8)
                wgluv = WGLU[:, :].rearrange("p (k n) -> p k n", k=4)
                wubv = WUB[:, :].rearrange("p (k n) -> p k n", k=8)
                wzbv = WZB[:, :].rearrange("p (k n) -> p k n", k=8)
                ybv = YB[:, :].rearrange("p (i n) -> p i n", i=4)
                mgv = MG[:, :].rearrange("p (i n) -> p i n", i=8)

                def proj(wv, i, c, nk, src_v, src_reads, wreads):
                    b = bank()

                    def mm(e, b=b):
                        ins = None
                        for k in range(nk):
                            ins = e.matmul(PS[:, b:b + 512], lhsT=wv[:, k, i * 128:(i + 1) * 128], rhs=src_v[:, k, c * 512:(c + 1) * 512],
                                           start=(k == 0), stop=(k == nk - 1))
                        return ins
                    S.op("pe", mm, reads=src_reads + wreads, writes=PS.r(b, b + 512))
                    return b

                for c in range(2):
                    tok0 = h * 1024 + c * 512
                    for w, win in enumerate((2, 4, 8, 16)):
                        b = proj(wubv, w, c, 8, htv, HT.r(), WUB.r())
                        ew("act", lambda e, b=b, w=w: e.activation(out=UBX[w][:, 16:528], in_=PS[:, b:b + 512], func=AF.Identity, bias=BPP[:, 4 + w:5 + w]),
                           PS.r(b, b + 512) + BPP.r(), UBX[w].r())
                        if h == 0 and c == 0:
                            ew("dve", lambda e, w=w: e.memset(UBX[w][:, 0:16], 0.0), (), UBX[w].r())
                        else:
                            ew("dve", lambda e, w=w: e.tensor_copy(out=UBX[w][:, 0:16], in_=HALO[:, w * 16:(w + 1) * 16]), HALO.r(), UBX[w].r())
                        ew("dve", lambda e, w=w: e.tensor_copy(out=HALO[:, w * 16:(w + 1) * 16], in_=UBX[w][:, 512:528]), UBX[w].r(), HALO.r())
                        srcT = UBX[w]
                        m = 1
                        dsts = [CF[1], CF[2], CF[1], CF[2]]
                        di = 0
                        while m < win:
                            dstT = dsts[di]
                            di += 1
                            ew("dve", lambda e, srcT=srcT, dstT=dstT, m=m: e.tensor_tensor(out=dstT[:, 16:528], in0=srcT[:, 16:528],
                                                                                          in1=srcT[:, 16 - m:528 - m], op=ALU.add),
                               srcT.r(), dstT.r())
                            if 2 * m < win:
                                ew("dve", lambda e, srcT=srcT, dstT=dstT, m=m: e.tensor_tensor(out=dstT[:, m:16], in0=srcT[:, m:16],
                                                                                              in1=srcT[:, 0:16 - m], op=ALU.add),
                                   srcT.r(), dstT.r())
                                ew("pool", lambda e, dstT=dstT, m=m: e.memset(dstT[:, 0:m], 0.0), dstT.r(), dstT.r())
                            srcT = dstT
                            m *= 2
                        ew("dve", lambda e, srcT=srcT, win=win, w=w: e.scalar_tensor_tensor(out=PB[w][:, :], in0=srcT[:, 16:528], scalar=1.0 / win,
                                                                                            in1=UBX[w][:, 16:528], op0=ALU.mult, op1=ALU.subtract),
                           srcT.r() + UBX[w].r(), PB[w].r())
                        if h == 0 and c == 0:
                            ew("dve", lambda e, srcT=srcT, w=w: e.tensor_tensor(out=CF[3][:, 16:32], in0=srcT[:, 16:32], in1=INVC[:, w * 16:(w + 1) * 16], op=ALU.mult),
                               srcT.r() + INVC.r() + CF[3].r(), CF[3].r())
                            ew("dve", lambda e, w=w: e.tensor_tensor(out=PB[w][:, 0:16], in0=CF[3][:, 16:32], in1=UBX[w][:, 16:32], op=ALU.subtract),
                               CF[3].r() + UBX[w].r() + PB[w].r(), PB[w].r())
                    for i in range(4):
                        b = proj(wgluv, i, 0, 4, ytk[:, :, tok0:tok0 + 512], YT.r(), WGLU.r())
                        ew("act", lambda e, b=b, i=i: e.activation(out=CH[i][:, :], in_=PS[:, b:b + 512], func=AF.Sigmoid, bias=BGLU[:, i:i + 1]),
                           PS.r(b, b + 512) + BGLU.r(), CH[i].r())
                    for i in range(4):
                        b = proj(wzav, i, c, 8, htv, HT.r(), WA.r())
                        sb = SLB[i % 2]
                        ew("act", lambda e, b=b, i=i, sb=sb: e.activation(out=sb[:, :], in_=PS[:, b:b + 512], func=AF.Silu, bias=BPP[:, i:i + 1]),
                           PS.r(b, b + 512) + BPP.r(), sb.r())
                        ysl = YT[:, i * L + tok0:i * L + tok0 + 512]
                        yr = YT.r(i * L + tok0, i * L + tok0 + 512)
                        ew("pool", lambda e, i=i, ysl=ysl, sb=sb: e.tensor_tensor(out=CH[i][:, :], in0=CH[i][:, :], in1=sb[:, :], op=ALU.mult),
                           CH[i].r() + sb.r(), CH[i].r())
                        ew("dve", lambda e, i=i, ysl=ysl: e.tensor_tensor(out=ysl, in0=ysl, in1=CH[i][:, :], op=ALU.mult),
                           CH[i].r() + yr, yr)
                    for w in range(4):
                        b6 = proj(wzbv, w, c, 8, htv, HT.r(), WZB.r())
                        b5 = bank()
                        S.op("pe", lambda e, b5=b5, w=w: e.matmul(PS[:, b5:b5 + 512], lhsT=WPOOL[:, w * 128:(w + 1) * 128], rhs=PB[w][:, :], start=True, stop=True),
                             reads=WPOOL.r() + PB[w].r(), writes=PS.r(b5, b5 + 512))
                        sb = SLB[w % 2]
                        ew("act", lambda e, b6=b6, w=w, sb=sb: e.activation(out=sb[:, :], in_=PS[:, b6:b6 + 512], func=AF.Silu, bias=BPP[:, 8 + w:9 + w]),
                           PS.r(b6, b6 + 512) + BPP.r(), sb.r())
                        ew("dve", lambda e, b5=b5, w=w, c=c: e.scalar_tensor_tensor(out=ybv[:, w, c * 512:(c + 1) * 512], in0=PS[:, b5:b5 + 512],
                                                                                    scalar=PSC[:, w:w + 1], in1=sb[:, :], op0=ALU.mult, op1=ALU.mult),
                           PS.r(b5, b5 + 512) + PSC.r() + sb.r(), YB.r(w * 1024 + c * 512, w * 1024 + (c + 1) * 512))
                stop(5)
                for o in range(8):
                    u = h * 8 + o
                    wg = WG[u % 2]
                    wgv = wg[:, :].rearrange("p (k n) -> p k n", n=128)
                    if u >= 1 and u + 1 < 16:
                        load_wg(l, (u + 1) % 8, WG[(u + 1) % 2], gp + "wgu%d" % (u + 1))
                    for c in range(2):
                        tok0 = h * 1024 + c * 512
                        outs = []
                        for (koff, nk, srcv, srd) in ((0, 8, htv, HT.r()), (8, 8, htv, HT.r()),
                                                      (16, 4, ytk[:, :, h * 1024:(h + 1) * 1024], YT.r()), (20, 4, ybv, YB.r())):
                            b = bank()

                            def mm(e, b=b, koff=koff, nk=nk, srcv=srcv, c=c):
                                ins = None
                                for k in range(nk):
                                    ins = e.matmul(PS[:, b:b + 512], lhsT=wgv[:, koff + k, :], rhs=srcv[:, k, c * 512:(c + 1) * 512],
                                                   start=(k == 0), stop=(k == nk - 1))
                                return ins
                            S.op("pe", mm, reads=srd + wg.r(), writes=PS.r(b, b + 512))
                            outs.append(b)
                        par = (o * 2 + c) % 2
                        sga, sgb = (CH[2], CH[0])[par], (CH[3], CH[1])[par]
                        pa, pb = (CF[1], CF[0])[par], (CF[2], CF[3])[par]
                        ew("act", lambda e, b=outs[0], o=o: e.activation(out=sga[:, :], in_=PS[:, b:b + 512], func=AF.Sigmoid, bias=BPP[:, 12 + o:13 + o]),
                           PS.r(outs[0], outs[0] + 512) + BPP.r(), sga.r())
                        ew("act", lambda e, b=outs[1], o=o: e.activation(out=sgb[:, :], in_=PS[:, b:b + 512], func=AF.Sigmoid, bias=BPP[:, 20 + o:21 + o]),
                           PS.r(outs[1], outs[1] + 512) + BPP.r(), sgb.r())
                        ew("dve", lambda e, b=outs[2]: e.tensor_tensor(out=pa[:, 0:512], in0=PS[:, b:b + 512], in1=sga[:, :], op=ALU.mult),
                           PS.r(outs[2], outs[2] + 512) + sga.r(), pa.r())
                        ew("dve", lambda e, b=outs[3]: e.tensor_tensor(out=pb[:, 0:512], in0=PS[:, b:b + 512], in1=sgb[:, :], op=ALU.mult),
                           PS.r(outs[3], outs[3] + 512) + sgb.r(), pb.r())
                        ew("pool", lambda e, o=o, c=c: e.tensor_tensor(out=mgv[:, o, c * 512:(c + 1) * 512], in0=pa[:, 0:512], in1=pb[:, 0:512], op=ALU.add),
                           pa.r() + pb.r(), MG.r(o * 1024 + c * 512, o * 1024 + (c + 1) * 512))
                stop(6)
                woutv = WOUT[:, :].rearrange("p (k n) -> p k n", k=8)
                npipe = NormPipe(1, [XN, XN2]) if h == 0 else None
                if npipe is not None:
                    npipe.front(0)
                if h == 1 and l + 1 < DEPTH:
                    prologue_dma(l + 1)
                for t8 in range(8):
                    tt = h * 8 + t8
                    if npipe is not None and t8 + 1 < 8:
                        npipe.front(t8 + 1)
                    if h == 1 and l + 1 < DEPTH and t8 == 3:
                        prologue_compute(l + 1)
                    for nh in range(2):
                        b = bank()

                        def mm(e, b=b, t8=t8, nh=nh):
                            ins = None
                            for k in range(8):
                                ins = e.matmul(PS[:, b:b + 512], lhsT=mgv[:, k, t8 * 128:(t8 + 1) * 128], rhs=woutv[:, k, nh * 512:(nh + 1) * 512],
                                               start=(k == 0), stop=(k == 7))
                            return ins
                        S.op("pe", mm, reads=MG.r() + WOUT.r(), writes=PS.r(b, b + 512))
                        xs = X[:, tt * D + nh * 512:tt * D + (nh + 1) * 512]
                        xr = X.r(tt * D + nh * 512, tt * D + (nh + 1) * 512)
                        rt = RT[(t8 * 2 + nh) % 2]
                        ew("act", lambda e, b=b, rt=rt: e.activation(out=rt[:, :], in_=PS[:, b:b + 512], func=AF.Copy),
                           PS.r(b, b + 512), rt.r())
                        ew("pool", lambda e, xs=xs, rt=rt: e.tensor_tensor(out=xs, in0=xs, in1=rt[:, :], op=ALU.add),
                           rt.r() + xr, xr)
                    if npipe is not None:
                        npipe.back(t8)
                    if l == DEPTH - 1 and h == 1:
                        final_tile(t8)
                        final_tile(8 + t8)
                if npipe is not None:
                    npipe.flush()

            stop(7)
      except StopBuild:
        break

    if os.environ.get("KDUMP"):
        want = os.environ["KDUMP"].split(",")
        loc = dict(ARE=ARE, AIM=AIM, P1=P1, P2=P2, P3=P3, GX1=GX1, GX2=GX2, R8=R8, X8=X8, CTR=CTR, CTI=CTI, U=U, YT=YT,
                   HT=HT, A_TM=A_TM, BF_=BF_, CFM=CFM, CF1=CF1, CF2=CF2, BM12=BM12, M0=M0, COS=COS, SIN=SIN, VV=VV, ZZ=ZZ,
                   E1=E1, E2=E2, YB=YB, MG=MG, X=X, JD=JD, CM=CM, IPM=IPM, DVEC=DVEC, BY1=BY1, LAMR=LAMR, BPP=BPP)
        for nm in want:
            t = loc[nm]
            dtp = BF16 if t.esz == 2 else F32
            dd = nc.dram_tensor("dbg_" + nm, [128, t.ncols], dtp, kind="ExternalOutput").ap()
            dma("sp", dd[:, :], t[:, :], "dbg" + nm, reads=t.r())
            S.final_wait("sp", "dbg" + nm + "_sp")

    for tt in range(16):
        if ("outst%d_sp" % tt) not in S.gtot:
            final_tile(tt)
    for tt in range(16):
        S.final_wait("sp", "outst%d_sp" % tt)
    S.emit(nc, es_global)
    return nc


es_global = None


def kernel(**inputs):
    global es_global
    ins = {k: np.ascontiguousarray(np.asarray(v, dtype=np.float32)) for k, v in inputs.items()}
    with ExitStack() as es:
        es_global = es
        nc = build_nc()
    shared = {
        "norm_g": ins["norm_g"], "w_in": ins["w_in"], "b_in": ins["b_in"],
        "ssm_log_dt": ins["ssm_log_dt"], "ssm_lam_re": ins["ssm_lam_re"], "ssm_lam_im": ins["ssm_lam_im"],
        "ssm_b_re": ins["ssm_b_re"], "ssm_b_im": ins["ssm_b_im"],
        "ssm_c_re": ins["ssm_c_re"].reshape(DEPTH, 512, 64), "ssm_c_im": ins["ssm_c_im"].reshape(DEPTH, 512, 64),
        "ssm_d": ins["ssm_d"].reshape(DEPTH, 32, 16), "ssm_w_glu": ins["ssm_w_glu"], "ssm_b_glu": ins["ssm_b_glu"],
        "pool_w": ins["pool_w"], "pool_scale": ins["pool_scale"],
        "w_branch_a": ins["w_branch_a"], "w_branch_b": ins["w_branch_b"], "w_out": ins["w_out"],
        "final_norm_g": ins["final_norm_g"],
    }
    in_maps = []
    for c in range(NCORES):
        m = dict(shared)
        m["x"] = np.ascontiguousarray(ins["x"][c])
        in_maps.append(m)
    res = run_bass_kernel_spmd(nc, in_maps, core_ids=list(range(NCORES)))
    out = np.stack([np.asarray(r["out"], dtype=np.float32) for r in res.results], axis=0)
    return out
```

```python
import os
import numpy as np
from contextlib import ExitStack
import concourse.bass as bass
import concourse.mybir as mybir
from concourse.bass_utils import run_bass_kernel_spmd

F32 = mybir.dt.float32
BF16 = mybir.dt.bfloat16
I32 = mybir.dt.int32
AF = mybir.ActivationFunctionType
ALU = mybir.AluOpType

NCORES = 8
D = 1024
L = 2048
DEPTH = 2
EPS = 1e-6
BLK = 64
SB_BASE = 16512
SB_END = 229376
TWO_PI = float(2 * np.pi)


class TT:
    def __init__(self, h, space, addr, esz, ncols):
        self.h, self.space, self.addr, self.esz, self.ncols = h, space, addr, esz, ncols

    def r(self, lo=0, hi=None):
        hi = self.ncols if hi is None else hi
        a0 = self.addr + lo * self.esz
        a1 = self.addr + hi * self.esz - 1
        if self.space == "ps":
            return [("ps", b) for b in range(a0 // 2048, a1 // 2048 + 1)]
        return [(self.space, b) for b in range(a0 // BLK, a1 // BLK + 1)]

    def __getitem__(self, k):
        return self.h[k]


class _Dummy:
    def then_inc(self, *a, **k):
        return self


class Rec:
    def __init__(self):
        self.items = []

    def __getattr__(self, nm):
        def f(*a, **kw):
            self.items.append((nm, a, kw))
            return _Dummy()
        return f


class Sched:
    ENG = ["pe", "act", "dve", "pool", "sp"]

    def __init__(self):
        self.ops = {e: [] for e in self.ENG}
        self.cnt = {e: 0 for e in self.ENG}
        self.waited = {}
        self.lw = {}
        self.rd = {}
        self.gtot = {}

    def op(self, eng, fn, reads=(), writes=(), group=None):
        tags = set()
        for b in reads:
            t = self.lw.get(b)
            if t:
                tags.add(t)
        for b in writes:
            t = self.lw.get(b)
            if t:
                tags.add(t)
            for t in self.rd.get(b, ()):
                tags.add(t)
        waits = []
        for (k, v) in sorted(tags, key=str):
            if isinstance(k, tuple):
                if group is not None and k[1] == group:
                    continue
                if self.waited.get((eng, k)):
                    continue
                self.waited[(eng, k)] = 1
                waits.append((k, None))
            else:
                if k == eng and eng == "pe" and group is None:
                    continue
                if self.waited.get((eng, k), 0) >= v:
                    continue
                self.waited[(eng, k)] = v
                waits.append((k, v))
        if group is None:
            self.cnt[eng] += 1
            tag = (eng, self.cnt[eng])
        else:
            self.gtot[group] = self.gtot.get(group, 0) + 16
            tag = (("g", group), 0)
        if callable(fn):
            rec = Rec()
            fn(rec)
            fn = rec.items
        self.ops[eng].append((waits, fn, group, tag[1] if group is None else None))
        for b in writes:
            self.lw[b] = tag
            self.rd[b] = []
        for b in reads:
            self.rd.setdefault(b, []).append(tag)

    def final_wait(self, eng, group):
        self.ops[eng].append(([(("g", group), None)], None, None, None))

    def emit(self, nc, es):
        sems = {e: es.enter_context(nc.semaphore("s_" + e)) for e in self.ENG}
        for g in self.gtot:
            sems[("g", g)] = es.enter_context(nc.semaphore("g_" + g))
        block = es.enter_context(nc.Block())
        import bisect
        ref = {e: set() for e in self.ENG}
        for e in self.ENG:
            for waits, fn, group, ordn in self.ops[e]:
                for (k, v) in waits:
                    if not isinstance(k, tuple):
                        ref[k].add(v)
        refl = {e: sorted(ref[e]) for e in self.ENG}

        def cnt_of(e, v):
            return bisect.bisect_right(refl[e], v)

        def run(eng_name):
            def body(eng):
                for _ in range(int(os.environ.get("KNOP_" + eng_name, "0"))):
                    eng.nop()
                for waits, fn, group, ordn in self.ops[eng_name]:
                    for (k, v) in waits:
                        if isinstance(k, tuple):
                            eng.wait_ge(sems[k], self.gtot[k[1]])
                        else:
                            eng.wait_ge(sems[k], cnt_of(k, v))
                    if fn is None:
                        continue
                    ins = None
                    for (nm, a, kw) in fn:
                        ins = getattr(eng, nm)(*a, **kw)
                    if group is None:
                        if ordn in ref[eng_name]:
                            ins.then_inc(sems[eng_name], 1)
                    else:
                        ins.then_inc(sems[("g", group)], 16)
            return body

        block.tensor(run("pe"))
        block.scalar(run("act"))
        block.vector(run("dve"))
        block.gpsimd(run("pool"))
        block.sync(run("sp"))


def build_nc():
    nc = bass.Bass("TRN2", target_bir_lowering=False)
    S = Sched()

    def din(name, shape):
        return nc.dram_tensor(name, list(shape), F32, kind="ExternalInput").ap()

    x_d = din("x", [L, D])
    norm_g_d = din("norm_g", [DEPTH, D])
    w_in_d = din("w_in", [DEPTH, D, 4096])
    b_in_d = din("b_in", [DEPTH, 4096])
    ldt_d = din("ssm_log_dt", [DEPTH, 32])
    lamr_d = din("ssm_lam_re", [DEPTH, 32, 64])
    lami_d = din("ssm_lam_im", [DEPTH, 32, 64])
    bre_d = din("ssm_b_re", [DEPTH, 32, 64, 16])
    bim_d = din("ssm_b_im", [DEPTH, 32, 64, 16])
    cre_d = din("ssm_c_re", [DEPTH, 512, 64])
    cim_d = din("ssm_c_im", [DEPTH, 512, 64])
    ssmd_d = din("ssm_d", [DEPTH, 32, 16])
    wglu_d = din("ssm_w_glu", [DEPTH, 512, 512])
    bglu_d = din("ssm_b_glu", [DEPTH, 512])
    poolw_d = din("pool_w", [DEPTH, 4, 128, 128])
    pscale_d = din("pool_scale", [DEPTH, 512])
    wa_d = din("w_branch_a", [DEPTH, 512, D])
    wb_d = din("w_branch_b", [DEPTH, 512, D])
    wout_d = din("w_out", [DEPTH, D, D])
    gfin_d = din("final_norm_g", [D])
    out_d = nc.dram_tensor("out", [L, D], F32, kind="ExternalOutput").ap()

    cur = [SB_BASE]
    names = [0]

    def alloc(ncols, dt, at=None):
        esz = 2 if dt == BF16 else 4
        if at is None:
            a = cur[0]
            cur[0] += (ncols * esz + 63) // 64 * 64
            assert cur[0] <= SB_END, "SBUF overflow %d" % cur[0]
        else:
            a = at
        names[0] += 1
        h = nc.alloc_sbuf_tensor_at("t%d" % names[0], [128, ncols], dt, offset=a)
        return TT(h, "sb", a, esz, ncols)

    X = alloc(16 * D, F32)
    YT = alloc(4 * L, BF16)
    WA = alloc(8 * 512, BF16)
    WGLU = alloc(4 * 512, BF16)
    WUB = alloc(8 * 512, BF16)
    WZB = alloc(8 * 512, BF16)
    WPOOL = alloc(4 * 128, BF16)
    WG = [alloc(8 * 128 * 2 + 4 * 128 * 2, BF16) for _ in range(2)]
    WOUT = alloc(8 * D, BF16)
    IDB = alloc(128, BF16)
    IDF = alloc(128, F32)
    IPM = alloc(256, BF16)
    CM = alloc(128, F32)
    JTAB = alloc(256, F32)
    ETAB = alloc(16, F32)
    ONES = alloc(128, BF16)
    INVC = alloc(64, F32)
    GTAB = alloc(D, F32)
    BPP = alloc(32, F32)
    BUA = alloc(512, BF16)
    BGLU = alloc(4, F32)
    PSC = alloc(4, F32)
    LDT = alloc(32, F32)
    LAMR = alloc(32, F32)
    LAMI = alloc(32, F32)
    BY1 = alloc(512, F32)
    BY2 = alloc(512, F32)
    DVEC = alloc(32, F32)
    HALO = alloc(64, F32)
    SMALL = alloc(64, F32)
    JD = alloc(128, F32)
    EPSB = alloc(16, F32)
    ONEB = alloc(16, F32)
    AR0 = cur[0]
    ARENA = SB_END - AR0
    assert ARENA >= 56 * 1024, ARENA

    def ar(off, ncols, dt):
        esz = 2 if dt == BF16 else 4
        assert off + ncols * esz <= ARENA, (off, ncols)
        return alloc(ncols, dt, at=AR0 + off)

    K = 1024
    HT = ar(0, 8 * 1024, BF16)
    A_TM = ar(16 * K, 4096, BF16)
    U = ar(24 * K, 32 * 256, BF16)
    CTR = ar(40 * K, 512, F32)
    CTI = ar(42 * K, 512, F32)
    P1 = ar(44 * K, 512, F32)
    P2 = ar(46 * K, 512, F32)
    P3 = ar(48 * K, 512, F32)
    GX1 = ar(50 * K, 256, F32)
    GX2 = ar(51 * K, 256, F32)
    R8 = ar(52 * K, 32, F32)
    X8 = ar(52 * K + 128, 32, F32)
    DTT = ar(52 * K + 256, 32, F32)
    LRD = ar(52 * K + 384, 32, F32)
    ANGV = ar(52 * K + 512, 32, F32)
    CNR = ar(0, 512, F32)
    CNI = ar(2 * K, 512, F32)
    ARE = ar(4 * K, 512, F32)
    AIM = ar(6 * K, 512, F32)
    TG = [ar(8 * K + i * 2 * K, 512, F32) for i in range(4)]
    TGI = alloc(512, I32, at=TG[3].addr)
    STG = ar(12 * K, 768, F32)
    COS = ar(0, 512, F32)
    SIN = ar(2 * K, 512, F32)
    VV = ar(4 * K, 512, F32)
    ZZ = ar(6 * K, 512, F32)
    TMPA = ar(10 * K, 384, F32)
    TMPB = ar(11 * K + 512, 256, F32)
    SINB = ar(12 * K + 512, 512, F32)
    COSB = ar(55936, 512, F32)
    CSET = [(COS, SIN, alloc(512, I32, at=COS.addr)), (COSB, SINB, alloc(512, I32, at=COSB.addr))]
    E1 = ar(14 * K + 512, 2 * 264, BF16)
    E2 = ar(15936, 2 * 264, BF16)
    TBL = []
    for i in range(2):
        o = 17 * K + i * 3584
        TBL.append((ar(o, 256, BF16), ar(o + 512, 256, BF16), ar(o + 1024, 256, BF16), ar(o + 1536, 256, BF16),
                    ar(o + 2048, 512, BF16), ar(o + 3072, 256, BF16)))
    BF_, CFM, CF1, CF2, BM12, M0 = TBL[0]
    YTM = [ar(53888, 1024, BF16), ar(8 * K, 1024, BF16)]
    YB = ar(16 * K, 4 * 1024, BF16)
    RT = [ar(16 * K + i * 2 * K, 512, F32) for i in range(2)]
    MG = ar(24 * K, 8 * 1024, BF16)
    PB = [ar(24 * K + i * K, 512, BF16) for i in range(4)]
    CH = [ar(40 * K + i * K, 512, BF16) for i in range(5)]
    SLB = [CH[4], ar(28 * K, 512, BF16)]
    UBX = [ar(29 * K + i * 2176, 528, F32) for i in range(4)]
    CF = [ar(45 * K + i * 2176, 512 + 16, F32) for i in range(4)]
    XN = ar(45 * K + 4 * 2176, 1024, BF16)
    XN2 = ar(45 * K, 1024, BF16)
    JUNK = alloc(512, mybir.dt.int8, at=AR0 + 45 * K + 4 * 2176 + 2048)
    JUNK.esz = 1
    SMALL2 = ar(45 * K + 4 * 2176 + 2048 + 512, 128, F32)
    print("ARENA", ARENA, "XN end", 45 * K + 4 * 2176 + 2048)
    psh = es_global.enter_context(nc.psum_tensor("PS", [128, 4096], F32))
    PS = TT(psh, "ps", 0, 4, 4096)
    psb = [0]

    def bank(n=1):
        b = psb[0]
        if b + n > 8:
            b = 0
        psb[0] = (b + n) % 8
        return b * 512

    def cp(eng_name):
        return eng_name

    def dma(q, out_ap, in_ap, group, reads=(), writes=(), slow=False):
        def fn(e):
            if slow:
                return e.dma_start(out=out_ap, in_=in_ap, allow_slow_non_contiguous=True)
            return e.dma_start(out=out_ap, in_=in_ap)
        S.op(q, fn, reads, writes, group=group + "_" + q)

    def v3(ap, **kw):
        return ap

    def _c(e):
        return e.iota(IDF[:, :], [[1, 128]], base=0, channel_multiplier=-1, allow_small_or_imprecise_dtypes=True)
    S.op("pool", _c, writes=IDF.r())
    S.op("pool", lambda e: e.iota(CM[:, :], [[16, 8], [0, 16]], base=0, channel_multiplier=1,
                                  allow_small_or_imprecise_dtypes=True), writes=CM.r())
    S.op("pool", lambda e: e.iota(JTAB[:, :], [[1, 256]], base=0, channel_multiplier=0,
                                  allow_small_or_imprecise_dtypes=True), writes=JTAB.r())
    S.op("pool", lambda e: e.iota(ETAB[:, :], [[1, 16]], base=-7, channel_multiplier=0,
                                  allow_small_or_imprecise_dtypes=True), writes=ETAB.r())
    S.op("dve", lambda e: e.tensor_scalar(out=IPM[:, 0:128], in0=IDF[:, :], scalar1=0.0, scalar2=None, op0=ALU.is_equal),
         reads=IDF.r(), writes=IPM.r(0, 128))
    S.op("dve", lambda e: e.tensor_scalar(out=TG[0][:, 0:128], in0=IDF[:, :], scalar1=-64.0, scalar2=None, op0=ALU.is_equal),
         reads=IDF.r(), writes=TG[0].r(0, 128))
    S.op("dve", lambda e: e.tensor_scalar(out=TG[1][:, 0:128], in0=IDF[:, :], scalar1=64.0, scalar2=None, op0=ALU.is_equal),
         reads=IDF.r(), writes=TG[1].r(0, 128))
    S.op("dve", lambda e: e.tensor_tensor(out=IPM[:, 128:256], in0=TG[0][:, 0:128], in1=TG[1][:, 0:128], op=ALU.subtract),
         reads=TG[0].r(0, 128) + TG[1].r(0, 128), writes=IPM.r(128, 256))
    S.op("dve", lambda e: e.tensor_copy(out=IDB[:, :], in_=IPM[:, 0:128]), reads=IPM.r(0, 128), writes=IDB.r())
    S.op("dve", lambda e: e.tensor_scalar(out=IDF[:, :], in0=IDF[:, :], scalar1=0.0, scalar2=None, op0=ALU.is_equal),
         reads=IDF.r() + IPM.r() + TG[0].r(0, 128) + TG[1].r(0, 128), writes=IDF.r())
    S.op("dve", lambda e: e.tensor_scalar(out=CM[:, :], in0=CM[:, :], scalar1=112.0, scalar2=None, op0=ALU.is_ge),
         reads=CM.r(), writes=CM.r())
    S.op("dve", lambda e: e.memset(ONES[:, :], 1.0), writes=ONES.r())
    S.op("pool", lambda e: e.iota(JD[:, :], [[-16, 8], [1, 16]], base=112, channel_multiplier=-1,
                                  allow_small_or_imprecise_dtypes=True), writes=JD.r())
    S.op("dve", lambda e: e.tensor_scalar(out=JD[:, :], in0=JD[:, :], scalar1=0.0, scalar2=None, op0=ALU.is_equal),
         reads=JD.r(), writes=JD.r())
    for wi, win in enumerate((2, 4, 8, 16)):
        S.op("dve", lambda e, wi=wi, win=win: e.tensor_scalar(out=INVC[:, wi * 16:(wi + 1) * 16], in0=JTAB[:, 0:16],
                                                              scalar1=1.0, scalar2=float(win), op0=ALU.add, op1=ALU.min),
             reads=JTAB.r(0, 16), writes=INVC.r(wi * 16, wi * 16 + 16))
    S.op("dve", lambda e: e.reciprocal(out=INVC[:, :], in_=INVC[:, :]), reads=INVC.r(), writes=INVC.r())

    class NormPipe:
        def __init__(self, h, xbufs):
            self.h = h
            self.pending = None
            self.xbufs = xbufs

        def front(self, t8):
            h = self.h
            tt = h * 8 + t8
            xn = self.xbufs[t8 % len(self.xbufs)]
            xs = X[:, tt * D:(tt + 1) * D]
            xr = X.r(tt * D, (tt + 1) * D)
            c0 = 16 * (tt % 8)
            sc = SMALL2[:, c0:c0 + 1]
            sc2 = SMALL2[:, c0 + 1:c0 + 2]
            scr = SMALL2.r(c0, c0 + 16)
            S.op("act", lambda e: e.activation(out=JUNK[:, :], in_=xs[:, 0:512], func=AF.Square, accum_out=sc), reads=xr, writes=JUNK.r() + scr)
            S.op("act", lambda e: e.activation(out=JUNK[:, :], in_=xs[:, 512:1024], func=AF.Square, accum_out=sc2), reads=xr, writes=JUNK.r() + scr)
            S.op("dve", lambda e: e.tensor_tensor(out=sc, in0=sc, in1=sc2, op=ALU.add), reads=scr, writes=scr)
            S.op("act", lambda e: e.activation(out=sc, in_=sc, func=AF.Sqrt, scale=1.0 / D, bias=EPSB[:, 0:1]), reads=scr + EPSB.r(), writes=scr)
            S.op("dve", lambda e: e.reciprocal(out=sc, in_=sc), reads=scr, writes=scr)
            S.op("dve", lambda e: e.scalar_tensor_tensor(out=xn[:, 0:1024], in0=xs, scalar=sc, in1=GTAB[:, :], op0=ALU.mult, op1=ALU.mult),
                 reads=xr + scr + GTAB.r(), writes=xn.r(0, 1024))

        def back(self, t8):
            htv = HT[:, :].rearrange("p (k n) -> p k n", k=8)
            xn = self.xbufs[t8 % len(self.xbufs)]
            b = bank()
            psb16 = PS[:, b:b + 512].bitcast(BF16)

            def tr(e):
                ins = None
                for k in range(8):
                    ins = e.transpose(psb16[:, k * 128:(k + 1) * 128], xn[:, k * 128:(k + 1) * 128], IDB[:, :])
                return ins
            S.op("pe", tr, reads=xn.r(0, 1024) + IDB.r(), writes=PS.r(b, b + 512))
            if self.pending is not None:
                self.pending()

            def evac():
                S.op("dve", lambda e: e.tensor_copy(out=htv[:, :, t8 * 128:(t8 + 1) * 128], in_=psb16.rearrange("p (k n) -> p k n", k=8)),
                     reads=PS.r(b, b + 512), writes=sum([HT.r(k * 1024 + t8 * 128, k * 1024 + (t8 + 1) * 128) for k in range(8)], []))
            self.pending = evac

        def flush(self):
            if self.pending is not None:
                self.pending()
                self.pending = None

    def norm_half(h):
        np_ = NormPipe(h, [XN])
        for t8 in range(8):
            np_.front(t8)
            np_.back(t8)
        np_.flush()

    S.op("dve", lambda e: e.memset(EPSB[:, :], EPS), writes=EPSB.r())
    S.op("dve", lambda e: e.memset(ONEB[:, :], 1.0), writes=ONEB.r())

    def load_w(dst, dst_lo, src_ap_kpn, ncols, group, ktiles):
        hi = dst_lo + ktiles * ncols
        dma("pool", dst[:, dst_lo:hi].rearrange("p (k n) -> p k n", k=ktiles),
            src_ap_kpn.rearrange("(k p) n -> p k n", p=128), group, writes=dst.r(dst_lo, hi))

    def load_wg(l, o, wg, grp):
        dma("pool", wg[:, 0:1024].rearrange("p (k n) -> p k n", k=8),
            w_in_d[l, :, 2048 + o * 128:2048 + (o + 1) * 128].rearrange("(k p) n -> p k n", p=128), grp, writes=wg.r(0, 1024))
        dma("pool", wg[:, 1024:2048].rearrange("p (k n) -> p k n", k=8),
            w_in_d[l, :, 3072 + o * 128:3072 + (o + 1) * 128].rearrange("(k p) n -> p k n", p=128), grp, writes=wg.r(1024, 2048))
        dma("pool", wg[:, 2048:2560].rearrange("p (k n) -> p k n", k=4),
            wa_d[l, :, o * 128:(o + 1) * 128].rearrange("(k p) n -> p k n", p=128), grp, writes=wg.r(2048, 2560))
        dma("pool", wg[:, 2560:3072].rearrange("p (k n) -> p k n", k=4),
            wb_d[l, :, o * 128:(o + 1) * 128].rearrange("(k p) n -> p k n", p=128), grp, writes=wg.r(2560, 3072))

    KSTOP = int(os.environ.get("KSTOP", "99"))

    class StopBuild(Exception):
        pass

    def stop(n):
        if KSTOP <= n:
            raise StopBuild()

    GF = ar(0, D, F32)
    OUTT = [ar(4 * K + i * 4 * K, D, F32) for i in range(3)]
    fin_state = {"gf": False}

    def final_tile(tt):
        if not fin_state["gf"]:
            dma("sp", GF[:, :], gfin_d.partition_broadcast(128), "gfin", writes=GF.r())
            fin_state["gf"] = True
        xs = X[:, tt * D:(tt + 1) * D]
        xr = X.r(tt * D, (tt + 1) * D)
        c0 = 16 * (tt % 8)
        sc = SMALL2[:, c0:c0 + 1]
        sc2 = SMALL2[:, c0 + 1:c0 + 2]
        scr = SMALL2.r(c0, c0 + 16)
        ot = OUTT[tt % 3]
        S.op("act", lambda e: e.activation(out=JUNK[:, :], in_=xs[:, 0:512], func=AF.Square, accum_out=sc), reads=xr, writes=JUNK.r() + scr)
        S.op("act", lambda e: e.activation(out=JUNK[:, :], in_=xs[:, 512:1024], func=AF.Square, accum_out=sc2), reads=xr, writes=JUNK.r() + scr)
        S.op("dve", lambda e: e.tensor_tensor(out=sc, in0=sc, in1=sc2, op=ALU.add), reads=scr, writes=scr)
        S.op("act", lambda e: e.activation(out=sc, in_=sc, func=AF.Sqrt, scale=1.0 / D, bias=EPSB[:, 0:1]), reads=scr + EPSB.r(), writes=scr)
        S.op("dve", lambda e: e.reciprocal(out=sc, in_=sc), reads=scr, writes=scr)
        S.op("dve", lambda e: e.scalar_tensor_tensor(out=ot[:, :], in0=xs, scalar=sc, in1=GF[:, :], op0=ALU.mult, op1=ALU.mult),
             reads=xr + scr + GF.r(), writes=ot.r())
        dma("sp", out_d[tt * 128:(tt + 1) * 128, :], ot[:, :], "outst%d" % tt, reads=ot.r())

    def ew(eng, fn, reads, writes):
        S.op(eng, fn, reads, writes)

    def g32(t):
        return t[:, 0:32]

    def v16(t):
        return t[:, :].rearrange("p (g e) -> p g e", g=32)

    def prologue_early(l):
        gp = "L%d" % l
        by1v = BY1[:, :].rearrange("p (g c) -> p g c", g=32)
        by2v = BY2[:, :].rearrange("p (g c) -> p g c", g=32)
        dma("sp", by1v[0:64], bre_d[l].rearrange("g p c -> p g c"), gp + "pe", writes=BY1.r())
        dma("sp", by1v[64:128], bim_d[l].rearrange("g p c -> p g c"), gp + "pe", writes=BY1.r())
        dma("sp", by2v[0:64], bim_d[l].rearrange("g p c -> p g c"), gp + "pe", writes=BY2.r())
        dma("sp", by2v[64:128], bre_d[l].rearrange("g p c -> p g c"), gp + "pe", writes=BY2.r())

    def prologue_dma(l):
        gp = "L%d" % l
        dma("sp", GTAB[:, :], norm_g_d[l].partition_broadcast(128), gp + "par", writes=GTAB.r())
        dma("sp", LDT[:, :], ldt_d[l].partition_broadcast(128), gp + "par", writes=LDT.r())
        dma("pool", BUA[0:1, :], b_in_d[l, 0:512].rearrange("(o n) -> o n", o=1), gp + "par", writes=BUA.r())
        S.op("dve", lambda e: e.memset(STG[:, :], 0.0), writes=STG.r())
        for hh in range(2):
            dma("sp", STG[0:32, hh * 64:(hh + 1) * 64], lamr_d[l], gp + "stg", writes=STG.r())
            dma("sp", STG[0:32, 128 + hh * 64:128 + (hh + 1) * 64], lami_d[l], gp + "stg", writes=STG.r())
        for q in range(8):
            dma("sp", STG[0:32, 256 + q * 16:256 + (q + 1) * 16], ssmd_d[l], gp + "stg", writes=STG.r())
        dma("sp", STG[0:28, 384:512], b_in_d[l, 512:4096].rearrange("(t p) -> t p", p=128), gp + "stg", writes=STG.r())
        dma("sp", STG[0:4, 512:640], bglu_d[l].rearrange("(t p) -> t p", p=128), gp + "stg", writes=STG.r())
        dma("sp", STG[0:4, 640:768], pscale_d[l].rearrange("(t p) -> t p", p=128), gp + "stg", writes=STG.r())
        cnrv = CNR[:, :].rearrange("p (i n) -> p i n", i=4)
        cniv = CNI[:, :].rearrange("p (i n) -> p i n", i=4)
        for hh in range(2):
            dma("sp", cnrv[:, :, hh * 64:(hh + 1) * 64], cre_d[l].rearrange("(i p) n -> p i n", p=128), gp + "par", writes=CNR.r())
            dma("sp", cniv[:, :, hh * 64:(hh + 1) * 64], cim_d[l].rearrange("(i p) n -> p i n", p=128), gp + "par", writes=CNI.r())
        if l == 0:
            xv = X[:, :].rearrange("p (t d) -> p t d", t=16)
            for tt in range(16):
                dma("sp", xv[:, tt, :], x_d[tt * 128:(tt + 1) * 128, :], "xin%d" % tt, writes=X.r(tt * D, (tt + 1) * D))
        load_w(WA, 0, w_in_d[l, :, 0:512], 512, gp + "wua", 8)
        load_w(WGLU, 0, wglu_d[l], 512, gp + "wglu", 4)
        load_w(WUB, 0, w_in_d[l, :, 1024:1536], 512, gp + "wub", 8)
        load_w(WZB, 0, w_in_d[l, :, 1536:2048], 512, gp + "wzb", 8)
        dma("pool", WPOOL[:, :].rearrange("p (w n) -> p w n", w=4), poolw_d[l].rearrange("w p n -> p w n"), gp + "wpool", writes=WPOOL.r())
        wg_use = [0]
        load_wg(l, 0, WG[0], gp + "wgu0")
        load_wg(l, 1, WG[1], gp + "wgu1")


    def prologue_compute(l):
        gp = "L%d" % l
        stop(1)
        bst = bank()

        def trs(e, bst=bst):
            ins = None
            for k in range(6):
                ins = e.transpose(PS[:, bst + k * 32:bst + (k + 1) * 32], STG[0:32, k * 128:(k + 1) * 128], IDF[0:32, 0:32])
            return ins
        S.op("pe", trs, reads=STG.r() + IDF.r(), writes=PS.r(bst, bst + 192))
        for (dst, k, n) in ((LAMR, 0, 32), (LAMI, 1, 32), (DVEC, 2, 32), (BPP, 3, 28), (BGLU, 4, 4), (PSC, 5, 4)):
            S.op("dve", lambda e, dst=dst, k=k, n=n, bst=bst: e.tensor_copy(out=dst[:, 0:n], in_=PS[:, bst + k * 32:bst + k * 32 + n]),
                 reads=PS.r(bst, bst + 192), writes=dst.r())
        def ew(eng, fn, reads, writes):
            S.op(eng, fn, reads, writes)

        def g32(t):
            return t[:, 0:32]

        T0, T1, T2, T3 = TG
        for (src, dst) in ((CNR, CTR), (CNI, CTI)):
            b = bank()

            def trc(e, src=src, b=b):
                ins = None
                for i in range(4):
                    ins = e.transpose(PS[:, b + i * 128:b + (i + 1) * 128], src[:, i * 128:(i + 1) * 128], IDF[:, :])
                return ins
            S.op("pe", trc, reads=src.r() + IDF.r(), writes=PS.r(b, b + 512))
            S.op("dve", lambda e, dst=dst, b=b: e.tensor_copy(out=dst[:, :], in_=PS[:, b:b + 512]),
                 reads=PS.r(b, b + 512), writes=dst.r())
        LN2H, LN2L = 0.693359375, -2.12194440e-4
        tA, tB, tC = T0[:, 0:32], T0[:, 32:64], T0[:, 64:96]
        R0 = T0.r()
        ew("dve", lambda e: e.tensor_scalar(out=tA, in0=LDT[:, :], scalar1=1.4426950408889634, scalar2=None, op0=ALU.mult), LDT.r(), R0)
        ew("dve", lambda e: e.tensor_copy(out=TGI[:, 0:32], in_=tA), R0, TGI.r())
        ew("dve", lambda e: e.tensor_copy(out=tA, in_=TGI[:, 0:32]), TGI.r(), R0)
        ew("dve", lambda e: e.scalar_tensor_tensor(out=tB, in0=tA, scalar=-LN2H, in1=LDT[:, :], op0=ALU.mult, op1=ALU.add), R0 + LDT.r(), R0)
        ew("dve", lambda e: e.scalar_tensor_tensor(out=tB, in0=tA, scalar=-LN2L, in1=tB, op0=ALU.mult, op1=ALU.add), R0, R0)
        ew("dve", lambda e: e.tensor_scalar(out=tC, in0=tB, scalar1=1.0 / 9, scalar2=1.0, op0=ALU.mult, op1=ALU.add), R0, R0)
        for kk in (8, 7, 6, 5, 4, 3, 2, 1):
            ew("dve", lambda e: e.tensor_tensor(out=tC, in0=tC, in1=tB, op=ALU.mult), R0, R0)
            ew("dve", lambda e, kk=kk: e.tensor_scalar(out=tC, in0=tC, scalar1=1.0 / kk, scalar2=1.0, op0=ALU.mult, op1=ALU.add), R0, R0)
        ew("dve", lambda e: e.tensor_scalar(out=tA, in0=tA, scalar1=127.0, scalar2=8388608.0, op0=ALU.add, op1=ALU.mult), R0, R0)
        ew("dve", lambda e: e.tensor_copy(out=TGI[:, 0:32], in_=tA), R0, TGI.r())
        ew("dve", lambda e: e.tensor_tensor(out=g32(DTT), in0=tC, in1=TGI[:, 0:32].bitcast(F32), op=ALU.mult), R0 + TGI.r(), DTT.r())
        ew("dve", lambda e: e.tensor_tensor(out=g32(LRD), in0=LAMR[:, :], in1=g32(DTT), op=ALU.mult), LAMR.r() + DTT.r(), LRD.r())
        ew("dve", lambda e: e.tensor_tensor(out=g32(ANGV), in0=LAMI[:, :], in1=g32(DTT), op=ALU.mult), LAMI.r() + DTT.r(), ANGV.r())
        ew("dve", lambda e: e.tensor_scalar(out=g32(ANGV), in0=g32(ANGV), scalar1=1.0 / TWO_PI, scalar2=None, op0=ALU.mult),
           ANGV.r(), ANGV.r())
        e3 = ETAB[:, :].rearrange("p (o e) -> p o e", o=1).to_broadcast([128, 32, 16])
        a3 = g32(ANGV).rearrange("p (g o) -> p g o", o=1).to_broadcast([128, 32, 16])
        l3 = g32(LRD).rearrange("p (g o) -> p g o", o=1).to_broadcast([128, 32, 16])

        def v16(t):
            return t[:, :].rearrange("p (g e) -> p g e", g=32)

        def sincos(vsrc_reads, make_v, n, out_sin, out_cos, tv, tc, ti, ri_reads=()):
            make_v()
            for (outt, shift, tw) in ((out_sin, 0.0, tv), (out_cos, 0.25, tc)):
                if outt is None:
                    continue
                if shift != 0.0:
                    ew("dve", lambda e, tw=tw: e.tensor_scalar(out=tw[:, 0:n], in0=tv[:, 0:n], scalar1=shift, scalar2=None, op0=ALU.add),
                       tv.r(0, n), tw.r(0, n))
                ew("dve", lambda e, tw=tw: e.tensor_copy(out=ti[:, 0:n], in_=tw[:, 0:n]), tw.r(0, n), ti.r(0, n))
                ew("dve", lambda e, tw=tw: e.tensor_tensor(out=tw[:, 0:n], in0=tw[:, 0:n], in1=ti[:, 0:n], op=ALU.subtract),
                   tw.r(0, n) + ti.r(0, n), tw.r(0, n))
                ew("act", lambda e, tw=tw, outt=outt: e.activation(out=outt[:, 0:n], in_=tw[:, 0:n], func=AF.Sin, scale=TWO_PI * 0.99999),
                   tw.r(0, n), outt.r(0, n))

        def mk_v_pow():
            ew("dve", lambda e: e.tensor_tensor(out=v16(T0), in0=a3, in1=e3, op=ALU.mult), ANGV.r() + ETAB.r(), T0.r())
        mk_v_pow()
        ew("dve", lambda e: e.tensor_scalar(out=T1[:, :], in0=T0[:, :], scalar1=0.25, scalar2=None, op0=ALU.add), T0.r(), T1.r())
        for tw, outt in ((T1, AIM), (T0, ARE)):
            ew("dve", lambda e, tw=tw: e.tensor_copy(out=TGI[:, :], in_=tw[:, :]), tw.r(), TGI.r())
            ew("dve", lambda e, tw=tw: e.tensor_tensor(out=tw[:, :], in0=tw[:, :], in1=TGI[:, :], op=ALU.subtract), tw.r() + TGI.r(), tw.r())
            ew("act", lambda e, tw=tw, outt=outt: e.activation(out=outt[:, :], in_=tw[:, :], func=AF.Sin, scale=TWO_PI * 0.99999),
               tw.r(), outt.r())
        ew("dve", lambda e: e.tensor_tensor(out=v16(T2), in0=l3, in1=e3, op=ALU.mult), LRD.r() + ETAB.r(), T2.r())
        ew("act", lambda e: e.activation(out=T2[:, :], in_=T2[:, :], func=AF.Exp), T2.r(), T2.r())
        ew("dve", lambda e: e.tensor_tensor(out=T0[:, :], in0=AIM[:, :], in1=T2[:, :], op=ALU.mult), AIM.r() + T2.r(), T0.r())
        ew("dve", lambda e: e.tensor_tensor(out=T1[:, :], in0=ARE[:, :], in1=T2[:, :], op=ALU.mult), ARE.r() + T2.r(), T1.r())
        ew("dve", lambda e: e.tensor_copy(out=ARE[:, :], in_=T0[:, :]), T0.r(), ARE.r())
        ew("dve", lambda e: e.tensor_copy(out=AIM[:, :], in_=T1[:, :]), T1.r(), AIM.r())
        for (PT, top, ts_, bot, bs_) in ((P1, ARE, 1.0, AIM, -1.0), (P2, AIM, -1.0, ARE, -1.0), (P3, ARE, -1.0, AIM, 1.0)):
            ew("act", lambda e, PT=PT, top=top, ts_=ts_: e.activation(out=PT[0:64, :], in_=top[0:64, :], func=AF.Copy, scale=ts_),
               top.r(), PT.r())
            ew("act", lambda e, PT=PT, bot=bot, bs_=bs_: e.activation(out=PT[64:128, :], in_=bot[64:128, :], func=AF.Copy, scale=bs_),
               bot.r(), PT.r())
        are1 = v16(ARE)[:, :, 8]
        aim1 = v16(AIM)[:, :, 8]
        c0, c1, c2, c3 = (T2[:, 0:32], T2[:, 32:64], T2[:, 64:96], T2[:, 96:128])
        c4, c5, c6, c7 = (T2[:, 128:160], T2[:, 160:192], T2[:, 192:224], T2[:, 224:256])
        RW = T2.r()
        AR_ = ARE.r() + AIM.r()
        ew("dve", lambda e: e.tensor_scalar(out=c0, in0=are1, scalar1=-1.0, scalar2=None, op0=ALU.add), AR_ + RW, RW)
        ew("dve", lambda e: e.tensor_tensor(out=c1, in0=LAMR[:, :], in1=LAMR[:, :], op=ALU.mult), LAMR.r() + RW, RW)
        ew("dve", lambda e: e.tensor_tensor(out=c2, in0=LAMI[:, :], in1=LAMI[:, :], op=ALU.mult), LAMI.r() + RW, RW)
        ew("dve", lambda e: e.tensor_tensor(out=c1, in0=c1, in1=c2, op=ALU.add), RW, RW)
        ew("dve", lambda e: e.reciprocal(out=c1, in_=c1), RW, RW)
        ew("dve", lambda e: e.tensor_tensor(out=c2, in0=c0, in1=LAMR[:, :], op=ALU.mult), LAMR.r() + RW, RW)
        ew("dve", lambda e: e.tensor_tensor(out=c3, in0=aim1, in1=LAMI[:, :], op=ALU.mult), AR_ + LAMI.r() + RW, RW)
        ew("dve", lambda e: e.tensor_tensor(out=c2, in0=c2, in1=c3, op=ALU.add), RW, RW)
        ew("dve", lambda e: e.tensor_tensor(out=c4, in0=c2, in1=c1, op=ALU.mult), RW, RW)
        ew("dve", lambda e: e.tensor_tensor(out=c2, in0=aim1, in1=LAMR[:, :], op=ALU.mult), AR_ + LAMR.r() + RW, RW)
        ew("dve", lambda e: e.tensor_tensor(out=c3, in0=c0, in1=LAMI[:, :], op=ALU.mult), LAMI.r() + RW, RW)
        ew("dve", lambda e: e.tensor_tensor(out=c2, in0=c2, in1=c3, op=ALU.subtract), RW, RW)
        ew("dve", lambda e: e.tensor_tensor(out=c5, in0=c2, in1=c1, op=ALU.mult), RW, RW)
        arq = v16(ARE)[:, :, 7:15]
        aiq = v16(AIM)[:, :, 7:15]
        cr3 = c4.rearrange("p (g o) -> p g o", o=1).to_broadcast([128, 32, 8])
        ci3 = c5.rearrange("p (g o) -> p g o", o=1).to_broadcast([128, 32, 8])

        def v8(t, lo=0):
            return t[:, lo:lo + 256].rearrange("p (g q) -> p g q", g=32)
        ew("dve", lambda e: e.tensor_tensor(out=v8(T0), in0=arq, in1=cr3, op=ALU.mult), AR_ + RW, T0.r())
        ew("dve", lambda e: e.tensor_tensor(out=v8(T0, 256), in0=aiq, in1=ci3, op=ALU.mult), AR_ + RW, T0.r())
        ew("dve", lambda e: e.tensor_tensor(out=v8(T1), in0=arq, in1=ci3, op=ALU.mult), AR_ + RW, T1.r())
        ew("dve", lambda e: e.tensor_tensor(out=v8(T1, 256), in0=aiq, in1=cr3, op=ALU.mult), AR_ + RW, T1.r())
        ew("dve", lambda e: e.tensor_tensor(out=GX1[:, :], in0=T0[:, 0:256], in1=T0[:, 256:512], op=ALU.subtract), T0.r(), GX1.r())
        ew("dve", lambda e: e.tensor_tensor(out=GX2[:, :], in0=T1[:, 0:256], in1=T1[:, 256:512], op=ALU.add), T1.r(), GX2.r())
        ew("dve", lambda e: e.tensor_scalar(out=GX2[0:64, :], in0=GX2[0:64, :], scalar1=-1.0, scalar2=None, op0=ALU.mult), GX2.r(), GX2.r())
        ew("act", lambda e: e.activation(out=g32(R8), in_=g32(LRD), func=AF.Exp, scale=8.0), LRD.r(), R8.r())
        ew("dve", lambda e: e.tensor_scalar(out=g32(X8), in0=g32(ANGV), scalar1=8.0, scalar2=None, op0=ALU.mult), ANGV.r(), X8.r())
        ew("dve", lambda e: e.tensor_copy(out=TGI[:, 0:32], in_=g32(X8)), X8.r(), TGI.r())
        ew("dve", lambda e: e.tensor_tensor(out=g32(X8), in0=g32(X8), in1=TGI[:, 0:32], op=ALU.subtract), X8.r() + TGI.r(), X8.r())


    for l in range(DEPTH if KSTOP > 0 else 0):
      try:
            gp = "L%d" % l
            if l == 0:
                prologue_dma(0)
                prologue_early(0)
                prologue_compute(0)
            load_w(WOUT, 0, wout_d[l], D, gp + "wout", 8)
            stop(2)
            htv4 = HT[:, :].rearrange("p (k j s) -> p k j s", k=8, s=8)
            wav = WA[:, :].rearrange("p (k n) -> p k n", k=8)
            atv = A_TM[:, :].rearrange("p (g q c) -> p g q c", g=32, q=8)
            uv = U[:, :].rearrange("p (g j) -> p g j", g=32)
            for h in range(2):
                norm_half(h)
                for s in range(8):
                    b = bank()

                    def mm(e, s=s, b=b):
                        for k in range(8):
                            e.matmul(PS[:, b:b + 512], lhsT=htv4[:, k, :, s], rhs=wav[:, k, :], start=(k == 0), stop=False)
                        return e.matmul(PS[:, b:b + 512], lhsT=ONES[0:1, :], rhs=BUA[0:1, :], start=False, stop=True)
                    S.op("pe", mm, reads=HT.r() + WA.r() + ONES.r() + BUA.r(), writes=PS.r(b, b + 512))
                    q = 7 - s
                    S.op("act" if s % 2 else "dve",
                         (lambda e, b=b, q=q: e.activation(out=atv[:, :, q, :], in_=PS[:, b:b + 512].rearrange("p (g c) -> p g c", g=32), func=AF.Copy))
                         if s % 2 else
                         (lambda e, b=b, q=q: e.tensor_copy(out=atv[:, :, q, :], in_=PS[:, b:b + 512].rearrange("p (g c) -> p g c", g=32))),
                         reads=PS.r(b, b + 512), writes=A_TM.r())
                for g4 in range(8):
                    b = bank()
                    psb16 = PS[:, b:b + 512].bitcast(BF16)

                    def tr(e, g4=g4, psb16=psb16):
                        ins = None
                        for gi in range(4):
                            g = g4 * 4 + gi
                            ins = e.transpose(psb16[:, gi * 128:(gi + 1) * 128], A_TM[:, g * 128:(g + 1) * 128], IDB[:, :])
                        return ins
                    S.op("pe", tr, reads=A_TM.r() + IDB.r(), writes=PS.r(b, b + 512))
                    S.op("act" if g4 % 2 else "dve",
                         (lambda e, g4=g4, psb16=psb16, h=h: e.activation(out=uv[:, g4 * 4:(g4 + 1) * 4, h * 128:(h + 1) * 128],
                                                                          in_=psb16[:, 0:512].rearrange("p (g j) -> p g j", g=4), func=AF.Copy))
                         if g4 % 2 else
                         (lambda e, g4=g4, psb16=psb16, h=h: e.tensor_copy(out=uv[:, g4 * 4:(g4 + 1) * 4, h * 128:(h + 1) * 128],
                                                                           in_=psb16[:, 0:512].rearrange("p (g j) -> p g j", g=4))),
                         reads=PS.r(b, b + 512), writes=U.r(g4 * 4 * 256, (g4 + 1) * 4 * 256))

            stop(3)
            load_w(WA, 0, w_in_d[l, :, 512:1024], 512, gp + "wza", 8)
            ctr3 = CTR[:, :].rearrange("p (g c) -> p g c", g=32)
            cti3 = CTI[:, :].rearrange("p (g c) -> p g c", g=32)
            by13 = BY1[:, :].rearrange("p (g c) -> p g c", g=32)
            by23 = BY2[:, :].rearrange("p (g c) -> p g c", g=32)
            gx13 = GX1[:, :].rearrange("p (g q) -> p g q", g=32)
            gx23 = GX2[:, :].rearrange("p (g q) -> p g q", g=32)
            p13 = v16(P1)
            p23 = v16(P2)
            p33 = v16(P3)
            ytv = YT[:, :].rearrange("p (i j t) -> p i j t", i=4, t=8)
            def bx(v):
                return v.rearrange("p g (a o) -> p g a o", o=1).to_broadcast([128, 2, 8, 16])

            def by(v):
                return v.rearrange("p g (o b) -> p g o b", o=1).to_broadcast([128, 2, 8, 16])

            def outer(dst, Xs, Ys, X2s, Y2s, rd, part=None):
                d4 = dst[:, :].rearrange("p (g a b) -> p g a b", g=2, a=8)
                ta = TMPA[:, 0:256].rearrange("p (g a b) -> p g a b", g=2, a=8)
                tb = TMPB[:, 0:256].rearrange("p (g a b) -> p g a b", g=2, a=8)
                if part in (None, 0):
                    ew("pool", lambda e: e.tensor_tensor(out=ta, in0=bx(Xs), in1=by(Ys), op=ALU.mult), rd, TMPA.r(0, 256))
                    ew("pool", lambda e: e.tensor_tensor(out=tb, in0=bx(X2s), in1=by(Y2s), op=ALU.mult), rd, TMPB.r(0, 256))
                if part in (None, 1):
                    ew("dve", lambda e: e.tensor_tensor(out=d4, in0=ta, in1=tb, op=ALU.add), TMPA.r(0, 256) + TMPB.r(0, 256), dst.r())

            rdB = GX1.r() + GX2.r() + BY1.r() + BY2.r()
            rdC = P1.r() + P2.r() + P3.r() + CTR.r() + CTI.r()

            def stage_T0(bt):
                g0 = bt * 2
                BF_ = TBL[bt % 2][0]
                gs = slice(g0, g0 + 2)
                outer(BF_, gx13[:, gs, :], by13[:, gs, :], gx23[:, gs, :], by23[:, gs, :], rdB, part=0)

            def stage_T(bt, first_done=False, mid_hook=None):
                g0 = bt * 2
                BF_, CFM, CF1, CF2, BM12, M0 = TBL[bt % 2]
                gs = slice(g0, g0 + 2)
                outer(BF_, gx13[:, gs, :], by13[:, gs, :], gx23[:, gs, :], by23[:, gs, :], rdB, part=(1 if first_done else None))
                outer(CFM, p13[:, gs, 0:8], ctr3[:, gs, :], p23[:, gs, 0:8], cti3[:, gs, :], rdC, part=0)
                if mid_hook is not None:
                    mid_hook()
                outer(CFM, p13[:, gs, 0:8], ctr3[:, gs, :], p23[:, gs, 0:8], cti3[:, gs, :], rdC, part=1)
                outer(CF1, p13[:, gs, 8:16], ctr3[:, gs, :], p23[:, gs, 8:16], cti3[:, gs, :], rdC)
                outer(CF2, p23[:, gs, 8:16], ctr3[:, gs, :], p33[:, gs, 8:16], cti3[:, gs, :], rdC)
                b = bank(2)
                for gi in range(2):
                    def mt(e, gi=gi, b=b):
                        o = b + gi * 512
                        e.matmul(PS[:, o:o + 256], lhsT=BF_[:, gi * 128:(gi + 1) * 128], rhs=IPM[:, :], start=True, stop=True)
                        return e.matmul(PS[:, o + 256:o + 384], lhsT=BF_[:, gi * 128:(gi + 1) * 128], rhs=CFM[:, gi * 128:(gi + 1) * 128],
                                        start=True, stop=True)
                    S.op("pe", mt, reads=BF_.r() + IPM.r() + CFM.r(), writes=PS.r(b + gi * 512, b + gi * 512 + 384))
                for gi in range(2):
                    o = b + gi * 512
                    ew("act", lambda e, gi=gi, o=o: e.activation(out=BM12[:, gi * 256:(gi + 1) * 256], in_=PS[:, o:o + 256], func=AF.Copy),
                       PS.r(o, o + 256), BM12.r(gi * 256, (gi + 1) * 256))
                psm = PS[:, b:b + 1024].rearrange("p (g n) -> p g n", g=2)[:, :, 256:384]
                cm3 = CM[:, :].rearrange("p (o n) -> p o n", o=1).to_broadcast([128, 2, 128])
                ew("dve", lambda e: e.tensor_tensor(out=TMPB[:, 0:256].rearrange("p (g n) -> p g n", g=2), in0=psm, in1=cm3, op=ALU.mult),
                   PS.r(b, b + 1024) + CM.r(), TMPB.r(0, 256))
                for gi in range(2):
                    ew("dve", lambda e, gi=gi: e.scalar_tensor_tensor(out=M0[:, gi * 128:(gi + 1) * 128], in0=JD[:, :],
                                                                      scalar=DVEC[:, g0 + gi:g0 + gi + 1], in1=TMPB[:, gi * 128:(gi + 1) * 128],
                                                                      op0=ALU.mult, op1=ALU.add),
                       JD.r() + DVEC.r() + TMPB.r(0, 256), M0.r(gi * 128, (gi + 1) * 128))

            j3 = JTAB[:, :].rearrange("p (o j) -> p o j", o=1).to_broadcast([128, 2, 256])

            def stage_P(bt):
                g0 = bt * 2
                COSp, SINp, TIp = CSET[bt % 2]
                ti3 = TIp[:, :].rearrange("p (g j) -> p g j", g=2)
                x83 = g32(X8)[:, g0:g0 + 2].rearrange("p (g o) -> p g o", o=1).to_broadcast([128, 2, 256])
                ew("dve", lambda e: e.tensor_tensor(out=ti3, in0=x83, in1=j3, op=ALU.mult), X8.r() + JTAB.r(), TIp.r())
                for gi in range(2):
                    ew("dve", lambda e, gi=gi: e.scalar_tensor_tensor(out=SINp[:, gi * 256:(gi + 1) * 256], in0=JTAB[:, :],
                                                                      scalar=X8[:, g0 + gi:g0 + gi + 1], in1=TIp[:, gi * 256:(gi + 1) * 256],
                                                                      op0=ALU.mult, op1=ALU.subtract),
                       JTAB.r() + X8.r() + TIp.r(), SINp.r(gi * 256, (gi + 1) * 256))
                ew("act", lambda e: e.activation(out=COSp[:, :], in_=SINp[:, :], func=AF.Sin, scale=TWO_PI * 0.5 * 0.99999), SINp.r() + TIp.r(), COSp.r())
                ew("act", lambda e: e.activation(out=SINp[:, :], in_=SINp[:, :], func=AF.Sin, scale=TWO_PI * 0.99999), SINp.r() + COSp.r(), SINp.r())
                ew("act", lambda e: e.activation(out=COSp[:, :], in_=COSp[:, :], func=AF.Square), COSp.r(), COSp.r())
                ew("act", lambda e: e.activation(out=COSp[:, :], in_=COSp[:, :], func=AF.Identity, scale=-2.0, bias=ONEB[:, 0:1]), COSp.r() + ONEB.r(), COSp.r())

            e13 = E1[:, :].rearrange("p (g j) -> p g j", g=2)
            e23 = E2[:, :].rearrange("p (g j) -> p g j", g=2)
            z3 = ZZ[:, :].rearrange("p (g j) -> p g j", g=2)

            def stage_S(bt):
                g0 = bt * 2
                BF_, CFM, CF1, CF2, BM12, M0 = TBL[bt % 2]
                COS, SIN, _ti = CSET[bt % 2]
                c3v = COS[:, :].rearrange("p (g j) -> p g j", g=2)
                s3v = SIN[:, :].rearrange("p (g j) -> p g j", g=2)
                b2 = bank(2)

                def ms(e, b2=b2):
                    ins = None
                    for gi in range(2):
                        for w in range(2):
                            o = b2 + w * 512 + gi * 256
                            ins = e.matmul(PS[:, o:o + 256], lhsT=BM12[:, gi * 256 + w * 128:gi * 256 + (w + 1) * 128],
                                           rhs=uv[:, g0 + gi, :], start=True, stop=True)
                    return ins
                S.op("pe", ms, reads=BM12.r() + U.r(g0 * 256, (g0 + 2) * 256), writes=PS.r(b2, b2 + 1024))
                ew("dve", lambda e, b2=b2: e.tensor_tensor(out=VV[:, :], in0=PS[:, b2:b2 + 512], in1=COS[:, :], op=ALU.mult),
                   PS.r(b2, b2 + 512) + COS.r(), VV.r())
                ew("dve", lambda e, b2=b2: e.tensor_tensor(out=ZZ[:, :], in0=PS[:, b2 + 512:b2 + 1024], in1=SIN[:, :], op=ALU.mult),
                   PS.r(b2 + 512, b2 + 1024) + SIN.r(), ZZ.r())
                ew("dve", lambda e: e.tensor_tensor(out=VV[:, :], in0=VV[:, :], in1=ZZ[:, :], op=ALU.add), VV.r() + ZZ.r(), VV.r())
                for gi in range(2):
                    ew("dve", lambda e, gi=gi: e.tensor_tensor_scan(
                        out=ZZ[:, gi * 256:(gi + 1) * 256], data0=R8[:, g0 + gi:g0 + gi + 1].to_broadcast([128, 256]),
                        data1=VV[:, gi * 256:(gi + 1) * 256], initial=0.0, op0=ALU.mult, op1=ALU.add),
                       R8.r() + VV.r(gi * 256, (gi + 1) * 256), ZZ.r(gi * 256, (gi + 1) * 256))
                ew("pool", lambda e: e.memset(e13[:, :, 0:1], 0.0), (), E1.r())
                ew("pool", lambda e: e.memset(e23[:, :, 0:1], 0.0), (), E2.r())
                if bt + 2 < 16:
                    stage_T0(bt + 2)
                ew("dve", lambda e: e.tensor_tensor(out=e13[:, :, 1:257], in0=z3, in1=c3v, op=ALU.mult), ZZ.r() + COS.r(), E1.r())
                if bt + 1 < 16 and bt >= 1:
                    stage_P(bt + 1)
                def part_b():
                    ew("pool", lambda e: e.tensor_tensor(out=e23[:, :, 1:257], in0=z3, in1=s3v, op=ALU.mult), ZZ.r() + SIN.r(), E2.r())
                    pair = bt % 4
                    for h in range(2):
                        b3 = bank()

                        def mo(e, h=h, b3=b3):
                            ins = None
                            for gi in range(2):
                                o = b3 + gi * 128
                                e.matmul(PS[:, o:o + 128], lhsT=uv[:, g0 + gi, h * 128:(h + 1) * 128], rhs=M0[:, gi * 128:(gi + 1) * 128],
                                         start=True, stop=False)
                                e.matmul(PS[:, o:o + 128], lhsT=e13[:, gi, h * 128:(h + 1) * 128], rhs=CF1[:, gi * 128:(gi + 1) * 128],
                                         start=False, stop=False)
                                ins = e.matmul(PS[:, o:o + 128], lhsT=e23[:, gi, h * 128:(h + 1) * 128], rhs=CF2[:, gi * 128:(gi + 1) * 128],
                                               start=False, stop=True)
                            return ins
                        S.op("pe", mo, reads=U.r(g0 * 256, (g0 + 2) * 256) + M0.r() + E1.r() + E2.r() + CF1.r() + CF2.r(), writes=PS.r(b3, b3 + 256))
                        yv = YTM[h][:, :].rearrange("p (t g c) -> p g t c", t=8, g=8)
                        ew("act", lambda e, h=h, b3=b3, yv=yv, pair=pair: e.activation(
                            out=yv[:, pair * 2:pair * 2 + 2, :, :],
                            in_=PS[:, b3:b3 + 256].rearrange("p (g t c) -> p g t c", g=2, t=8), func=AF.Gelu_apprx_tanh),
                           PS.r(b3, b3 + 256), YTM[h].r())
                    if pair == 3:
                        i = bt // 4
                        for h in range(2):
                            for t2 in range(2):
                                b4 = bank()
                                psb16 = PS[:, b4:b4 + 512].bitcast(BF16)

                                def ty(e, h=h, t2=t2, psb16=psb16):
                                    ins = None
                                    for tq in range(4):
                                        t = t2 * 4 + tq
                                        ins = e.transpose(psb16[:, tq * 128:(tq + 1) * 128], YTM[h][:, t * 128:(t + 1) * 128], IDB[:, :])
                                    return ins
                                S.op("pe", ty, reads=YTM[h].r() + IDB.r(), writes=PS.r(b4, b4 + 512))
                                ew("act", lambda e, h=h, t2=t2, psb16=psb16, i=i: e.activation(
                                    out=ytv[:, i, h * 128:(h + 1) * 128, t2 * 4:(t2 + 1) * 4].rearrange("p j t -> p t j"),
                                    in_=psb16[:, 0:512].rearrange("p (t j) -> p t j", t=4), func=AF.Copy),
                                   PS.r(b4, b4 + 512), YT.r(i * L, (i + 1) * L))
                return part_b

            stage_T(0)
            stage_T(1)
            stage_P(0)
            stage_P(1)
            for bt in range(16):
                pb = stage_S(bt)
                if bt + 2 < 16:
                    stage_T(bt + 2, first_done=True, mid_hook=pb)
                else:
                    pb()

            if l + 1 < DEPTH:
                prologue_early(l + 1)
            stop(4)
            for h in range(2):
                hp = gp + "h%d" % h
                if h == 0:
                    norm_half(h)
                htv = HT[:, :].rearrange("p (k n) -> p k n", k=8)
                ytk = YT[:, :].rearrange("p (i n) -> p i n", i=4)
                wzav = WA[:, :].rearrange("p (k n) -> p k n", k=8)
                wgluv = WGLU[:, :].rearrange("p (k n) -> p k n", k=4)
                wubv = WUB[:, :].rearrange("p (k n) -> p k n", k=8)
                wzbv = WZB[:, :].rearrange("p (k n) -> p k n", k=8)
                ybv = YB[:, :].rearrange("p (i n) -> p i n", i=4)
                mgv = MG[:, :].rearrange("p (i n) -> p i n", i=8)

                def proj(wv, i, c, nk, src_v, src_reads, wreads):
                    b = bank()

                    def mm(e, b=b):
                        ins = None
                        for k in range(nk):
                            ins = e.matmul(PS[:, b:b + 512], lhsT=wv[:, k, i * 128:(i + 1) * 128], rhs=src_v[:, k, c * 512:(c + 1) * 512],
                                           start=(k == 0), stop=(k == nk - 1))
                        return ins
                    S.op("pe", mm, reads=src_reads + wreads, writes=PS.r(b, b + 512))
                    return b

                for c in range(2):
                    tok0 = h * 1024 + c * 512
                    for w, win in enumerate((2, 4, 8, 16)):
                        b = proj(wubv, w, c, 8, htv, HT.r(), WUB.r())
                        ew("act", lambda e, b=b, w=w: e.activation(out=UBX[w][:, 16:528], in_=PS[:, b:b + 512], func=AF.Identity, bias=BPP[:, 4 + w:5 + w]),
                           PS.r(b, b + 512) + BPP.r(), UBX[w].r())
                        if h == 0 and c == 0:
                            ew("dve", lambda e, w=w: e.memset(UBX[w][:, 0:16], 0.0), (), UBX[w].r())
                        else:
                            ew("dve", lambda e, w=w: e.tensor_copy(out=UBX[w][:, 0:16], in_=HALO[:, w * 16:(w + 1) * 16]), HALO.r(), UBX[w].r())
                        ew("dve", lambda e, w=w: e.tensor_copy(out=HALO[:, w * 16:(w + 1) * 16], in_=UBX[w][:, 512:528]), UBX[w].r(), HALO.r())
                        srcT = UBX[w]
                        m = 1
                        dsts = [CF[1], CF[2], CF[1], CF[2]]
                        di = 0
                        while m < win:
                            dstT = dsts[di]
                            di += 1
                            ew("dve", lambda e, srcT=srcT, dstT=dstT, m=m: e.tensor_tensor(out=dstT[:, 16:528], in0=srcT[:, 16:528],
                                                                                          in1=srcT[:, 16 - m:528 - m], op=ALU.add),
                               srcT.r(), dstT.r())
                            if 2 * m < win:
                                ew("dve", lambda e, srcT=srcT, dstT=dstT, m=m: e.tensor_tensor(out=dstT[:, m:16], in0=srcT[:, m:16],
                                                                                              in1=srcT[:, 0:16 - m], op=ALU.add),
                                   srcT.r(), dstT.r())
                                ew("pool", lambda e, dstT=dstT, m=m: e.memset(dstT[:, 0:m], 0.0), dstT.r(), dstT.r())
                            srcT = dstT
                            m *= 2
                        ew("dve", lambda e, srcT=srcT, win=win, w=w: e.scalar_tensor_tensor(out=PB[w][:, :], in0=srcT[:, 16:528], scalar=1.0 / win,
                                                                                            in1=UBX[w][:, 16:528], op0=ALU.mult, op1=ALU.subtract),
                           srcT.r() + UBX[w].r(), PB[w].r())
                        if h == 0 and c == 0:
                            ew("dve", lambda e, srcT=srcT, w=w: e.tensor_tensor(out=CF[3][:, 16:32], in0=srcT[:, 16:32], in1=INVC[:, w * 16:(w + 1) * 16], op=ALU.mult),
                               srcT.r() + INVC.r() + CF[3].r(), CF[3].r())
                            ew("dve", lambda e, w=w: e.tensor_tensor(out=PB[w][:, 0:16], in0=CF[3][:, 16:32], in1=UBX[w][:, 16:32], op=ALU.subtract),
                               CF[3].r() + UBX[w].r() + PB[w].r(), PB[w].r())
                    for i in range(4):
                        b = proj(wgluv, i, 0, 4, ytk[:, :, tok0:tok0 + 512], YT.r(), WGLU.r())
                        ew("act", lambda e, b=b, i=i: e.activation(out=CH[i][:, :], in_=PS[:, b:b + 512], func=AF.Sigmoid, bias=BGLU[:, i:i + 1]),
                           PS.r(b, b + 512) + BGLU.r(), CH[i].r())
                    for i in range(4):
                        b = proj(wzav, i, c, 8, htv, HT.r(), WA.r())
                        sb = SLB[i % 2]
                        ew("act", lambda e, b=b, i=i, sb=sb: e.activation(out=sb[:, :], in_=PS[:, b:b + 512], func=AF.Silu, bias=BPP[:, i:i + 1]),
                           PS.r(b, b + 512) + BPP.r(), sb.r())
                        ysl = YT[:, i * L + tok0:i * L + tok0 + 512]
                        yr = YT.r(i * L + tok0, i * L + tok0 + 512)
                        ew("pool", lambda e, i=i, ysl=ysl, sb=sb: e.tensor_tensor(out=CH[i][:, :], in0=CH[i][:, :], in1=sb[:, :], op=ALU.mult),
                           CH[i].r() + sb.r(), CH[i].r())
                        ew("dve", lambda e, i=i, ysl=ysl: e.tensor_tensor(out=ysl, in0=ysl, in1=CH[i][:, :], op=ALU.mult),
                           CH[i].r() + yr, yr)
                    for w in range(4):
                        b6 = proj(wzbv, w, c, 8, htv, HT.r(), WZB.r())
                        b5 = bank()
                        S.op("pe", lambda e, b5=b5, w=w: e.matmul(PS[:, b5:b5 + 512], lhsT=WPOOL[:, w * 128:(w + 1) * 128], rhs=PB[w][:, :], start=True, stop=True),
                             reads=WPOOL.r() + PB[w].r(), writes=PS.r(b5, b5 + 512))
                        sb = SLB[w % 2]
                        ew("act", lambda e, b6=b6, w=w, sb=sb: e.activation(out=sb[:, :], in_=PS[:, b6:b6 + 512], func=AF.Silu, bias=BPP[:, 8 + w:9 + w]),
                           PS.r(b6, b6 + 512) + BPP.r(), sb.r())
                        ew("dve", lambda e, b5=b5, w=w, c=c: e.scalar_tensor_tensor(out=ybv[:, w, c * 512:(c + 1) * 512], in0=PS[:, b5:b5 + 512],
                                                                                    scalar=PSC[:, w:w + 1], in1=sb[:, :], op0=ALU.mult, op1=ALU.mult),
                           PS.r(b5, b5 + 512) + PSC.r() + sb.r(), YB.r(w * 1024 + c * 512, w * 1024 + (c + 1) * 512))
                stop(5)
                for o in range(8):
                    u = h * 8 + o
                    wg = WG[u % 2]
                    wgv = wg[:, :].rearrange("p (k n) -> p k n", n=128)
                    if u >= 1 and u + 1 < 16:
                        load_wg(l, (u + 1) % 8, WG[(u + 1) % 2], gp + "wgu%d" % (u + 1))
                    for c in range(2):
                        tok0 = h * 1024 + c * 512
                        outs = []
                        for (koff, nk, srcv, srd) in ((0, 8, htv, HT.r()), (8, 8, htv, HT.r()),
                                                      (16, 4, ytk[:, :, h * 1024:(h + 1) * 1024], YT.r()), (20, 4, ybv, YB.r())):
                            b = bank()

                            def mm(e, b=b, koff=koff, nk=nk, srcv=srcv, c=c):
                                ins = None
                                for k in range(nk):
                                    ins = e.matmul(PS[:, b:b + 512], lhsT=wgv[:, koff + k, :], rhs=srcv[:, k, c * 512:(c + 1) * 512],
                                                   start=(k == 0), stop=(k == nk - 1))
                                return ins
                            S.op("pe", mm, reads=srd + wg.r(), writes=PS.r(b, b + 512))
                            outs.append(b)
                        par = (o * 2 + c) % 2
                        sga, sgb = (CH[2], CH[0])[par], (CH[3], CH[1])[par]
                        pa, pb = (CF[1], CF[0])[par], (CF[2], CF[3])[par]
                        ew("act", lambda e, b=outs[0], o=o: e.activation(out=sga[:, :], in_=PS[:, b:b + 512], func=AF.Sigmoid, bias=BPP[:, 12 + o:13 + o]),
                           PS.r(outs[0], outs[0] + 512) + BPP.r(), sga.r())
                        ew("act", lambda e, b=outs[1], o=o: e.activation(out=sgb[:, :], in_=PS[:, b:b + 512], func=AF.Sigmoid, bias=BPP[:, 20 + o:21 + o]),
                           PS.r(outs[1], outs[1] + 512) + BPP.r(), sgb.r())
                        ew("dve", lambda e, b=outs[2]: e.tensor_tensor(out=pa[:, 0:512], in0=PS[:, b:b + 512], in1=sga[:, :], op=ALU.mult),
                           PS.r(outs[2], outs[2] + 512) + sga.r(), pa.r())
                        ew("dve", lambda e, b=outs[3]: e.tensor_tensor(out=pb[:, 0:512], in0=PS[:, b:b + 512], in1=sgb[:, :], op=ALU.mult),
                           PS.r(outs[3], outs[3] + 512) + sgb.r(), pb.r())
                        ew("pool", lambda e, o=o, c=c: e.tensor_tensor(out=mgv[:, o, c * 512:(c + 1) * 512], in0=pa[:, 0:512], in1=pb[:, 0:512], op=ALU.add),
                           pa.r() + pb.r(), MG.r(o * 1024 + c * 512, o * 1024 + (c + 1) * 512))
                stop(6)
                woutv = WOUT[:, :].rearrange("p (k n) -> p k n", k=8)
                npipe = NormPipe(1, [XN, XN2]) if h == 0 else None
                if npipe is not None:
                    npipe.front(0)
                if h == 1 and l + 1 < DEPTH:
                    prologue_dma(l + 1)
                for t8 in range(8):
                    tt = h * 8 + t8
                    if npipe is not None and t8 + 1 < 8:
                        npipe.front(t8 + 1)
                    if h == 1 and l + 1 < DEPTH and t8 == 3:
                        prologue_compute(l + 1)
                    for nh in range(2):
                        b = bank()

                        def mm(e, b=b, t8=t8, nh=nh):
                            ins = None
                            for k in range(8):
                                ins = e.matmul(PS[:, b:b + 512], lhsT=mgv[:, k, t8 * 128:(t8 + 1) * 128], rhs=woutv[:, k, nh * 512:(nh + 1) * 512],
                                               start=(k == 0), stop=(k == 7))
                            return ins
                        S.op("pe", mm, reads=MG.r() + WOUT.r(), writes=PS.r(b, b + 512))
                        xs = X[:, tt * D + nh * 512:tt * D + (nh + 1) * 512]
                        xr = X.r(tt * D + nh * 512, tt * D + (nh + 1) * 512)
                        rt = RT[(t8 * 2 + nh) % 2]
                        ew("act", lambda e, b=b, rt=rt: e.activation(out=rt[:, :], in_=PS[:, b:b + 512], func=AF.Copy),
                           PS.r(b, b + 512), rt.r())
                        ew("pool", lambda e, xs=xs, rt=rt: e.tensor_tensor(out=xs, in0=xs, in1=rt[:, :], op=ALU.add),
                           rt.r() + xr, xr)
                    if npipe is not None:
                        npipe.back(t8)
                    if l == DEPTH - 1 and h == 1:
                        final_tile(t8)
                        final_tile(8 + t8)
                if npipe is not None:
                    npipe.flush()

            stop(7)
      except StopBuild:
        break

    if os.environ.get("KDUMP"):
        want = os.environ["KDUMP"].split(",")
        loc = dict(ARE=ARE, AIM=AIM, P1=P1, P2=P2, P3=P3, GX1=GX1, GX2=GX2, R8=R8, X8=X8, CTR=CTR, CTI=CTI, U=U, YT=YT,
                   HT=HT, A_TM=A_TM, BF_=BF_, CFM=CFM, CF1=CF1, CF2=CF2, BM12=BM12, M0=M0, COS=COS, SIN=SIN, VV=VV, ZZ=ZZ,
                   E1=E1, E2=E2, YB=YB, MG=MG, X=X, JD=JD, CM=CM, IPM=IPM, DVEC=DVEC, BY1=BY1, LAMR=LAMR, BPP=BPP)
        for nm in want:
            t = loc[nm]
            dtp = BF16 if t.esz == 2 else F32
            dd = nc.dram_tensor("dbg_" + nm, [128, t.ncols], dtp, kind="ExternalOutput").ap()
            dma("sp", dd[:, :], t[:, :], "dbg" + nm, reads=t.r())
            S.final_wait("sp", "dbg" + nm + "_sp")

    for tt in range(16):
        if ("outst%d_sp" % tt) not in S.gtot:
            final_tile(tt)
    for tt in range(16):
        S.final_wait("sp", "outst%d_sp" % tt)
    S.emit(nc, es_global)
    return nc


es_global = None


def kernel(**inputs):
    global es_global
    ins = {k: np.ascontiguousarray(np.asarray(v, dtype=np.float32)) for k, v in inputs.items()}
    with ExitStack() as es:
        es_global = es
        nc = build_nc()
    shared = {
        "norm_g": ins["norm_g"], "w_in": ins["w_in"], "b_in": ins["b_in"],
        "ssm_log_dt": ins["ssm_log_dt"], "ssm_lam_re": ins["ssm_lam_re"], "ssm_lam_im": ins["ssm_lam_im"],
        "ssm_b_re": ins["ssm_b_re"], "ssm_b_im": ins["ssm_b_im"],
        "ssm_c_re": ins["ssm_c_re"].reshape(DEPTH, 512, 64), "ssm_c_im": ins["ssm_c_im"].reshape(DEPTH, 512, 64),
        "ssm_d": ins["ssm_d"].reshape(DEPTH, 32, 16), "ssm_w_glu": ins["ssm_w_glu"], "ssm_b_glu": ins["ssm_b_glu"],
        "pool_w": ins["pool_w"], "pool_scale": ins["pool_scale"],
        "w_branch_a": ins["w_branch_a"], "w_branch_b": ins["w_branch_b"], "w_out": ins["w_out"],
        "final_norm_g": ins["final_norm_g"],
    }
    in_maps = []
    for c in range(NCORES):
        m = dict(shared)
        m["x"] = np.ascontiguousarray(ins["x"][c])
        in_maps.append(m)
    res = run_bass_kernel_spmd(nc, in_maps, core_ids=list(range(NCORES)))
    out = np.stack([np.asarray(r["out"], dtype=np.float32) for r in res.results], axis=0)
    return out
```

```python
import os
import numpy as np
from contextlib import ExitStack
import concourse.bass as bass
import concourse.mybir as mybir
from concourse.bass_utils import run_bass_kernel_spmd

F32 = mybir.dt.float32
BF16 = mybir.dt.bfloat16
I32 = mybir.dt.int32
AF = mybir.ActivationFunctionType
ALU = mybir.AluOpType

NCORES = 8
D = 1024
L = 2048
DEPTH = 2
EPS = 1e-6
BLK = 64
SB_BASE = 16512
SB_END = 229376
TWO_PI = float(2 * np.pi)


class TT:
    def __init__(self, h, space, addr, esz, ncols):
        self.h, self.space, self.addr, self.esz, self.ncols = h, space, addr, esz, ncols

    def r(self, lo=0, hi=None):
        hi = self.ncols if hi is None else hi
        a0 = self.addr + lo * self.esz
        a1 = self.addr + hi * self.esz - 1
        if self.space == "ps":
            return [("ps", b) for b in range(a0 // 2048, a1 // 2048 + 1)]
        return [(self.space, b) for b in range(a0 // BLK, a1 // BLK + 1)]

    def __getitem__(self, k):
        return self.h[k]


class _Dummy:
    def then_inc(self, *a, **k):
        return self


class Rec:
    def __init__(self):
        self.items = []

    def __getattr__(self, nm):
        def f(*a, **kw):
            self.items.append((nm, a, kw))
            return _Dummy()
        return f


class Sched:
    ENG = ["pe", "act", "dve", "pool", "sp"]

    def __init__(self):
        self.ops = {e: [] for e in self.ENG}
        self.cnt = {e: 0 for e in self.ENG}
        self.waited = {}
        self.lw = {}
        self.rd = {}
        self.gtot = {}

    def op(self, eng, fn, reads=(), writes=(), group=None):
        tags = set()
        for b in reads:
            t = self.lw.get(b)
            if t:
                tags.add(t)
        for b in writes:
            t = self.lw.get(b)
            if t:
                tags.add(t)
            for t in self.rd.get(b, ()):
                tags.add(t)
        waits = []
        for (k, v) in sorted(tags, key=str):
            if isinstance(k, tuple):
                if group is not None and k[1] == group:
                    continue
                if self.waited.get((eng, k)):
                    continue
                self.waited[(eng, k)] = 1
                waits.append((k, None))
            else:
                if k == eng and eng == "pe" and group is None:
                    continue
                if self.waited.get((eng, k), 0) >= v:
                    continue
                self.waited[(eng, k)] = v
                waits.append((k, v))
        if group is None:
            self.cnt[eng] += 1
            tag = (eng, self.cnt[eng])
        else:
            self.gtot[group] = self.gtot.get(group, 0) + 16
            tag = (("g", group), 0)
        if callable(fn):
            rec = Rec()
            fn(rec)
            fn = rec.items
        self.ops[eng].append((waits, fn, group, tag[1] if group is None else None))
        for b in writes:
            self.lw[b] = tag
            self.rd[b] = []
        for b in reads:
            self.rd.setdefault(b, []).append(tag)

    def final_wait(self, eng, group):
        self.ops[eng].append(([(("g", group), None)], None, None, None))

    def emit(self, nc, es):
        sems = {e: es.enter_context(nc.semaphore("s_" + e)) for e in self.ENG}
        for g in self.gtot:
            sems[("g", g)] = es.enter_context(nc.semaphore("g_" + g))
        block = es.enter_context(nc.Block())
        import bisect
        ref = {e: set() for e in self.ENG}
        for e in self.ENG:
            for waits, fn, group, ordn in self.ops[e]:
                for (k, v) in waits:
                    if not isinstance(k, tuple):
                        ref[k].add(v)
        refl = {e: sorted(ref[e]) for e in self.ENG}

        def cnt_of(e, v):
            return bisect.bisect_right(refl[e], v)

        def run(eng_name):
            def body(eng):
                for _ in range(int(os.environ.get("KNOP_" + eng_name, "0"))):
                    eng.nop()
                for waits, fn, group, ordn in self.ops[eng_name]:
                    for (k, v) in waits:
                        if isinstance(k, tuple):
                            eng.wait_ge(sems[k], self.gtot[k[1]])
                        else:
                            eng.wait_ge(sems[k], cnt_of(k, v))
                    if fn is None:
                        continue
                    ins = None
                    for (nm, a, kw) in fn:
                        ins = getattr(eng, nm)(*a, **kw)
                    if group is None:
                        if ordn in ref[eng_name]:
                            ins.then_inc(sems[eng_name], 1)
                    else:
                        ins.then_inc(sems[("g", group)], 16)
            return body

        block.tensor(run("pe"))
        block.scalar(run("act"))
        block.vector(run("dve"))
        block.gpsimd(run("pool"))
        block.sync(run("sp"))


def build_nc():
    nc = bass.Bass("TRN2", target_bir_lowering=False)
    S = Sched()

    def din(name, shape):
        return nc.dram_tensor(name, list(shape), F32, kind="ExternalInput").ap()

    x_d = din("x", [L, D])
    norm_g_d = din("norm_g", [DEPTH, D])
    w_in_d = din("w_in", [DEPTH, D, 4096])
    b_in_d = din("b_in", [DEPTH, 4096])
    ldt_d = din("ssm_log_dt", [DEPTH, 32])
    lamr_d = din("ssm_lam_re", [DEPTH, 32, 64])
    lami_d = din("ssm_lam_im", [DEPTH, 32, 64])
    bre_d = din("ssm_b_re", [DEPTH, 32, 64, 16])
    bim_d = din("ssm_b_im", [DEPTH, 32, 64, 16])
    cre_d = din("ssm_c_re", [DEPTH, 512, 64])
    cim_d = din("ssm_c_im", [DEPTH, 512, 64])
    ssmd_d = din("ssm_d", [DEPTH, 32, 16])
    wglu_d = din("ssm_w_glu", [DEPTH, 512, 512])
    bglu_d = din("ssm_b_glu", [DEPTH, 512])
    poolw_d = din("pool_w", [DEPTH, 4, 128, 128])
    pscale_d = din("pool_scale", [DEPTH, 512])
    wa_d = din("w_branch_a", [DEPTH, 512, D])
    wb_d = din("w_branch_b", [DEPTH, 512, D])
    wout_d = din("w_out", [DEPTH, D, D])
    gfin_d = din("final_norm_g", [D])
    out_d = nc.dram_tensor("out", [L, D], F32, kind="ExternalOutput").ap()

    cur = [SB_BASE]
    names = [0]

    def alloc(ncols, dt, at=None):
        esz = 2 if dt == BF16 else 4
        if at is None:
            a = cur[0]
            cur[0] += (ncols * esz + 63) // 64 * 64
            assert cur[0] <= SB_END, "SBUF overflow %d" % cur[0]
        else:
            a = at
        names[0] += 1
        h = nc.alloc_sbuf_tensor_at("t%d" % names[0], [128, ncols], dt, offset=a)
        return TT(h, "sb", a, esz, ncols)

    X = alloc(16 * D, F32)
    YT = alloc(4 * L, BF16)
    WA = alloc(8 * 512, BF16)
    WGLU = alloc(4 * 512, BF16)
    WUB = alloc(8 * 512, BF16)
    WZB = alloc(8 * 512, BF16)
    WPOOL = alloc(4 * 128, BF16)
    WG = [alloc(8 * 128 * 2 + 4 * 128 * 2, BF16) for _ in range(2)]
    WOUT = alloc(8 * D, BF16)
    IDB = alloc(128, BF16)
    IDF = alloc(128, F32)
    IPM = alloc(256, BF16)
    CM = alloc(128, F32)
    JTAB = alloc(256, F32)
    ETAB = alloc(16, F32)
    ONES = alloc(128, BF16)
    INVC = alloc(64, F32)
    GTAB = alloc(D, F32)
    BPP = alloc(32, F32)
    BUA = alloc(512, BF16)
    BGLU = alloc(4, F32)
    PSC = alloc(4, F32)
    LDT = alloc(32, F32)
    LAMR = alloc(32, F32)
    LAMI = alloc(32, F32)
    BY1 = alloc(512, F32)
    BY2 = alloc(512, F32)
    DVEC = alloc(32, F32)
    HALO = alloc(64, F32)
    SMALL = alloc(64, F32)
    JD = alloc(128, F32)
    EPSB = alloc(16, F32)
    ONEB = alloc(16, F32)
    AR0 = cur[0]
    ARENA = SB_END - AR0
    assert ARENA >= 56 * 1024, ARENA

    def ar(off, ncols, dt):
        esz = 2 if dt == BF16 else 4
        assert off + ncols * esz <= ARENA, (off, ncols)
        return alloc(ncols, dt, at=AR0 + off)

    K = 1024
    HT = ar(0, 8 * 1024, BF16)
    A_TM = ar(16 * K, 4096, BF16)
    U = ar(24 * K, 32 * 256, BF16)
    CTR = ar(40 * K, 512, F32)
    CTI = ar(42 * K, 512, F32)
    P1 = ar(44 * K, 512, F32)
    P2 = ar(46 * K, 512, F32)
    P3 = ar(48 * K, 512, F32)
    GX1 = ar(50 * K, 256, F32)
    GX2 = ar(51 * K, 256, F32)
    R8 = ar(52 * K, 32, F32)
    X8 = ar(52 * K + 128, 32, F32)
    DTT = ar(52 * K + 256, 32, F32)
    LRD = ar(52 * K + 384, 32, F32)
    ANGV = ar(52 * K + 512, 32, F32)
    CNR = ar(0, 512, F32)
    CNI = ar(2 * K, 512, F32)
    ARE = ar(4 * K, 512, F32)
    AIM = ar(6 * K, 512, F32)
    TG = [ar(8 * K + i * 2 * K, 512, F32) for i in range(4)]
    TGI = alloc(512, I32, at=TG[3].addr)
    STG = ar(12 * K, 768, F32)
    COS = ar(0, 512, F32)
    SIN = ar(2 * K, 512, F32)
    VV = ar(4 * K, 512, F32)
    ZZ = ar(6 * K, 512, F32)
    TMPA = ar(10 * K, 384, F32)
    TMPB = ar(11 * K + 512, 256, F32)
    SINB = ar(12 * K + 512, 512, F32)
    COSB = ar(55936, 512, F32)
    CSET = [(COS, SIN, alloc(512, I32, at=COS.addr)), (COSB, SINB, alloc(512, I32, at=COSB.addr))]
    E1 = ar(14 * K + 512, 2 * 264, BF16)
    E2 = ar(15936, 2 * 264, BF16)
    TBL = []
    for i in range(2):
        o = 17 * K + i * 3584
        TBL.append((ar(o, 256, BF16), ar(o + 512, 256, BF16), ar(o + 1024, 256, BF16), ar(o + 1536, 256, BF16),
                    ar(o + 2048, 512, BF16), ar(o + 3072, 256, BF16)))
    BF_, CFM, CF1, CF2, BM12, M0 = TBL[0]
    YTM = [ar(53888, 1024, BF16), ar(8 * K, 1024, BF16)]
    YB = ar(16 * K, 4 * 1024, BF16)
    RT = [ar(16 * K + i * 2 * K, 512, F32) for i in range(2)]
    MG = ar(24 * K, 8 * 1024, BF16)
    PB = [ar(24 * K + i * K, 512, BF16) for i in range(4)]
    CH = [ar(40 * K + i * K, 512, BF16) for i in range(5)]
    SLB = [CH[4], ar(28 * K, 512, BF16)]
    UBX = [ar(29 * K + i * 2176, 528, F32) for i in range(4)]
    CF = [ar(45 * K + i * 2176, 512 + 16, F32) for i in range(4)]
    XN = ar(45 * K + 4 * 2176, 1024, BF16)
    XN2 = ar(45 * K, 1024, BF16)
    JUNK = alloc(512, mybir.dt.int8, at=AR0 + 45 * K + 4 * 2176 + 2048)
    JUNK.esz = 1
    SMALL2 = ar(45 * K + 4 * 2176 + 2048 + 512, 128, F32)
    print("ARENA", ARENA, "XN end", 45 * K + 4 * 2176 + 2048)
    psh = es_global.enter_context(nc.psum_tensor("PS", [128, 4096], F32))
    PS = TT(psh, "ps", 0, 4, 4096)
    psb = [0]

    def bank(n=1):
        b = psb[0]
        if b + n > 8:
            b = 0
        psb[0] = (b + n) % 8
        return b * 512

    def cp(eng_name):
        return eng_name

    def dma(q, out_ap, in_ap, group, reads=(), writes=(), slow=False):
        def fn(e):
            if slow:
                return e.dma_start(out=out_ap, in_=in_ap, allow_slow_non_contiguous=True)
            return e.dma_start(out=out_ap, in_=in_ap)
        S.op(q, fn, reads, writes, group=group + "_" + q)

    def v3(ap, **kw):
        return ap

    def _c(e):
        return e.iota(IDF[:, :], [[1, 128]], base=0, channel_multiplier=-1, allow_small_or_imprecise_dtypes=True)
    S.op("pool", _c, writes=IDF.r())
    S.op("pool", lambda e: e.iota(CM[:, :], [[16, 8], [0, 16]], base=0, channel_multiplier=1,
                                  allow_small_or_imprecise_dtypes=True), writes=CM.r())
    S.op("pool", lambda e: e.iota(JTAB[:, :], [[1, 256]], base=0, channel_multiplier=0,
                                  allow_small_or_imprecise_dtypes=True), writes=JTAB.r())
    S.op("pool", lambda e: e.iota(ETAB[:, :], [[1, 16]], base=-7, channel_multiplier=0,
                                  allow_small_or_imprecise_dtypes=True), writes=ETAB.r())
    S.op("dve", lambda e: e.tensor_scalar(out=IPM[:, 0:128], in0=IDF[:, :], scalar1=0.0, scalar2=None, op0=ALU.is_equal),
         reads=IDF.r(), writes=IPM.r(0, 128))
    S.op("dve", lambda e: e.tensor_scalar(out=TG[0][:, 0:128], in0=IDF[:, :], scalar1=-64.0, scalar2=None, op0=ALU.is_equal),
         reads=IDF.r(), writes=TG[0].r(0, 128))
    S.op("dve", lambda e: e.tensor_scalar(out=TG[1][:, 0:128], in0=IDF[:, :], scalar1=64.0, scalar2=None, op0=ALU.is_equal),
         reads=IDF.r(), writes=TG[1].r(0, 128))
    S.op("dve", lambda e: e.tensor_tensor(out=IPM[:, 128:256], in0=TG[0][:, 0:128], in1=TG[1][:, 0:128], op=ALU.subtract),
         reads=TG[0].r(0, 128) + TG[1].r(0, 128), writes=IPM.r(128, 256))
    S.op("dve", lambda e: e.tensor_copy(out=IDB[:, :], in_=IPM[:, 0:128]), reads=IPM.r(0, 128), writes=IDB.r())
    S.op("dve", lambda e: e.tensor_scalar(out=IDF[:, :], in0=IDF[:, :], scalar1=0.0, scalar2=None, op0=ALU.is_equal),
         reads=IDF.r() + IPM.r() + TG[0].r(0, 128) + TG[1].r(0, 128), writes=IDF.r())
    S.op("dve", lambda e: e.tensor_scalar(out=CM[:, :], in0=CM[:, :], scalar1=112.0, scalar2=None, op0=ALU.is_ge),
         reads=CM.r(), writes=CM.r())
    S.op("dve", lambda e: e.memset(ONES[:, :], 1.0), writes=ONES.r())
    S.op("pool", lambda e: e.iota(JD[:, :], [[-16, 8], [1, 16]], base=112, channel_multiplier=-1,
                                  allow_small_or_imprecise_dtypes=True), writes=JD.r())
    S.op("dve", lambda e: e.tensor_scalar(out=JD[:, :], in0=JD[:, :], scalar1=0.0, scalar2=None, op0=ALU.is_equal),
         reads=JD.r(), writes=JD.r())
    for wi, win in enumerate((2, 4, 8, 16)):
        S.op("dve", lambda e, wi=wi, win=win: e.tensor_scalar(out=INVC[:, wi * 16:(wi + 1) * 16], in0=JTAB[:, 0:16],
                                                              scalar1=1.0, scalar2=float(win), op0=ALU.add, op1=ALU.min),
             reads=JTAB.r(0, 16), writes=INVC.r(wi * 16, wi * 16 + 16))
    S.op("dve", lambda e: e.reciprocal(out=INVC[:, :], in_=INVC[:, :]), reads=INVC.r(), writes=INVC.r())

    class NormPipe:
        def __init__(self, h, xbufs):
            self.h = h
            self.pending = None
            self.xbufs = xbufs

        def front(self, t8):
            h = self.h
            tt = h * 8 + t8
            xn = self.xbufs[t8 % len(self.xbufs)]
            xs = X[:, tt * D:(tt + 1) * D]
            xr = X.r(tt * D, (tt + 1) * D)
            c0 = 16 * (tt % 8)
            sc = SMALL2[:, c0:c0 + 1]
            sc2 = SMALL2[:, c0 + 1:c0 + 2]
            scr = SMALL2.r(c0, c0 + 16)
            S.op("act", lambda e: e.activation(out=JUNK[:, :], in_=xs[:, 0:512], func=AF.Square, accum_out=sc), reads=xr, writes=JUNK.r() + scr)
            S.op("act", lambda e: e.activation(out=JUNK[:, :], in_=xs[:, 512:1024], func=AF.Square, accum_out=sc2), reads=xr, writes=JUNK.r() + scr)
            S.op("dve", lambda e: e.tensor_tensor(out=sc, in0=sc, in1=sc2, op=ALU.add), reads=scr, writes=scr)
            S.op("act", lambda e: e.activation(out=sc, in_=sc, func=AF.Sqrt, scale=1.0 / D, bias=EPSB[:, 0:1]), reads=scr + EPSB.r(), writes=scr)
            S.op("dve", lambda e: e.reciprocal(out=sc, in_=sc), reads=scr, writes=scr)
            S.op("dve", lambda e: e.scalar_tensor_tensor(out=xn[:, 0:1024], in0=xs, scalar=sc, in1=GTAB[:, :], op0=ALU.mult, op1=ALU.mult),
                 reads=xr + scr + GTAB.r(), writes=xn.r(0, 1024))

        def back(self, t8):
            htv = HT[:, :].rearrange("p (k n) -> p k n", k=8)
            xn = self.xbufs[t8 % len(self.xbufs)]
            b = bank()
            psb16 = PS[:, b:b + 512].bitcast(BF16)

            def tr(e):
                ins = None
                for k in range(8):
                    ins = e.transpose(psb16[:, k * 128:(k + 1) * 128], xn[:, k * 128:(k + 1) * 128], IDB[:, :])
                return ins
            S.op("pe", tr, reads=xn.r(0, 1024) + IDB.r(), writes=PS.r(b, b + 512))
            if self.pending is not None:
                self.pending()

            def evac():
                S.op("dve", lambda e: e.tensor_copy(out=htv[:, :, t8 * 128:(t8 + 1) * 128], in_=psb16.rearrange("p (k n) -> p k n", k=8)),
                     reads=PS.r(b, b + 512), writes=sum([HT.r(k * 1024 + t8 * 128, k * 1024 + (t8 + 1) * 128) for k in range(8)], []))
            self.pending = evac

        def flush(self):
            if self.pending is not None:
                self.pending()
                self.pending = None

    def norm_half(h):
        np_ = NormPipe(h, [XN])
        for t8 in range(8):
            np_.front(t8)
            np_.back(t8)
        np_.flush()

    S.op("dve", lambda e: e.memset(EPSB[:, :], EPS), writes=EPSB.r())
    S.op("dve", lambda e: e.memset(ONEB[:, :], 1.0), writes=ONEB.r())

    def load_w(dst, dst_lo, src_ap_kpn, ncols, group, ktiles):
        hi = dst_lo + ktiles * ncols
        dma("pool", dst[:, dst_lo:hi].rearrange("p (k n) -> p k n", k=ktiles),
            src_ap_kpn.rearrange("(k p) n -> p k n", p=128), group, writes=dst.r(dst_lo, hi))

    def load_wg(l, o, wg, grp):
        dma("pool", wg[:, 0:1024].rearrange("p (k n) -> p k n", k=8),
            w_in_d[l, :, 2048 + o * 128:2048 + (o + 1) * 128].rearrange("(k p) n -> p k n", p=128), grp, writes=wg.r(0, 1024))
        dma("pool", wg[:, 1024:2048].rearrange("p (k n) -> p k n", k=8),
            w_in_d[l, :, 3072 + o * 128:3072 + (o + 1) * 128].rearrange("(k p) n -> p k n", p=128), grp, writes=wg.r(1024, 2048))
        dma("pool", wg[:, 2048:2560].rearrange("p (k n) -> p k n", k=4),
            wa_d[l, :, o * 128:(o + 1) * 128].rearrange("(k p) n -> p k n", p=128), grp, writes=wg.r(2048, 2560))
        dma("pool", wg[:, 2560:3072].rearrange("p (k n) -> p k n", k=4),
            wb_d[l, :, o * 128:(o + 1) * 128].rearrange("(k p) n -> p k n", p=128), grp, writes=wg.r(2560, 3072))

    KSTOP = int(os.environ.get("KSTOP", "99"))

    class StopBuild(Exception):
        pass

    def stop(n):
        if KSTOP <= n:
            raise StopBuild()

    GF = ar(0, D, F32)
    OUTT = [ar(4 * K + i * 4 * K, D, F32) for i in range(3)]
    fin_state = {"gf": False}

    def final_tile(tt):
        if not fin_state["gf"]:
            dma("sp", GF[:, :], gfin_d.partition_broadcast(128), "gfin", writes=GF.r())
            fin_state["gf"] = True
        xs = X[:, tt * D:(tt + 1) * D]
        xr = X.r(tt * D, (tt + 1) * D)
        c0 = 16 * (tt % 8)
        sc = SMALL2[:, c0:c0 + 1]
        sc2 = SMALL2[:, c0 + 1:c0 + 2]
        scr = SMALL2.r(c0, c0 + 16)
        ot = OUTT[tt % 3]
        S.op("act", lambda e: e.activation(out=JUNK[:, :], in_=xs[:, 0:512], func=AF.Square, accum_out=sc), reads=xr, writes=JUNK.r() + scr)
        S.op("act", lambda e: e.activation(out=JUNK[:, :], in_=xs[:, 512:1024], func=AF.Square, accum_out=sc2), reads=xr, writes=JUNK.r() + scr)
        S.op("dve", lambda e: e.tensor_tensor(out=sc, in0=sc, in1=sc2, op=ALU.add), reads=scr, writes=scr)
        S.op("act", lambda e: e.activation(out=sc, in_=sc, func=AF.Sqrt, scale=1.0 / D, bias=EPSB[:, 0:1]), reads=scr + EPSB.r(), writes=scr)
        S.op("dve", lambda e: e.reciprocal(out=sc, in_=sc), reads=scr, writes=scr)
        S.op("dve", lambda e: e.scalar_tensor_tensor(out=ot[:, :], in0=xs, scalar=sc, in1=GF[:, :], op0=ALU.mult, op1=ALU.mult),
             reads=xr + scr + GF.r(), writes=ot.r())
        dma("sp", out_d[tt * 128:(tt + 1) * 128, :], ot[:, :], "outst%d" % tt, reads=ot.r())

    def ew(eng, fn, reads, writes):
        S.op(eng, fn, reads, writes)

    def g32(t):
        return t[:, 0:32]

    def v16(t):
        return t[:, :].rearrange("p (g e) -> p g e", g=32)

    def prologue_early(l):
        gp = "L%d" % l
        by1v = BY1[:, :].rearrange("p (g c) -> p g c", g=32)
        by2v = BY2[:, :].rearrange("p (g c) -> p g c", g=32)
        dma("sp", by1v[0:64], bre_d[l].rearrange("g p c -> p g c"), gp + "pe", writes=BY1.r())
        dma("sp", by1v[64:128], bim_d[l].rearrange("g p c -> p g c"), gp + "pe", writes=BY1.r())
        dma("sp", by2v[0:64], bim_d[l].rearrange("g p c -> p g c"), gp + "pe", writes=BY2.r())
        dma("sp", by2v[64:128], bre_d[l].rearrange("g p c -> p g c"), gp + "pe", writes=BY2.r())

    def prologue_dma(l):
        gp = "L%d" % l
        dma("sp", GTAB[:, :], norm_g_d[l].partition_broadcast(128), gp + "par", writes=GTAB.r())
        dma("sp", LDT[:, :], ldt_d[l].partition_broadcast(128), gp + "par", writes=LDT.r())
        dma("pool", BUA[0:1, :], b_in_d[l, 0:512].rearrange("(o n) -> o n", o=1), gp + "par", writes=BUA.r())
        S.op("dve", lambda e: e.memset(STG[:, :], 0.0), writes=STG.r())
        for hh in range(2):
            dma("sp", STG[0:32, hh * 64:(hh + 1) * 64], lamr_d[l], gp + "stg", writes=STG.r())
            dma("sp", STG[0:32, 128 + hh * 64:128 + (hh + 1) * 64], lami_d[l], gp + "stg", writes=STG.r())
        for q in range(8):
            dma("sp", STG[0:32, 256 + q * 16:256 + (q + 1) * 16], ssmd_d[l], gp + "stg", writes=STG.r())
        dma("sp", STG[0:28, 384:512], b_in_d[l, 512:4096].rearrange("(t p) -> t p", p=128), gp + "stg", writes=STG.r())
        dma("sp", STG[0:4, 512:640], bglu_d[l].rearrange("(t p) -> t p", p=128), gp + "stg", writes=STG.r())
        dma("sp", STG[0:4, 640:768], pscale_d[l].rearrange("(t p) -> t p", p=128), gp + "stg", writes=STG.r())
        cnrv = CNR[:, :].rearrange("p (i n) -> p i n", i=4)
        cniv = CNI[:, :].rearrange("p (i n) -> p i n", i=4)
        for hh in range(2):
            dma("sp", cnrv[:, :, hh * 64:(hh + 1) * 64], cre_d[l].rearrange("(i p) n -> p i n", p=128), gp + "par", writes=CNR.r())
            dma("sp", cniv[:, :, hh * 64:(hh + 1) * 64], cim_d[l].rearrange("(i p) n -> p i n", p=128), gp + "par", writes=CNI.r())
        if l == 0:
            xv = X[:, :].rearrange("p (t d) -> p t d", t=16)
            for tt in range(16):
                dma("sp", xv[:, tt, :], x_d[tt * 128:(tt + 1) * 128, :], "xin%d" % tt, writes=X.r(tt * D, (tt + 1) * D))
        load_w(WA, 0, w_in_d[l, :, 0:512], 512, gp + "wua", 8)
        load_w(WGLU, 0, wglu_d[l], 512, gp + "wglu", 4)
        load_w(WUB, 0, w_in_d[l, :, 1024:1536], 512, gp + "wub", 8)
        load_w(WZB, 0, w_in_d[l, :, 1536:2048], 512, gp + "wzb", 8)
        dma("pool", WPOOL[:, :].rearrange("p (w n) -> p w n", w=4), poolw_d[l].rearrange("w p n -> p w n"), gp + "wpool", writes=WPOOL.r())
        wg_use = [0]
        load_wg(l, 0, WG[0], gp + "wgu0")
        load_wg(l, 1, WG[1], gp + "wgu1")


    def prologue_compute(l):
        gp = "L%d" % l
        stop(1)
        bst = bank()

        def trs(e, bst=bst):
            ins = None
            for k in range(6):
                ins = e.transpose(PS[:, bst + k * 32:bst + (k + 1) * 32], STG[0:32, k * 128:(k + 1) * 128], IDF[0:32, 0:32])
            return ins
        S.op("pe", trs, reads=STG.r() + IDF.r(), writes=PS.r(bst, bst + 192))
        for (dst, k, n) in ((LAMR, 0, 32), (LAMI, 1, 32), (DVEC, 2, 32), (BPP, 3, 28), (BGLU, 4, 4), (PSC, 5, 4)):
            S.op("dve", lambda e, dst=dst, k=k, n=n, bst=bst: e.tensor_copy(out=dst[:, 0:n], in_=PS[:, bst + k * 32:bst + k * 32 + n]),
                 reads=PS.r(bst, bst + 192), writes=dst.r())
        def ew(eng, fn, reads, writes):
            S.op(eng, fn, reads, writes)

        def g32(t):
            return t[:, 0:32]

        T0, T1, T2, T3 = TG
        for (src, dst) in ((CNR, CTR), (CNI, CTI)):
            b = bank()

            def trc(e, src=src, b=b):
                ins = None
                for i in range(4):
                    ins = e.transpose(PS[:, b + i * 128:b + (i + 1) * 128], src[:, i * 128:(i + 1) * 128], IDF[:, :])
                return ins
            S.op("pe", trc, reads=src.r() + IDF.r(), writes=PS.r(b, b + 512))
            S.op("dve", lambda e, dst=dst, b=b: e.tensor_copy(out=dst[:, :], in_=PS[:, b:b + 512]),
                 reads=PS.r(b, b + 512), writes=dst.r())
        LN2H, LN2L = 0.693359375, -2.12194440e-4
        tA, tB, tC = T0[:, 0:32], T0[:, 32:64], T0[:, 64:96]
        R0 = T0.r()
        ew("dve", lambda e: e.tensor_scalar(out=tA, in0=LDT[:, :], scalar1=1.4426950408889634, scalar2=None, op0=ALU.mult), LDT.r(), R0)
        ew("dve", lambda e: e.tensor_copy(out=TGI[:, 0:32], in_=tA), R0, TGI.r())
        ew("dve", lambda e: e.tensor_copy(out=tA, in_=TGI[:, 0:32]), TGI.r(), R0)
        ew("dve", lambda e: e.scalar_tensor_tensor(out=tB, in0=tA, scalar=-LN2H, in1=LDT[:, :], op0=ALU.mult, op1=ALU.add), R0 + LDT.r(), R0)
        ew("dve", lambda e: e.scalar_tensor_tensor(out=tB, in0=tA, scalar=-LN2L, in1=tB, op0=ALU.mult, op1=ALU.add), R0, R0)
        ew("dve", lambda e: e.tensor_scalar(out=tC, in0=tB, scalar1=1.0 / 9, scalar2=1.0, op0=ALU.mult, op1=ALU.add), R0, R0)
        for kk in (8, 7, 6, 5, 4, 3, 2, 1):
            ew("dve", lambda e: e.tensor_tensor(out=tC, in0=tC, in1=tB, op=ALU.mult), R0, R0)
            ew("dve", lambda e, kk=kk: e.tensor_scalar(out=tC, in0=tC, scalar1=1.0 / kk, scalar2=1.0, op0=ALU.mult, op1=ALU.add), R0, R0)
        ew("dve", lambda e: e.tensor_scalar(out=tA, in0=tA, scalar1=127.0, scalar2=8388608.0, op0=ALU.add, op1=ALU.mult), R0, R0)
        ew("dve", lambda e: e.tensor_copy(out=TGI[:, 0:32], in_=tA), R0, TGI.r())
        ew("dve", lambda e: e.tensor_tensor(out=g32(DTT), in0=tC, in1=TGI[:, 0:32].bitcast(F32), op=ALU.mult), R0 + TGI.r(), DTT.r())
        ew("dve", lambda e: e.tensor_tensor(out=g32(LRD), in0=LAMR[:, :], in1=g32(DTT), op=ALU.mult), LAMR.r() + DTT.r(), LRD.r())
        ew("dve", lambda e: e.tensor_tensor(out=g32(ANGV), in0=LAMI[:, :], in1=g32(DTT), op=ALU.mult), LAMI.r() + DTT.r(), ANGV.r())
        ew("dve", lambda e: e.tensor_scalar(out=g32(ANGV), in0=g32(ANGV), scalar1=1.0 / TWO_PI, scalar2=None, op0=ALU.mult),
           ANGV.r(), ANGV.r())
        e3 = ETAB[:, :].rearrange("p (o e) -> p o e", o=1).to_broadcast([128, 32, 16])
        a3 = g32(ANGV).rearrange("p (g o) -> p g o", o=1).to_broadcast([128, 32, 16])
        l3 = g32(LRD).rearrange("p (g o) -> p g o", o=1).to_broadcast([128, 32, 16])

        def v16(t):
            return t[:, :].rearrange("p (g e) -> p g e", g=32)

        def sincos(vsrc_reads, make_v, n, out_sin, out_cos, tv, tc, ti, ri_reads=()):
            make_v()
            for (outt, shift, tw) in ((out_sin, 0.0, tv), (out_cos, 0.25, tc)):
                if outt is None:
                    continue
                if shift != 0.0:
                    ew("dve", lambda e, tw=tw: e.tensor_scalar(out=tw[:, 0:n], in0=tv[:, 0:n], scalar1=shift, scalar2=None, op0=ALU.add),
                       tv.r(0, n), tw.r(0, n))
                ew("dve", lambda e, tw=tw: e.tensor_copy(out=ti[:, 0:n], in_=tw[:, 0:n]), tw.r(0, n), ti.r(0, n))
                ew("dve", lambda e, tw=tw: e.tensor_tensor(out=tw[:, 0:n], in0=tw[:, 0:n], in1=ti[:, 0:n], op=ALU.subtract),
                   tw.r(0, n) + ti.r(0, n), tw.r(0, n))
                ew("act", lambda e, tw=tw, outt=outt: e.activation(out=outt[:, 0:n], in_=tw[:, 0:n], func=AF.Sin, scale=TWO_PI * 0.99999),
                   tw.r(0, n), outt.r(0, n))

        def mk_v_pow():
            ew("dve", lambda e: e.tensor_tensor(out=v16(T0), in0=a3, in1=e3, op=ALU.mult), ANGV.r() + ETAB.r(), T0.r())
        mk_v_pow()
        ew("dve", lambda e: e.tensor_scalar(out=T1[:, :], in0=T0[:, :], scalar1=0.25, scalar2=None, op0=ALU.add), T0.r(), T1.r())
        for tw, outt in ((T1, AIM), (T0, ARE)):
            ew("dve", lambda e, tw=tw: e.tensor_copy(out=TGI[:, :], in_=tw[:, :]), tw.r(), TGI.r())
            ew("dve", lambda e, tw=tw: e.tensor_tensor(out=tw[:, :], in0=tw[:, :], in1=TGI[:, :], op=ALU.subtract), tw.r() + TGI.r(), tw.r())
            ew("act", lambda e, tw=tw, outt=outt: e.activation(out=outt[:, :], in_=tw[:, :], func=AF.Sin, scale=TWO_PI * 0.99999),
               tw.r(), outt.r())
        ew("dve", lambda e: e.tensor_tensor(out=v16(T2), in0=l3, in1=e3, op=ALU.mult), LRD.r() + ETAB.r(), T2.r())
        ew("act", lambda e: e.activation(out=T2[:, :], in_=T2[:, :], func=AF.Exp), T2.r(), T2.r())
        ew("dve", lambda e: e.tensor_tensor(out=T0[:, :], in0=AIM[:, :], in1=T2[:, :], op=ALU.mult), AIM.r() + T2.r(), T0.r())
        ew("dve", lambda e: e.tensor_tensor(out=T1[:, :], in0=ARE[:, :], in1=T2[:, :], op=ALU.mult), ARE.r() + T2.r(), T1.r())
        ew("dve", lambda e: e.tensor_copy(out=ARE[:, :], in_=T0[:, :]), T0.r(), ARE.r())
        ew("dve", lambda e: e.tensor_copy(out=AIM[:, :], in_=T1[:, :]), T1.r(), AIM.r())
        for (PT, top, ts_, bot, bs_) in ((P1, ARE, 1.0, AIM, -1.0), (P2, AIM, -1.0, ARE, -1.0), (P3, ARE, -1.0, AIM, 1.0)):
            ew("act", lambda e, PT=PT, top=top, ts_=ts_: e.activation(out=PT[0:64, :], in_=top[0:64, :], func=AF.Copy, scale=ts_),
               top.r(), PT.r())
            ew("act", lambda e, PT=PT, bot=bot, bs_=bs_: e.activation(out=PT[64:128, :], in_=bot[64:128, :], func=AF.Copy, scale=bs_),
               bot.r(), PT.r())
        are1 = v16(ARE)[:, :, 8]
        aim1 = v16(AIM)[:, :, 8]
        c0, c1, c2, c3 = (T2[:, 0:32], T2[:, 32:64], T2[:, 64:96], T2[:, 96:128])
        c4, c5, c6, c7 = (T2[:, 128:160], T2[:, 160:192], T2[:, 192:224], T2[:, 224:256])
        RW = T2.r()
        AR_ = ARE.r() + AIM.r()
        ew("dve", lambda e: e.tensor_scalar(out=c0, in0=are1, scalar1=-1.0, scalar2=None, op0=ALU.add), AR_ + RW, RW)
        ew("dve", lambda e: e.tensor_tensor(out=c1, in0=LAMR[:, :], in1=LAMR[:, :], op=ALU.mult), LAMR.r() + RW, RW)
        ew("dve", lambda e: e.tensor_tensor(out=c2, in0=LAMI[:, :], in1=LAMI[:, :], op=ALU.mult), LAMI.r() + RW, RW)
        ew("dve", lambda e: e.tensor_tensor(out=c1, in0=c1, in1=c2, op=ALU.add), RW, RW)
        ew("dve", lambda e: e.reciprocal(out=c1, in_=c1), RW, RW)
        ew("dve", lambda e: e.tensor_tensor(out=c2, in0=c0, in1=LAMR[:, :], op=ALU.mult), LAMR.r() + RW, RW)
        ew("dve", lambda e: e.tensor_tensor(out=c3, in0=aim1, in1=LAMI[:, :], op=ALU.mult), AR_ + LAMI.r() + RW, RW)
        ew("dve", lambda e: e.tensor_tensor(out=c2, in0=c2, in1=c3, op=ALU.add), RW, RW)
        ew("dve", lambda e: e.tensor_tensor(out=c4, in0=c2, in1=c1, op=ALU.mult), RW, RW)
        ew("dve", lambda e: e.tensor_tensor(out=c2, in0=aim1, in1=LAMR[:, :], op=ALU.mult), AR_ + LAMR.r() + RW, RW)
        ew("dve", lambda e: e.tensor_tensor(out=c3, in0=c0, in1=LAMI[:, :], op=ALU.mult), LAMI.r() + RW, RW)
        ew("dve", lambda e: e.tensor_tensor(out=c2, in0=c2, in1=c3, op=ALU.subtract), RW, RW)
        ew("dve", lambda e: e.tensor_tensor(out=c5, in0=c2, in1=c1, op=ALU.mult), RW, RW)
        arq = v16(ARE)[:, :, 7:15]
        aiq = v16(AIM)[:, :, 7:15]
        cr3 = c4.rearrange("p (g o) -> p g o", o=1).to_broadcast([128, 32, 8])
        ci3 = c5.rearrange("p (g o) -> p g o", o=1).to_broadcast([128, 32, 8])

        def v8(t, lo=0):
            return t[:, lo:lo + 256].rearrange("p (g q) -> p g q", g=32)
        ew("dve", lambda e: e.tensor_tensor(out=v8(T0), in0=arq, in1=cr3, op=ALU.mult), AR_ + RW, T0.r())
        ew("dve", lambda e: e.tensor_tensor(out=v8(T0, 256), in0=aiq, in1=ci3, op=ALU.mult), AR_ + RW, T0.r())
        ew("dve", lambda e: e.tensor_tensor(out=v8(T1), in0=arq, in1=ci3, op=ALU.mult), AR_ + RW, T1.r())
        ew("dve", lambda e: e.tensor_tensor(out=v8(T1, 256), in0=aiq, in1=cr3, op=ALU.mult), AR_ + RW, T1.r())
        ew("dve", lambda e: e.tensor_tensor(out=GX1[:, :], in0=T0[:, 0:256], in1=T0[:, 256:512], op=ALU.subtract), T0.r(), GX1.r())
        ew("dve", lambda e: e.tensor_tensor(out=GX2[:, :], in0=T1[:, 0:256], in1=T1[:, 256:512], op=ALU.add), T1.r(), GX2.r())
        ew("dve", lambda e: e.tensor_scalar(out=GX2[0:64, :], in0=GX2[0:64, :], scalar1=-1.0, scalar2=None, op0=ALU.mult), GX2.r(), GX2.r())
        ew("act", lambda e: e.activation(out=g32(R8), in_=g32(LRD), func=AF.Exp, scale=8.0), LRD.r(), R8.r())
        ew("dve", lambda e: e.tensor_scalar(out=g32(X8), in0=g32(ANGV), scalar1=8.0, scalar2=None, op0=ALU.mult), ANGV.r(), X8.r())
        ew("dve", lambda e: e.tensor_copy(out=TGI[:, 0:32], in_=g32(X8)), X8.r(), TGI.r())
        ew("dve", lambda e: e.tensor_tensor(out=g32(X8), in0=g32(X8), in1=TGI[:, 0:32], op=ALU.subtract), X8.r() + TGI.r(), X8.r())


    for l in range(DEPTH if KSTOP > 0 else 0):
      try:
            gp = "L%d" % l
            if l == 0:
                prologue_dma(0)
                prologue_early(0)
                prologue_compute(0)
            load_w(WOUT, 0, wout_d[l], D, gp + "wout", 8)
            stop(2)
            htv4 = HT[:, :].rearrange("p (k j s) -> p k j s", k=8, s=8)
            wav = WA[:, :].rearrange("p (k n) -> p k n", k=8)
            atv = A_TM[:, :].rearrange("p (g q c) -> p g q c", g=32, q=8)
            uv = U[:, :].rearrange("p (g j) -> p g j", g=32)
            for h in range(2):
                norm_half(h)
                for s in range(8):
                    b = bank()

                    def mm(e, s=s, b=b):
                        for k in range(8):
                            e.matmul(PS[:, b:b + 512], lhsT=htv4[:, k, :, s], rhs=wav[:, k, :], start=(k == 0), stop=False)
                        return e.matmul(PS[:, b:b + 512], lhsT=ONES[0:1, :], rhs=BUA[0:1, :], start=False, stop=True)
                    S.op("pe", mm, reads=HT.r() + WA.r() + ONES.r() + BUA.r(), writes=PS.r(b, b + 512))
                    q = 7 - s
                    S.op("act" if s % 2 else "dve",
                         (lambda e, b=b, q=q: e.activation(out=atv[:, :, q, :], in_=PS[:, b:b + 512].rearrange("p (g c) -> p g c", g=32), func=AF.Copy))
                         if s % 2 else
                         (lambda e, b=b, q=q: e.tensor_copy(out=atv[:, :, q, :], in_=PS[:, b:b + 512].rearrange("p (g c) -> p g c", g=32))),
                         reads=PS.r(b, b + 512), writes=A_TM.r())
                for g4 in range(8):
                    b = bank()
                    psb16 = PS[:, b:b + 512].bitcast(BF16)

                    def tr(e, g4=g4, psb16=psb16):
                        ins = None
                        for gi in range(4):
                            g = g4 * 4 + gi
                            ins = e.transpose(psb16[:, gi * 128:(gi + 1) * 128], A_TM[:, g * 128:(g + 1) * 128], IDB[:, :])
                        return ins
                    S.op("pe", tr, reads=A_TM.r() + IDB.r(), writes=PS.r(b, b + 512))
                    S.op("act" if g4 % 2 else "dve",
                         (lambda e, g4=g4, psb16=psb16, h=h: e.activation(out=uv[:, g4 * 4:(g4 + 1) * 4, h * 128:(h + 1) * 128],
                                                                          in_=psb16[:, 0:512].rearrange("p (g j) -> p g j", g=4), func=AF.Copy))
                         if g4 % 2 else
                         (lambda e, g4=g4, psb16=psb16, h=h: e.tensor_copy(out=uv[:, g4 * 4:(g4 + 1) * 4, h * 128:(h + 1) * 128],
                                                                           in_=psb16[:, 0:512].rearrange("p (g j) -> p g j", g=4))),
                         reads=PS.r(b, b + 512), writes=U.r(g4 * 4 * 256, (g4 + 1) * 4 * 256))

            stop(3)
            load_w(WA, 0, w_in_d[l, :, 512:1024], 512, gp + "wza", 8)
            ctr3 = CTR[:, :].rearrange("p (g c) -> p g c", g=32)
            cti3 = CTI[:, :].rearrange("p (g c) -> p g c", g=32)
            by13 = BY1[:, :].rearrange("p (g c) -> p g c", g=32)
            by23 = BY2[:, :].rearrange("p (g c) -> p g c", g=32)
            gx13 = GX1[:, :].rearrange("p (g q) -> p g q", g=32)
            gx23 = GX2[:, :].rearrange("p (g q) -> p g q", g=32)
            p13 = v16(P1)
            p23 = v16(P2)
            p33 = v16(P3)
            ytv = YT[:, :].rearrange("p (i j t) -> p i j t", i=4, t=8)
            def bx(v):
                return v.rearrange("p g (a o) -> p g a o", o=1).to_broadcast([128, 2, 8, 16])

            def by(v):
                return v.rearrange("p g (o b) -> p g o b", o=1).to_broadcast([128, 2, 8, 16])

            def outer(dst, Xs, Ys, X2s, Y2s, rd, part=None, pair=0):
                d4 = dst[:, :].rearrange("p (g a b) -> p g a b", g=2, a=8)
                if pair == 0:
                    tA, tB, rA, rB = TMPA[:, 0:256], TMPB[:, 0:256], TMPA.r(0, 256), TMPB.r(0, 256)
                else:
                    tA, tB, rA, rB = VV[:, 0:256], VV[:, 256:512], VV.r(0, 256), VV.r(256, 512)
                ta = tA.rearrange("p (g a b) -> p g a b", g=2, a=8)
                tb = tB.rearrange("p (g a b) -> p g a b", g=2, a=8)
                if part in (None, 0):
                    ew("pool", lambda e: e.tensor_tensor(out=ta, in0=bx(Xs), in1=by(Ys), op=ALU.mult), rd, rA)
                    ew("pool", lambda e: e.tensor_tensor(out=tb, in0=bx(X2s), in1=by(Y2s), op=ALU.mult), rd, rB)
                if part in (None, 1):
                    ew("dve", lambda e: e.tensor_tensor(out=d4, in0=ta, in1=tb, op=ALU.add), rA + rB, dst.r())

            rdB = GX1.r() + GX2.r() + BY1.r() + BY2.r()
            rdC = P1.r() + P2.r() + P3.r() + CTR.r() + CTI.r()

            def stage_T0(bt):
                g0 = bt * 2
                BF_, CFM = TBL[bt % 2][0], TBL[bt % 2][1]
                gs = slice(g0, g0 + 2)
                outer(BF_, gx13[:, gs, :], by13[:, gs, :], gx23[:, gs, :], by23[:, gs, :], rdB, part=0, pair=0)
                outer(CFM, p13[:, gs, 0:8], ctr3[:, gs, :], p23[:, gs, 0:8], cti3[:, gs, :], rdC, part=0, pair=1)

            def stage_T(bt, first_done=False, mid_hook=None):
                g0 = bt * 2
                BF_, CFM, CF1, CF2, BM12, M0 = TBL[bt % 2]
                gs = slice(g0, g0 + 2)
                outer(BF_, gx13[:, gs, :], by13[:, gs, :], gx23[:, gs, :], by23[:, gs, :], rdB, part=(1 if first_done else None), pair=0)
                outer(CFM, p13[:, gs, 0:8], ctr3[:, gs, :], p23[:, gs, 0:8], cti3[:, gs, :], rdC, part=(1 if first_done else None), pair=1)
                outer(CF1, p13[:, gs, 8:16], ctr3[:, gs, :], p23[:, gs, 8:16], cti3[:, gs, :], rdC, part=0, pair=0)
                if mid_hook is not None:
                    mid_hook()
                outer(CF1, p13[:, gs, 8:16], ctr3[:, gs, :], p23[:, gs, 8:16], cti3[:, gs, :], rdC, part=1, pair=0)
                outer(CF2, p23[:, gs, 8:16], ctr3[:, gs, :], p33[:, gs, 8:16], cti3[:, gs, :], rdC, pair=1)
                b = bank(2)
                for gi in range(2):
                    def mt(e, gi=gi, b=b):
                        o = b + gi * 512
                        e.matmul(PS[:, o:o + 256], lhsT=BF_[:, gi * 128:(gi + 1) * 128], rhs=IPM[:, :], start=True, stop=True)
                        return e.matmul(PS[:, o + 256:o + 384], lhsT=BF_[:, gi * 128:(gi + 1) * 128], rhs=CFM[:, gi * 128:(gi + 1) * 128],
                                        start=True, stop=True)
                    S.op("pe", mt, reads=BF_.r() + IPM.r() + CFM.r(), writes=PS.r(b + gi * 512, b + gi * 512 + 384))
                for gi in range(2):
                    o = b + gi * 512
                    ew("act", lambda e, gi=gi, o=o: e.activation(out=BM12[:, gi * 256:(gi + 1) * 256], in_=PS[:, o:o + 256], func=AF.Copy),
                       PS.r(o, o + 256), BM12.r(gi * 256, (gi + 1) * 256))
                psm = PS[:, b:b + 1024].rearrange("p (g n) -> p g n", g=2)[:, :, 256:384]
                cm3 = CM[:, :].rearrange("p (o n) -> p o n", o=1).to_broadcast([128, 2, 128])
                ew("dve", lambda e: e.tensor_tensor(out=TMPB[:, 0:256].rearrange("p (g n) -> p g n", g=2), in0=psm, in1=cm3, op=ALU.mult),
                   PS.r(b, b + 1024) + CM.r(), TMPB.r(0, 256))
                for gi in range(2):
                    ew("dve", lambda e, gi=gi: e.scalar_tensor_tensor(out=M0[:, gi * 128:(gi + 1) * 128], in0=JD[:, :],
                                                                      scalar=DVEC[:, g0 + gi:g0 + gi + 1], in1=TMPB[:, gi * 128:(gi + 1) * 128],
                                                                      op0=ALU.mult, op1=ALU.add),
                       JD.r() + DVEC.r() + TMPB.r(0, 256), M0.r(gi * 128, (gi + 1) * 128))

            j3 = JTAB[:, :].rearrange("p (o j) -> p o j", o=1).to_broadcast([128, 2, 256])

            def stage_P(bt):
                g0 = bt * 2
                COSp, SINp, TIp = CSET[bt % 2]
                ti3 = TIp[:, :].rearrange("p (g j) -> p g j", g=2)
                x83 = g32(X8)[:, g0:g0 + 2].rearrange("p (g o) -> p g o", o=1).to_broadcast([128, 2, 256])
                ew("dve", lambda e: e.tensor_tensor(out=ti3, in0=x83, in1=j3, op=ALU.mult), X8.r() + JTAB.r(), TIp.r())
                for gi in range(2):
                    ew("dve", lambda e, gi=gi: e.scalar_tensor_tensor(out=SINp[:, gi * 256:(gi + 1) * 256], in0=JTAB[:, :],
                                                                      scalar=X8[:, g0 + gi:g0 + gi + 1], in1=TIp[:, gi * 256:(gi + 1) * 256],
                                                                      op0=ALU.mult, op1=ALU.subtract),
                       JTAB.r() + X8.r() + TIp.r(), SINp.r(gi * 256, (gi + 1) * 256))
                ew("act", lambda e: e.activation(out=COSp[:, :], in_=SINp[:, :], func=AF.Sin, scale=TWO_PI * 0.5 * 0.99999), SINp.r() + TIp.r(), COSp.r())
                ew("act", lambda e: e.activation(out=SINp[:, :], in_=SINp[:, :], func=AF.Sin, scale=TWO_PI * 0.99999), SINp.r() + COSp.r(), SINp.r())
                ew("act", lambda e: e.activation(out=COSp[:, :], in_=COSp[:, :], func=AF.Square), COSp.r(), COSp.r())
                ew("act", lambda e: e.activation(out=COSp[:, :], in_=COSp[:, :], func=AF.Identity, scale=-2.0, bias=ONEB[:, 0:1]), COSp.r() + ONEB.r(), COSp.r())

            e13 = E1[:, :].rearrange("p (g j) -> p g j", g=2)
            e23 = E2[:, :].rearrange("p (g j) -> p g j", g=2)
            z3 = ZZ[:, :].rearrange("p (g j) -> p g j", g=2)

            def stage_S(bt):
                g0 = bt * 2
                BF_, CFM, CF1, CF2, BM12, M0 = TBL[bt % 2]
                COS, SIN, _ti = CSET[bt % 2]
                c3v = COS[:, :].rearrange("p (g j) -> p g j", g=2)
                s3v = SIN[:, :].rearrange("p (g j) -> p g j", g=2)
                b2 = bank(2)

                def ms(e, b2=b2):
                    ins = None
                    for gi in range(2):
                        for w in range(2):
                            o = b2 + w * 512 + gi * 256
                            ins = e.matmul(PS[:, o:o + 256], lhsT=BM12[:, gi * 256 + w * 128:gi * 256 + (w + 1) * 128],
                                           rhs=uv[:, g0 + gi, :], start=True, stop=True)
                    return ins
                S.op("pe", ms, reads=BM12.r() + U.r(g0 * 256, (g0 + 2) * 256), writes=PS.r(b2, b2 + 1024))
                ew("dve", lambda e, b2=b2: e.tensor_tensor(out=VV[:, :], in0=PS[:, b2:b2 + 512], in1=COS[:, :], op=ALU.mult),
                   PS.r(b2, b2 + 512) + COS.r(), VV.r())
                ew("dve", lambda e, b2=b2: e.tensor_tensor(out=ZZ[:, :], in0=PS[:, b2 + 512:b2 + 1024], in1=SIN[:, :], op=ALU.mult),
                   PS.r(b2 + 512, b2 + 1024) + SIN.r(), ZZ.r())
                ew("dve", lambda e: e.tensor_tensor(out=VV[:, :], in0=VV[:, :], in1=ZZ[:, :], op=ALU.add), VV.r() + ZZ.r(), VV.r())
                for gi in range(2):
                    ew("dve", lambda e, gi=gi: e.tensor_tensor_scan(
                        out=ZZ[:, gi * 256:(gi + 1) * 256], data0=R8[:, g0 + gi:g0 + gi + 1].to_broadcast([128, 256]),
                        data1=VV[:, gi * 256:(gi + 1) * 256], initial=0.0, op0=ALU.mult, op1=ALU.add),
                       R8.r() + VV.r(gi * 256, (gi + 1) * 256), ZZ.r(gi * 256, (gi + 1) * 256))
                ew("pool", lambda e: e.memset(e13[:, :, 0:1], 0.0), (), E1.r())
                ew("pool", lambda e: e.memset(e23[:, :, 0:1], 0.0), (), E2.r())
                if bt + 2 < 16:
                    stage_T0(bt + 2)
                ew("dve", lambda e: e.tensor_tensor(out=e13[:, :, 1:257], in0=z3, in1=c3v, op=ALU.mult), ZZ.r() + COS.r(), E1.r())
                if bt + 1 < 16 and bt >= 1:
                    stage_P(bt + 1)
                def part_b():
                    ew("pool", lambda e: e.tensor_tensor(out=e23[:, :, 1:257], in0=z3, in1=s3v, op=ALU.mult), ZZ.r() + SIN.r(), E2.r())
                    pair = bt % 4
                    for h in range(2):
                        b3 = bank()

                        def mo(e, h=h, b3=b3):
                            ins = None
                            for gi in range(2):
                                o = b3 + gi * 128
                                e.matmul(PS[:, o:o + 128], lhsT=uv[:, g0 + gi, h * 128:(h + 1) * 128], rhs=M0[:, gi * 128:(gi + 1) * 128],
                                         start=True, stop=False)
                                e.matmul(PS[:, o:o + 128], lhsT=e13[:, gi, h * 128:(h + 1) * 128], rhs=CF1[:, gi * 128:(gi + 1) * 128],
                                         start=False, stop=False)
                                ins = e.matmul(PS[:, o:o + 128], lhsT=e23[:, gi, h * 128:(h + 1) * 128], rhs=CF2[:, gi * 128:(gi + 1) * 128],
                                               start=False, stop=True)
                            return ins
                        S.op("pe", mo, reads=U.r(g0 * 256, (g0 + 2) * 256) + M0.r() + E1.r() + E2.r() + CF1.r() + CF2.r(), writes=PS.r(b3, b3 + 256))
                        yv = YTM[h][:, :].rearrange("p (t g c) -> p g t c", t=8, g=8)
                        ew("act", lambda e, h=h, b3=b3, yv=yv, pair=pair: e.activation(
                            out=yv[:, pair * 2:pair * 2 + 2, :, :],
                            in_=PS[:, b3:b3 + 256].rearrange("p (g t c) -> p g t c", g=2, t=8), func=AF.Gelu_apprx_tanh),
                           PS.r(b3, b3 + 256), YTM[h].r())
                    if pair == 3:
                        i = bt // 4
                        for h in range(2):
                            for t2 in range(2):
                                b4 = bank()
                                psb16 = PS[:, b4:b4 + 512].bitcast(BF16)

                                def ty(e, h=h, t2=t2, psb16=psb16):
                                    ins = None
                                    for tq in range(4):
                                        t = t2 * 4 + tq
                                        ins = e.transpose(psb16[:, tq * 128:(tq + 1) * 128], YTM[h][:, t * 128:(t + 1) * 128], IDB[:, :])
                                    return ins
                                S.op("pe", ty, reads=YTM[h].r() + IDB.r(), writes=PS.r(b4, b4 + 512))
                                ew("act", lambda e, h=h, t2=t2, psb16=psb16, i=i: e.activation(
                                    out=ytv[:, i, h * 128:(h + 1) * 128, t2 * 4:(t2 + 1) * 4].rearrange("p j t -> p t j"),
                                    in_=psb16[:, 0:512].rearrange("p (t j) -> p t j", t=4), func=AF.Copy),
                                   PS.r(b4, b4 + 512), YT.r(i * L, (i + 1) * L))
                return part_b

            stage_T(0)
            stage_T(1)
            stage_P(0)
            stage_P(1)
            for bt in range(16):
                pb = stage_S(bt)
                if bt + 2 < 16:
                    stage_T(bt + 2, first_done=True, mid_hook=pb)
                else:
                    pb()

            if l + 1 < DEPTH:
                prologue_early(l + 1)
            stop(4)
            for h in range(2):
                hp = gp + "h%d" % h
                if h == 0:
                    norm_half(h)
                htv = HT[:, :].rearrange("p (k n) -> p k n", k=8)
                ytk = YT[:, :].rearrange("p (i n) -> p i n", i=4)
                wzav = WA[:, :].rearrange("p (k n) -> p k n", k=8)
                wgluv = WGLU[:, :].rearrange("p (k n) -> p k n", k=4)
                wubv = WUB[:, :].rearrange("p (k n) -> p k n", k=8)
                wzbv = WZB[:, :].rearrange("p (k n) -> p k n", k=8)
                ybv = YB[:, :].rearrange("p (i n) -> p i n", i=4)
                mgv = MG[:, :].rearrange("p (i n) -> p i n", i=8)

                def proj(wv, i, c, nk, src_v, src_reads, wreads):
                    b = bank()

                    def mm(e, b=b):
                        ins = None
                        for k in range(nk):
                            ins = e.matmul(PS[:, b:b + 512], lhsT=wv[:, k, i * 128:(i + 1) * 128], rhs=src_v[:, k, c * 512:(c + 1) * 512],
                                           start=(k == 0), stop=(k == nk - 1))
                        return ins
                    S.op("pe", mm, reads=src_reads + wreads, writes=PS.r(b, b + 512))
                    return b

                for c in range(2):
                    tok0 = h * 1024 + c * 512
                    for w, win in enumerate((2, 4, 8, 16)):
                        b = proj(wubv, w, c, 8, htv, HT.r(), WUB.r())
                        ew("act", lambda e, b=b, w=w: e.activation(out=UBX[w][:, 16:528], in_=PS[:, b:b + 512], func=AF.Identity, bias=BPP[:, 4 + w:5 + w]),
                           PS.r(b, b + 512) + BPP.r(), UBX[w].r())
                        if h == 0 and c == 0:
                            ew("dve", lambda e, w=w: e.memset(UBX[w][:, 0:16], 0.0), (), UBX[w].r())
                        else:
                            ew("dve", lambda e, w=w: e.tensor_copy(out=UBX[w][:, 0:16], in_=HALO[:, w * 16:(w + 1) * 16]), HALO.r(), UBX[w].r())
                        ew("dve", lambda e, w=w: e.tensor_copy(out=HALO[:, w * 16:(w + 1) * 16], in_=UBX[w][:, 512:528]), UBX[w].r(), HALO.r())
                        srcT = UBX[w]
                        m = 1
                        dsts = [CF[1], CF[2], CF[1], CF[2]]
                        di = 0
                        while m < win:
                            dstT = dsts[di]
                            di += 1
                            ew("dve", lambda e, srcT=srcT, dstT=dstT, m=m: e.tensor_tensor(out=dstT[:, 16:528], in0=srcT[:, 16:528],
                                                                                          in1=srcT[:, 16 - m:528 - m], op=ALU.add),
                               srcT.r(), dstT.r())
                            if 2 * m < win:
                                ew("dve", lambda e, srcT=srcT, dstT=dstT, m=m: e.tensor_tensor(out=dstT[:, m:16], in0=srcT[:, m:16],
                                                                                              in1=srcT[:, 0:16 - m], op=ALU.add),
                                   srcT.r(), dstT.r())
                                ew("pool", lambda e, dstT=dstT, m=m: e.memset(dstT[:, 0:m], 0.0), dstT.r(), dstT.r())
                            srcT = dstT
                            m *= 2
                        ew("dve", lambda e, srcT=srcT, win=win, w=w: e.scalar_tensor_tensor(out=PB[w][:, :], in0=srcT[:, 16:528], scalar=1.0 / win,
                                                                                            in1=UBX[w][:, 16:528], op0=ALU.mult, op1=ALU.subtract),
                           srcT.r() + UBX[w].r(), PB[w].r())
                        if h == 0 and c == 0:
                            ew("dve", lambda e, srcT=srcT, w=w: e.tensor_tensor(out=CF[3][:, 16:32], in0=srcT[:, 16:32], in1=INVC[:, w * 16:(w + 1) * 16], op=ALU.mult),
                               srcT.r() + INVC.r() + CF[3].r(), CF[3].r())
                            ew("dve", lambda e, w=w: e.tensor_tensor(out=PB[w][:, 0:16], in0=CF[3][:, 16:32], in1=UBX[w][:, 16:32], op=ALU.subtract),
                               CF[3].r() + UBX[w].r() + PB[w].r(), PB[w].r())
                    for i in range(4):
                        b = proj(wgluv, i, 0, 4, ytk[:, :, tok0:tok0 + 512], YT.r(), WGLU.r())
                        ew("act", lambda e, b=b, i=i: e.activation(out=CH[i][:, :], in_=PS[:, b:b + 512], func=AF.Sigmoid, bias=BGLU[:, i:i + 1]),
                           PS.r(b, b + 512) + BGLU.r(), CH[i].r())
                    for i in range(4):
                        b = proj(wzav, i, c, 8, htv, HT.r(), WA.r())
                        sb = SLB[i % 2]
                        ew("act", lambda e, b=b, i=i, sb=sb: e.activation(out=sb[:, :], in_=PS[:, b:b + 512], func=AF.Silu, bias=BPP[:, i:i + 1]),
                           PS.r(b, b + 512) + BPP.r(), sb.r())
                        ysl = YT[:, i * L + tok0:i * L + tok0 + 512]
                        yr = YT.r(i * L + tok0, i * L + tok0 + 512)
                        ew("pool", lambda e, i=i, ysl=ysl, sb=sb: e.tensor_tensor(out=CH[i][:, :], in0=CH[i][:, :], in1=sb[:, :], op=ALU.mult),
                           CH[i].r() + sb.r(), CH[i].r())
                        ew("dve", lambda e, i=i, ysl=ysl: e.tensor_tensor(out=ysl, in0=ysl, in1=CH[i][:, :], op=ALU.mult),
                           CH[i].r() + yr, yr)
                    for w in range(4):
                        b6 = proj(wzbv, w, c, 8, htv, HT.r(), WZB.r())
                        b5 = bank()
                        S.op("pe", lambda e, b5=b5, w=w: e.matmul(PS[:, b5:b5 + 512], lhsT=WPOOL[:, w * 128:(w + 1) * 128], rhs=PB[w][:, :], start=True, stop=True),
                             reads=WPOOL.r() + PB[w].r(), writes=PS.r(b5, b5 + 512))
                        sb = SLB[w % 2]
                        ew("act", lambda e, b6=b6, w=w, sb=sb: e.activation(out=sb[:, :], in_=PS[:, b6:b6 + 512], func=AF.Silu, bias=BPP[:, 8 + w:9 + w]),
                           PS.r(b6, b6 + 512) + BPP.r(), sb.r())
                        ew("dve", lambda e, b5=b5, w=w, c=c: e.scalar_tensor_tensor(out=ybv[:, w, c * 512:(c + 1) * 512], in0=PS[:, b5:b5 + 512],
                                                                                    scalar=PSC[:, w:w + 1], in1=sb[:, :], op0=ALU.mult, op1=ALU.mult),
                           PS.r(b5, b5 + 512) + PSC.r() + sb.r(), YB.r(w * 1024 + c * 512, w * 1024 + (c + 1) * 512))
                stop(5)
                for o in range(8):
                    u = h * 8 + o
                    wg = WG[u % 2]
                    wgv = wg[:, :].rearrange("p (k n) -> p k n", n=128)
                    if u >= 1 and u + 1 < 16:
                        load_wg(l, (u + 1) % 8, WG[(u + 1) % 2], gp + "wgu%d" % (u + 1))
                    for c in range(2):
                        tok0 = h * 1024 + c * 512
                        outs = []
                        for (koff, nk, srcv, srd) in ((0, 8, htv, HT.r()), (8, 8, htv, HT.r()),
                                                      (16, 4, ytk[:, :, h * 1024:(h + 1) * 1024], YT.r()), (20, 4, ybv, YB.r())):
                            b = bank()

                            def mm(e, b=b, koff=koff, nk=nk, srcv=srcv, c=c):
                                ins = None
                                for k in range(nk):
                                    ins = e.matmul(PS[:, b:b + 512], lhsT=wgv[:, koff + k, :], rhs=srcv[:, k, c * 512:(c + 1) * 512],
                                                   start=(k == 0), stop=(k == nk - 1))
                                return ins
                            S.op("pe", mm, reads=srd + wg.r(), writes=PS.r(b, b + 512))
                            outs.append(b)
                        par = (o * 2 + c) % 2
                        sga, sgb = (CH[2], CH[0])[par], (CH[3], CH[1])[par]
                        pa, pb = (CF[1], CF[0])[par], (CF[2], CF[3])[par]
                        ew("act", lambda e, b=outs[0], o=o: e.activation(out=sga[:, :], in_=PS[:, b:b + 512], func=AF.Sigmoid, bias=BPP[:, 12 + o:13 + o]),
                           PS.r(outs[0], outs[0] + 512) + BPP.r(), sga.r())
                        ew("act", lambda e, b=outs[1], o=o: e.activation(out=sgb[:, :], in_=PS[:, b:b + 512], func=AF.Sigmoid, bias=BPP[:, 20 + o:21 + o]),
                           PS.r(outs[1], outs[1] + 512) + BPP.r(), sgb.r())
                        ew("dve", lambda e, b=outs[2]: e.tensor_tensor(out=pa[:, 0:512], in0=PS[:, b:b + 512], in1=sga[:, :], op=ALU.mult),
                           PS.r(outs[2], outs[2] + 512) + sga.r(), pa.r())
                        ew("dve", lambda e, b=outs[3]: e.tensor_tensor(out=pb[:, 0:512], in0=PS[:, b:b + 512], in1=sgb[:, :], op=ALU.mult),
                           PS.r(outs[3], outs[3] + 512) + sgb.r(), pb.r())
                        ew("pool", lambda e, o=o, c=c: e.tensor_tensor(out=mgv[:, o, c * 512:(c + 1) * 512], in0=pa[:, 0:512], in1=pb[:, 0:512], op=ALU.add),
                           pa.r() + pb.r(), MG.r(o * 1024 + c * 512, o * 1024 + (c + 1) * 512))
                stop(6)
                woutv = WOUT[:, :].rearrange("p (k n) -> p k n", k=8)
                npipe = NormPipe(1, [XN, XN2]) if h == 0 else None
                if npipe is not None:
                    npipe.front(0)
                if h == 1 and l + 1 < DEPTH:
                    prologue_dma(l + 1)
                for t8 in range(8):
                    tt = h * 8 + t8
                    if npipe is not None and t8 + 1 < 8:
                        npipe.front(t8 + 1)
                    if h == 1 and l + 1 < DEPTH and t8 == 3:
                        prologue_compute(l + 1)
                    for nh in range(2):
                        b = bank()

                        def mm(e, b=b, t8=t8, nh=nh):
                            ins = None
                            for k in range(8):
                                ins = e.matmul(PS[:, b:b + 512], lhsT=mgv[:, k, t8 * 128:(t8 + 1) * 128], rhs=woutv[:, k, nh * 512:(nh + 1) * 512],
                                               start=(k == 0), stop=(k == 7))
                            return ins
                        S.op("pe", mm, reads=MG.r() + WOUT.r(), writes=PS.r(b, b + 512))
                        xs = X[:, tt * D + nh * 512:tt * D + (nh + 1) * 512]
                        xr = X.r(tt * D + nh * 512, tt * D + (nh + 1) * 512)
                        rt = RT[(t8 * 2 + nh) % 2]
                        ew("act", lambda e, b=b, rt=rt: e.activation(out=rt[:, :], in_=PS[:, b:b + 512], func=AF.Copy),
                           PS.r(b, b + 512), rt.r())
                        ew("pool", lambda e, xs=xs, rt=rt: e.tensor_tensor(out=xs, in0=xs, in1=rt[:, :], op=ALU.add),
                           rt.r() + xr, xr)
                    if npipe is not None:
                        npipe.back(t8)
                    if l == DEPTH - 1 and h == 1:
                        final_tile(t8)
                        final_tile(8 + t8)
                if npipe is not None:
                    npipe.flush()

            stop(7)
      except StopBuild:
        break

    if os.environ.get("KDUMP"):
        want = os.environ["KDUMP"].split(",")
        loc = dict(ARE=ARE, AIM=AIM, P1=P1, P2=P2, P3=P3, GX1=GX1, GX2=GX2, R8=R8, X8=X8, CTR=CTR, CTI=CTI, U=U, YT=YT,
                   HT=HT, A_TM=A_TM, BF_=BF_, CFM=CFM, CF1=CF1, CF2=CF2, BM12=BM12, M0=M0, COS=COS, SIN=SIN, VV=VV, ZZ=ZZ,
                   E1=E1, E2=E2, YB=YB, MG=MG, X=X, JD=JD, CM=CM, IPM=IPM, DVEC=DVEC, BY1=BY1, LAMR=LAMR, BPP=BPP)
        for nm in want:
            t = loc[nm]
            dtp = BF16 if t.esz == 2 else F32
            dd = nc.dram_tensor("dbg_" + nm, [128, t.ncols], dtp, kind="ExternalOutput").ap()
            dma("sp", dd[:, :], t[:, :], "dbg" + nm, reads=t.r())
            S.final_wait("sp", "dbg" + nm + "_sp")

    for tt in range(16):
        if ("outst%d_sp" % tt) not in S.gtot:
            final_tile(tt)
    for tt in range(16):
        S.final_wait("sp", "outst%d_sp" % tt)
    S.emit(nc, es_global)
    return nc


es_global = None


def kernel(**inputs):
    global es_global
    ins = {k: np.ascontiguousarray(np.asarray(v, dtype=np.float32)) for k, v in inputs.items()}
    with ExitStack() as es:
        es_global = es
        nc = build_nc()
    shared = {
        "norm_g": ins["norm_g"], "w_in": ins["w_in"], "b_in": ins["b_in"],
        "ssm_log_dt": ins["ssm_log_dt"], "ssm_lam_re": ins["ssm_lam_re"], "ssm_lam_im": ins["ssm_lam_im"],
        "ssm_b_re": ins["ssm_b_re"], "ssm_b_im": ins["ssm_b_im"],
        "ssm_c_re": ins["ssm_c_re"].reshape(DEPTH, 512, 64), "ssm_c_im": ins["ssm_c_im"].reshape(DEPTH, 512, 64),
        "ssm_d": ins["ssm_d"].reshape(DEPTH, 32, 16), "ssm_w_glu": ins["ssm_w_glu"], "ssm_b_glu": ins["ssm_b_glu"],
        "pool_w": ins["pool_w"], "pool_scale": ins["pool_scale"],
        "w_branch_a": ins["w_branch_a"], "w_branch_b": ins["w_branch_b"], "w_out": ins["w_out"],
        "final_norm_g": ins["final_norm_g"],
    }
    in_maps = []
    for c in range(NCORES):
        m = dict(shared)
        m["x"] = np.ascontiguousarray(ins["x"][c])
        in_maps.append(m)
    res = run_bass_kernel_spmd(nc, in_maps, core_ids=list(range(NCORES)))
    out = np.stack([np.asarray(r["out"], dtype=np.float32) for r in res.results], axis=0)
    return out
```
